# Optimizing a Trainium2 kernel written in Bass

```python
import jax, jax.numpy as jnp
from jax import lax
import numpy as np

D_MODEL = 1024
BATCH = 1
SEQ = 16384
DEPTH = 4

GRID_W = 64
CTX_LEN = 256
D_MIX = D_MODEL
HEAD_DIM = 64
POOL_GROUPS = 4
POOL_WINDOWS = (2, 4, 8, 16)
POOL_DIM = D_MIX // 4
POOL_GROUP_DIM = POOL_DIM // POOL_GROUPS
NA_HEADS = (D_MIX - POOL_DIM) // 2 // HEAD_DIM
NA_DIM = NA_HEADS * HEAD_DIM
NA_KH = 8
NA_KW = 16
RET_HEADS = (D_MIX - POOL_DIM - NA_DIM) // HEAD_DIM
RET_DIM = RET_HEADS * HEAD_DIM
RET_CHUNK = 128
D_FF = 2816
CONV_W = 3
ROPE_BASE = 10000.0
ROT_QUARTER = HEAD_DIM // 4
EPS = 1e-6
NEG_INF = -1e30
IN_SIZES = (POOL_DIM, NA_DIM, NA_DIM, NA_DIM, RET_DIM, RET_DIM, RET_DIM, RET_DIM)
IN_DIM = sum(IN_SIZES)
SPLIT_POINTS = tuple(int(s) for s in np.cumsum(IN_SIZES)[:-1])

kernel_name = "hybrid_pool_na_retention_dit"


def rmsnorm(t, g):
    tf = t.astype(jnp.float32)
    y = tf * lax.rsqrt(jnp.mean(tf * tf, axis=-1, keepdims=True) + EPS) * g.astype(jnp.float32)
    return y.astype(t.dtype)


def modulate(t, shift, scale):
    return t * (1 + scale) + shift


def heads(t, n):
    return t.reshape(t.shape[0], t.shape[1], n, HEAD_DIM)


def pool_mixer(p, w_grp, scale):
    B, L, _ = p.shape
    pf = p.astype(jnp.float32).reshape(B, L, POOL_GROUPS, POOL_GROUP_DIM)
    csum = jnp.concatenate([jnp.zeros_like(pf[:, :1]), jnp.cumsum(pf, axis=1)], axis=1)
    t = jnp.arange(L)[:, None]
    win = jnp.asarray(POOL_WINDOWS, dtype=jnp.int32)[None, :]
    lo = jnp.clip(t - win // 2, 0, L)
    hi = jnp.clip(t - win // 2 + win, 0, L)
    g = jnp.arange(POOL_GROUPS)[None, :]
    total = csum[:, hi, g] - csum[:, lo, g]
    mean = total / (hi - lo).astype(jnp.float32)[None, :, :, None]
    y = jnp.einsum('blgc,gcd->blgd', mean - pf, w_grp.astype(jnp.float32))
    return (y.reshape(B, L, POOL_DIM) * scale.astype(jnp.float32)).astype(p.dtype)


def neighborhood_attention(q, k, v, kc, vc, rpb, rows):
    B, L, H, dh = q.shape
    kh = min(NA_KH, rows)
    span = 2 * NA_KW
    n_cb = GRID_W // NA_KW
    qg = (q * dh ** -0.5).reshape(B, rows, GRID_W, H, dh)
    kg = k.reshape(B, rows, GRID_W, H, dh)
    vg = v.reshape(B, rows, GRID_W, H, dh)
    qcol = np.arange(GRID_W).reshape(n_cb, NA_KW)
    win_start = np.clip(qcol - NA_KW // 2, 0, GRID_W - NA_KW)
    col_start = np.clip(np.arange(n_cb) * NA_KW - NA_KW // 2, 0, GRID_W - span)
    kcol = col_start[:, None] + np.arange(span)
    col_valid = (kcol[:, None, :] >= win_start[:, :, None]) & (kcol[:, None, :] < win_start[:, :, None] + NA_KW)
    dc_idx = np.clip(kcol[:, None, :] - qcol[:, :, None] + NA_KW - 1, 0, 2 * NA_KW - 2)
    rpb_c = rpb.astype(jnp.float32)[:, :, dc_idx]
    valid = jnp.asarray(col_valid)[:, :, None, :]

    def one_row(r):
        rs = jnp.clip(r - kh // 2, 0, rows - kh)
        qr = lax.dynamic_index_in_dim(qg, r, axis=1, keepdims=False)
        kr = lax.dynamic_slice_in_dim(kg, rs, kh, axis=1)
        vr = lax.dynamic_slice_in_dim(vg, rs, kh, axis=1)
        kb = kr[:, :, kcol]
        vb = vr[:, :, kcol]
        qb = qr.reshape(B, n_cb, NA_KW, H, dh)
        dr_idx = rs + jnp.arange(kh) - r + NA_KH - 1
        bias = rpb_c[:, dr_idx].transpose(0, 2, 3, 1, 4)
        s_loc = jnp.einsum('bnqhd,binshd->bhnqis', qb, kb).astype(jnp.float32)
        s_loc = jnp.where(valid, s_loc + bias, NEG_INF)
        s_ctx = jnp.einsum('bnqhd,bkhd->bhnqk', qb, kc).astype(jnp.float32)
        s = jnp.concatenate([s_loc.reshape(B, H, n_cb, NA_KW, kh * span), s_ctx], axis=-1)
        p = jax.nn.softmax(s, axis=-1).astype(v.dtype)
        p_loc = p[..., :kh * span].reshape(B, H, n_cb, NA_KW, kh, span)
        p_ctx = p[..., kh * span:]
        o = jnp.einsum('bhnqis,binshd->bnqhd', p_loc, vb) + jnp.einsum('bhnqk,bkhd->bnqhd', p_ctx, vc)
        return o.reshape(B, GRID_W, H, dh)

    out = lax.map(one_row, jnp.arange(rows))
    return out.transpose(1, 0, 2, 3, 4).reshape(B, L, H, dh)


def context_attention(q, k, v):
    s = jnp.einsum('bqhd,bkhd->bhqk', q * HEAD_DIM ** -0.5, k).astype(jnp.float32)
    p = jax.nn.softmax(s, axis=-1).astype(v.dtype)
    return jnp.einsum('bhqk,bkhd->bqhd', p, v)


def axial_rotary(t, cos_r, sin_r, cos_c, sin_c):
    def rot(u, cos, sin):
        u1, u2 = jnp.split(u, 2, axis=-1)
        cos = cos[None, :, None, :]
        sin = sin[None, :, None, :]
        return jnp.concatenate([u1 * cos - u2 * sin, u1 * sin + u2 * cos], axis=-1)
    tr, tc = jnp.split(t, 2, axis=-1)
    return jnp.concatenate([rot(tr, cos_r, sin_r), rot(tc, cos_c, sin_c)], axis=-1)


def retention_chunks(q, k, v, log_gamma, state, with_outputs):
    B, L, H, _ = q.shape
    n = L // RET_CHUNK
    lg = log_gamma.astype(jnp.float32)
    idx = jnp.arange(RET_CHUNK, dtype=jnp.float32)
    diff = idx[:, None] - idx[None, :]
    intra = jnp.exp(jnp.where(diff[None] >= 0, diff[None] * lg[:, None, None], -jnp.inf))
    q_dec = jnp.exp((idx + 1)[None, :] * lg[:, None])
    k_dec = jnp.exp((RET_CHUNK - 1 - idx)[None, :] * lg[:, None])
    c_dec = jnp.exp(RET_CHUNK * lg)

    def to_chunks(t):
        return t.reshape(B, n, RET_CHUNK, H, t.shape[-1]).transpose(1, 0, 3, 2, 4)

    def step(s, blk):
        qb, kb, vb = blk
        s_new = s * c_dec[None, :, None, None] + jnp.einsum('bhcd,bhce->bhde', kb * k_dec[None, :, :, None], vb)
        if not with_outputs:
            return s_new, None
        a = jnp.einsum('bhid,bhjd->bhij', qb, kb) * intra[None]
        y = jnp.einsum('bhij,bhje->bhie', a, vb) + jnp.einsum('bhid,bhde->bhie', qb, s) * q_dec[None, :, :, None]
        return s_new, y

    s_fin, ys = lax.scan(step, state, (to_chunks(q), to_chunks(k), to_chunks(v)))
    if not with_outputs:
        return None, s_fin
    return ys.transpose(1, 0, 3, 2, 4).reshape(B, L, H, v.shape[-1]), s_fin


def bidirectional_retention(q, k, v, qc, kc, vc, dec_f, dec_b, need_ctx):
    B = q.shape[0]
    s0 = jnp.zeros((B, RET_HEADS, HEAD_DIM, HEAD_DIM), jnp.float32)
    rev = lambda t: jnp.flip(t, axis=1)
    yc_f, s_f = retention_chunks(qc, kc, vc, dec_f, s0, need_ctx)
    yx_f, _ = retention_chunks(q, k, v, dec_f, s_f, True)
    yc_b, s_b = retention_chunks(rev(qc), rev(kc), rev(vc), dec_b, s0, need_ctx)
    yx_b, _ = retention_chunks(rev(q), rev(k), rev(v), dec_b, s_b, True)
    yc = yc_f + rev(yc_b) if need_ctx else None
    return yx_f + rev(yx_b), yc


def retention_output(y, gate, gn_g):
    B, L = y.shape[:2]
    mu = jnp.mean(y, axis=-1, keepdims=True)
    var = jnp.mean((y - mu) ** 2, axis=-1, keepdims=True)
    yn = ((y - mu) * lax.rsqrt(var + EPS)).reshape(B, L, RET_DIM) * gn_g.astype(jnp.float32)
    return (jax.nn.silu(gate.astype(jnp.float32)) * yn).astype(gate.dtype)


def mixing_layer(px, pc, pool_w, pool_scale, na_rpb, dec_f, dec_b, gn_g, rope, rows, need_ctx):
    B, L, _ = px.shape
    xs = jnp.split(px, SPLIT_POINTS, axis=-1)
    cs = jnp.split(pc, SPLIT_POINTS, axis=-1)
    f32 = jnp.float32
    pool_x = pool_mixer(xs[0], pool_w, pool_scale)
    na_q, na_k, na_v = [heads(t, NA_HEADS) for t in xs[1:4]]
    nc_q, nc_k, nc_v = [heads(t, NA_HEADS) for t in cs[1:4]]
    na_x = neighborhood_attention(na_q, na_k, na_v, nc_k, nc_v, na_rpb, rows).reshape(B, L, NA_DIM)
    rq = axial_rotary(heads(xs[4], RET_HEADS).astype(f32), *rope)
    rk = axial_rotary(heads(xs[5], RET_HEADS).astype(f32), *rope) * HEAD_DIM ** -0.5
    rv = heads(xs[6], RET_HEADS).astype(f32)
    cq = heads(cs[4], RET_HEADS).astype(f32)
    ck = heads(cs[5], RET_HEADS).astype(f32) * HEAD_DIM ** -0.5
    cv = heads(cs[6], RET_HEADS).astype(f32)
    ret_x, ret_c = bidirectional_retention(rq, rk, rv, cq, ck, cv, dec_f, dec_b, need_ctx)
    ret_x = retention_output(ret_x, xs[7], gn_g)
    yx = jnp.concatenate([pool_x, na_x.astype(px.dtype), ret_x.astype(px.dtype)], axis=-1)
    if not need_ctx:
        return yx, None
    pool_c = pool_mixer(cs[0], pool_w, pool_scale)
    na_c = context_attention(nc_q, nc_k, nc_v).reshape(B, pc.shape[1], NA_DIM)
    ret_c = retention_output(ret_c, cs[7], gn_g)
    yc = jnp.concatenate([pool_c, na_c.astype(pc.dtype), ret_c.astype(pc.dtype)], axis=-1)
    return yx, yc


def conv_ffn(h, w_up, conv_w, conv_b, w_down):
    u = h @ w_up
    up = jnp.pad(u, ((0, 0), (1, 1), (0, 0)))
    u = up[:, :-2] * conv_w[0] + up[:, 1:-1] * conv_w[1] + up[:, 2:] * conv_w[2] + conv_b
    a, b = jnp.split(u, 2, axis=-1)
    return (jax.nn.silu(a) * b) @ w_down


def setup_inputs(seed: int = 0) -> dict:
    key = jax.random.key(seed)
    ks = jax.random.split(key, 24)
    f32 = jnp.float32
    nrm = lambda k, shape, s: jax.random.normal(k, shape, f32) * s
    exps_f = 5.0 + jnp.arange(RET_HEADS, dtype=f32)[None, :] + 0.25 * jax.random.uniform(ks[11], (DEPTH, RET_HEADS), f32)
    exps_b = 5.0 + jnp.arange(RET_HEADS, dtype=f32)[None, :] + 0.25 * jax.random.uniform(ks[12], (DEPTH, RET_HEADS), f32)
    return {
        "x": nrm(ks[0], (BATCH, SEQ, D_MODEL), 1.0),
        "c": nrm(ks[1], (BATCH, D_MODEL), 1.0),
        "ctx": nrm(ks[2], (BATCH, CTX_LEN, D_MODEL), 1.0),
        "c_ctx": nrm(ks[3], (D_MODEL,), 1.0),
        "w_mod": nrm(ks[4], (DEPTH, D_MODEL, 6 * D_MODEL), 0.5 * D_MODEL ** -0.5),
        "b_mod": nrm(ks[5], (DEPTH, 6 * D_MODEL), 0.02),
        "norm1_g": 1.0 + nrm(ks[6], (DEPTH, D_MODEL), 0.02),
        "w_in": nrm(ks[7], (DEPTH, D_MODEL, IN_DIM), D_MODEL ** -0.5),
        "pool_w": nrm(ks[8], (DEPTH, POOL_GROUPS, POOL_GROUP_DIM, POOL_GROUP_DIM), POOL_GROUP_DIM ** -0.5),
        "pool_scale": 1.0 + nrm(ks[9], (DEPTH, POOL_DIM), 0.1),
        "na_rpb": nrm(ks[10], (DEPTH, NA_HEADS, 2 * NA_KH - 1, 2 * NA_KW - 1), 0.1),
        "ret_decay_fwd": jnp.log1p(-jnp.exp2(-exps_f)),
        "ret_decay_bwd": jnp.log1p(-jnp.exp2(-exps_b)),
        "ret_gn_g": 1.0 + nrm(ks[13], (DEPTH, RET_DIM), 0.02),
        "w_out": nrm(ks[14], (DEPTH, D_MIX, D_MODEL), D_MIX ** -0.5),
        "norm2_g": 1.0 + nrm(ks[15], (DEPTH, D_MODEL), 0.02),
        "w_up": nrm(ks[16], (DEPTH, D_MODEL, 2 * D_FF), D_MODEL ** -0.5),
        "conv_w": nrm(ks[17], (DEPTH, CONV_W, 2 * D_FF), CONV_W ** -0.5),
        "conv_b": nrm(ks[18], (DEPTH, 2 * D_FF), 0.02),
        "w_down": nrm(ks[19], (DEPTH, D_FF, D_MODEL), D_FF ** -0.5),
        "final_g": 1.0 + nrm(ks[20], (D_MODEL,), 0.02),
    }


def reference(x, c, ctx, c_ctx, w_mod, b_mod, norm1_g, w_in, pool_w, pool_scale, na_rpb, ret_decay_fwd, ret_decay_bwd, ret_gn_g, w_out, norm2_g, w_up, conv_w, conv_b, w_down, final_g):
    B, L, _ = x.shape
    rows = L // GRID_W
    pos = jnp.arange(L)
    inv = ROPE_BASE ** (-jnp.arange(ROT_QUARTER, dtype=jnp.float32) / ROT_QUARTER)
    ang_r = (pos // GRID_W).astype(jnp.float32)[:, None] * inv[None, :]
    ang_c = (pos % GRID_W).astype(jnp.float32)[:, None] * inv[None, :]
    rope = (jnp.cos(ang_r), jnp.sin(ang_r), jnp.cos(ang_c), jnp.sin(ang_c))
    silu_c = jax.nn.silu(c)[:, None, :]
    silu_cc = jax.nn.silu(c_ctx)
    h = x
    hc = ctx
    for l in range(DEPTH):
        need_ctx = l < DEPTH - 1
        mx = jnp.split(silu_c @ w_mod[l] + b_mod[l], 6, axis=-1)
        mc = jnp.split(silu_cc @ w_mod[l] + b_mod[l], 6, axis=-1)
        ax = modulate(rmsnorm(h, norm1_g[l]), mx[0], mx[1])
        ac = modulate(rmsnorm(hc, norm1_g[l]), mc[0], mc[1])
        yx, yc = mixing_layer(ax @ w_in[l], ac @ w_in[l], pool_w[l], pool_scale[l], na_rpb[l],
                              ret_decay_fwd[l], ret_decay_bwd[l], ret_gn_g[l], rope, rows, need_ctx)
        h = h + mx[2] * (yx @ w_out[l])
        h = h + mx[5] * conv_ffn(modulate(rmsnorm(h, norm2_g[l]), mx[3], mx[4]), w_up[l], conv_w[l], conv_b[l], w_down[l])
        if need_ctx:
            hc = hc + mc[2] * (yc @ w_out[l])
            hc = hc + mc[5] * conv_ffn(modulate(rmsnorm(hc, norm2_g[l]), mc[3], mc[4]), w_up[l], conv_w[l], conv_b[l], w_down[l])
    return rmsnorm(h, final_g)
```

```python
import numpy as np
from contextlib import ExitStack
import ml_dtypes
import concourse.bass as bass
import concourse.mybir as mybir
from concourse.bass_utils import run_bass_kernel_spmd

F32 = mybir.dt.float32
BF16 = mybir.dt.bfloat16
ALU = mybir.AluOpType
AF = mybir.ActivationFunctionType
NPBF = ml_dtypes.bfloat16

NCORE = 8
D = 1024
L = 16384
GW = 64
ROWS = 256
RPC = 32
NL = 2048
NCTX = 256
NT = NL + NCTX
HB = 256
HA = 192
NS = HB + NL + HA
NT1 = NS + NCTX
DEPTH = 4
DFF = 2816
EPS = 1e-6
POOL_WINDOWS = (2, 4, 8, 16)
NEG = -1e30
SPECIAL = ((0, 0), (1, 1), (14, 13), (15, 14))


class Buf:
    __slots__ = ("w", "r")

    def __init__(self):
        self.w = None
        self.r = {}


class Sched:
    def __init__(self, nc, es, n_dma_sems=10):
        self.nc = nc
        self.names = ("pe", "act", "dve", "pool", "sp")
        self.sems = []
        self.cnt = []
        self._es = es
        self.esem = {e: self._new_sem("s_" + e) for e in ("pe", "act", "dve", "pool")}
        self.dq = {q: [self._new_sem(f"d_{q}{i}") for i in range(n_dma_sems)] for q in ("sp", "pool", "act")}
        self.dq_next = {q: 0 for q in self.dq}
        self.obs = {e: {} for e in self.names}
        self.prog = {e: [] for e in self.names}

    def _new_sem(self, name):
        s = self._es.enter_context(self.nc.semaphore(name))
        self.sems.append(s)
        self.cnt.append(0)
        return len(self.sems) - 1

    def _wait(self, e, deps):
        best = {}
        for d in deps:
            if d is None:
                continue
            s, v = d
            if best.get(s, 0) < v:
                best[s] = v
        for s, v in best.items():
            if e == "pe" and s == self.esem["pe"]:
                continue
            if self.obs[e].get(s, 0) >= v:
                continue
            self.prog[e].append((0, s, v))
            self.obs[e][s] = v

    def _deps(self, reads, writes):
        deps = []
        for b in reads:
            deps.append(b.w)
        for b in writes:
            deps.append(b.w)
            deps.extend(b.r.items())
        return deps

    def _commit(self, ev, reads, writes):
        s, v = ev
        for b in reads:
            if b.r.get(s, 0) < v:
                b.r[s] = v
        for b in writes:
            b.w = ev
            b.r = {}

    def op(self, e, fn, reads=(), writes=()):
        self._wait(e, self._deps(reads, writes))
        s = self.esem[e]
        self.prog[e].append((1, fn, s, 1))
        self.cnt[s] += 1
        ev = (s, self.cnt[s])
        self._commit(ev, reads, writes)
        return ev

    def dma(self, q, out, in_, reads=(), writes=()):
        pool = self.dq[q]
        i = self.dq_next[q]
        self.dq_next[q] = (i + 1) % len(pool)
        s = pool[i]
        deps = self._deps(reads, writes)
        deps.append((s, self.cnt[s]))
        self._wait(q, deps)
        self.prog[q].append((1, (lambda e, out=out, in_=in_: e.dma_start(out=out, in_=in_)), s, 16))
        self.cnt[s] += 16
        ev = (s, self.cnt[s])
        self._commit(ev, reads, writes)
        return ev

    def mm(self, out, lhsT, rhs, start=True, stop=True, reads=(), writes=(), tp=None):
        kw = {} if tp is None else {"tile_position": tp}
        return self.op("pe", lambda e: e.matmul(out, lhsT, rhs, start=start, stop=stop, **kw), reads, writes)

    def tr(self, out, in_, ident, reads=(), writes=()):
        return self.op("pe", lambda e: e.transpose(out, in_, ident), reads, writes)

    def act(self, out, in_, func, bias=None, scale=None, reads=(), writes=()):
        kw = {}
        if bias is not None:
            kw["bias"] = bias
        if scale is not None:
            kw["scale"] = scale
        return self.op("act", lambda e: e.activation(out, in_, func, **kw), reads, writes)

    def tt(self, eng, out, a, b, op, reads=(), writes=()):
        return self.op(eng, lambda e: e.tensor_tensor(out, a, b, op), reads, writes)

    def ts(self, eng, out, a, s1, op0, s2=None, op1=None, reads=(), writes=()):
        if op1 is None:
            return self.op(eng, lambda e: e.tensor_scalar(out, a, s1, None, op0), reads, writes)
        return self.op(eng, lambda e: e.tensor_scalar(out, a, s1, s2, op0, op1), reads, writes)

    def stt(self, eng, out, in0, scalar, in1, op0, op1, reads=(), writes=()):
        return self.op(eng, lambda e: e.scalar_tensor_tensor(out, in0, scalar, in1, op0, op1), reads, writes)

    def cp(self, eng, out, in_, reads=(), writes=()):
        if eng == "act":
            return self.op("act", lambda e: e.copy(out, in_), reads, writes)
        return self.op(eng, lambda e: e.tensor_copy(out, in_), reads, writes)

    def recip(self, out, in_, reads=(), writes=()):
        return self.op("dve", lambda e: e.reciprocal(out, in_), reads, writes)

    def memset(self, eng, ap, val, writes=()):
        return self.op(eng, lambda e: e.memset(ap, val), (), writes)

    def barrier(self):
        deps = [(s, c) for s, c in enumerate(self.cnt) if c > 0]
        for e in self.names:
            self._wait(e, deps)

    def emit(self):
        self.barrier()
        nc = self.nc
        sems = self.sems
        prog = self.prog

        def replay(name):
            def f(eng):
                for it in prog[name]:
                    if it[0] == 0:
                        eng.wait_ge(sems[it[1]], it[2])
                    else:
                        it[1](eng).then_inc(sems[it[2]], it[3])
            return f
        with nc.Block() as block:
            block.sync(replay("sp"))
            block.scalar(replay("act"))
            block.vector(replay("dve"))
            block.gpsimd(replay("pool"))
            block.tensor(replay("pe"))


class Ring:
    def __init__(self, nc, es, name, shape, dt, n, psum=False):
        alloc = nc.psum_tensor if psum else nc.sbuf_tensor
        self.t = [es.enter_context(alloc(f"{name}{i}", shape, dt)) for i in range(n)]
        self.b = [Buf() for _ in range(n)]
        self.i = 0

    def next(self):
        i = self.i
        self.i = (i + 1) % len(self.t)
        return self.t[i], self.b[i]


def _sb(nc, es, name, shape, dt):
    return es.enter_context(nc.sbuf_tensor(name, shape, dt))


def _ps(nc, es, name, shape, dt):
    return es.enter_context(nc.psum_tensor(name, shape, dt))


def _split(c0, c1, step):
    out = []
    while c0 < c1:
        out.append((c0, min(c0 + step, c1)))
        c0 += step
    return out


def load_const(S, nc, es, name, dram, shape, dt=F32, q="sp"):
    t = _sb(nc, es, name, shape, dt)
    b = Buf()
    S.dma(q, t[:], dram[:], writes=[b])
    return t, b


def emit_mod_scalars(S, nc, es, mod6, g1, g2, pfx):
    (m6, m6b), (g1t, g1b), (g2t, g2b) = mod6, g1, g2
    A = _sb(nc, es, pfx + "A", [128, 2, 8], F32)
    b = Buf()
    S.ts("dve", A[:, 0, :], m6[:, 1, :], 1.0, ALU.add, reads=[m6b], writes=[b])
    S.tt("dve", A[:, 0, :], A[:, 0, :], g1t[:], ALU.mult, reads=[b, g1b], writes=[b])
    S.ts("dve", A[:, 1, :], m6[:, 4, :], 1.0, ALU.add, reads=[m6b, b], writes=[b])
    S.tt("dve", A[:, 1, :], A[:, 1, :], g2t[:], ALU.mult, reads=[b, g2b], writes=[b])
    return dict(A1=A[:, 0, :], B1=m6[:, 0, :], G1=m6[:, 2, :], A2=A[:, 1, :], B2=m6[:, 3, :], G2=m6[:, 5, :],
                bufs=[b, m6b])


def emit_norm_tile(S, cst, ht, htb, w, Asc, Bsc, scb, out_fn, ring_sq, ring_ps, ring_small, ring_tmp):
    sq, sqb = ring_sq.next()
    S.act(sq[:, :, 0:w], ht, AF.Square, reads=[htb], writes=[sqb])
    ps, psb = ring_ps.next()
    for k in range(8):
        S.mm(ps[:, 0:w], cst["ones"][:], sq[:, k, 0:w], start=(k == 0), stop=(k == 7),
             reads=[sqb, cst["b"]], writes=[psb])
    sd, sdb = ring_small.next()
    S.act(sd[:, 0:w], ps[:, 0:w], AF.Sqrt, bias=cst["eps"][:], scale=1.0, reads=[psb, cst["b"]], writes=[sdb])
    S.recip(sd[:, 0:w], sd[:, 0:w], reads=[sdb], writes=[sdb])
    for k in range(8):
        dst, dstb = out_fn(k)
        if Bsc is None:
            S.stt("dve", dst, ht[:, k, :], Asc[:, k:k + 1], sd[:, 0:w], ALU.mult, ALU.mult,
                  reads=[htb, sdb] + scb, writes=[dstb])
        else:
            tmp, tmpb = ring_tmp.next()
            S.stt("dve", tmp[:, 0:w], ht[:, k, :], Asc[:, k:k + 1], sd[:, 0:w], ALU.mult, ALU.mult,
                  reads=[htb, sdb] + scb, writes=[tmpb])
            S.act(dst, tmp[:, 0:w], AF.Identity, bias=Bsc[:, k:k + 1], scale=1.0, reads=[tmpb] + scb, writes=[dstb])


def emit_consts(S, nc, es, d_cst):
    t, b = load_const(S, nc, es, "cst_sb", d_cst, [128, 385], F32)
    identb = _sb(nc, es, "identb", [128, 128], BF16)
    bb = Buf()
    S.cp("dve", identb[:], t[:, 128:256], reads=[b], writes=[bb])
    return dict(ones=t[:, 0:128], identf=t[:, 128:256], bd64=t[:, 256:384], eps=t[:, 384:385], b=b,
                identb=identb, identb_b=bb)


def make_cst():
    c = np.zeros((128, 385), np.float32)
    c[:, 0:128] = 1.0 / 1024.0
    c[:, 128:256] = np.eye(128, dtype=np.float32)
    bd = np.zeros((128, 128), np.float32)
    bd[:64, :64] = 1.0 / 64
    bd[64:, 64:] = 1.0 / 64
    c[:, 256:384] = bd
    c[:, 384] = EPS
    return c


def emit_rotary(S, cst_rot, ps, psb, w, tcol0, scale, dst, dstb, rings):
    C, Sg, P, rb = cst_rot
    rawb, rawbb = rings["raw"].next()
    S.act(rawb[:, 0:w], ps[:, 0:w], AF.Copy, scale=scale, reads=[psb], writes=[rawbb])
    t1, t1b = rings["t1"].next()
    S.tt("dve", t1[:, 0:w], rawb[:, 0:w], C[:, tcol0:tcol0 + w], ALU.mult, reads=[rawbb, rb], writes=[t1b])
    pp, ppb = rings["pp"].next()
    S.mm(pp[:, 0:w], P[:], rawb[:, 0:w], reads=[rawbb, rb], writes=[ppb])
    t2, t2b = rings["t2"].next()
    S.tt("dve", t2[:, 0:w], pp[:, 0:w], Sg[:, tcol0:tcol0 + w], ALU.mult, reads=[ppb, rb], writes=[t2b])
    S.tt("pool", dst, t1[:, 0:w], t2[:, 0:w], ALU.add, reads=[t1b, t2b], writes=[dstb])


def emit_ret_chain(S, nc, rk, rkb, vret, vretb, vcol0, c, blocks, kcols, vblks, Sf_init, Sb_init,
                   dec, cst, Sfb, Sbb, Sfb_b, Sbb_b, rings, sidx):
    nb = len(blocks)
    ktf, ktb_, cfb, decb = dec["ktab_f"], dec["ktab_b"], dec["cfb"], dec["b"]
    Dbs, Dbsb = rings["dbs"]
    Dfs, Dfsb = rings["dfs"]
    halves = ((0, 64), (64, 128))
    for n in range(nb):
        kc0 = kcols[n]
        tp_, tpb = rings["tps"].next()
        S.tr(tp_[:], rk[:, c, kc0:kc0 + 128], cst["identb"][:], reads=[rkb, cst["identb_b"]], writes=[tpb])
        kdf, kdfb = rings["kd"].next()
        S.tt("dve", kdf[:], tp_[:], ktf[c], ALU.mult, reads=[tpb, decb], writes=[kdfb])
        kdb, kdbb = rings["kd"].next()
        S.tt("dve", kdb[:], tp_[:], ktb_[c], ALU.mult, reads=[tpb, decb], writes=[kdbb])
        vv = vret[:, vblks[n], vcol0 + c * 128: vcol0 + (c + 1) * 128]
        dfp, dfpb = rings["dps"].next()
        S.mm(dfp[:], kdf[:], vv, reads=[kdfb, vretb], writes=[dfpb])
        dbp, dbpb = rings["dps"].next()
        S.mm(dbp[:], kdb[:], vv, reads=[kdbb, vretb], writes=[dbpb])
        S.cp("act", Dfs[:, n, :], dfp[:], reads=[dfpb], writes=[Dfsb[n]])
        S.cp("act", Dbs[:, n, :], dbp[:], reads=[dbpb], writes=[Dbsb[n]])
    f32a, f32ab = rings["sf32"].next()
    if Sf_init is None:
        S.memset("pool", f32a[:], 0.0, writes=[f32ab])
    else:
        S.cp("pool", f32a[:], Sf_init[0], reads=[Sf_init[1]], writes=[f32ab])
    for (p0, p1) in halves:
        S.cp("pool", Sfb[p0:p1, sidx(0), p0:p1], f32a[p0:p1, p0:p1], reads=[f32ab], writes=[Sfb_b])
    cur, curb = f32a, f32ab
    for n in range(nb):
        nxt, nxtb = rings["sf32"].next()
        for (p0, p1) in halves:
            S.stt("dve", nxt[p0:p1, p0:p1], cur[p0:p1, p0:p1], cfb[p0:p1, 0, c:c + 1], Dfs[p0:p1, n, p0:p1],
                  ALU.mult, ALU.add, reads=[curb, Dfsb[n], decb], writes=[nxtb])
            S.cp("pool", Sfb[p0:p1, sidx(n + 1), p0:p1], nxt[p0:p1, p0:p1], reads=[nxtb], writes=[Sfb_b])
        cur, curb = nxt, nxtb
    sf_last = (cur, curb)
    g32, g32b = rings["sb32"].next()
    if Sb_init is None:
        S.memset("pool", g32[:], 0.0, writes=[g32b])
    else:
        S.cp("pool", g32[:], Sb_init[0], reads=[Sb_init[1]], writes=[g32b])
    for (p0, p1) in halves:
        S.cp("pool", Sbb[p0:p1, sidx(nb), p0:p1], g32[p0:p1, p0:p1], reads=[g32b], writes=[Sbb_b])
    cur, curb = g32, g32b
    for n in range(nb - 1, -1, -1):
        nxt, nxtb = rings["sb32"].next()
        for (p0, p1) in halves:
            S.stt("dve", nxt[p0:p1, p0:p1], cur[p0:p1, p0:p1], cfb[p0:p1, 1, c:c + 1], Dbs[p0:p1, n, p0:p1],
                  ALU.mult, ALU.add, reads=[curb, Dbsb[n], decb], writes=[nxtb])
            S.cp("pool", Sbb[p0:p1, sidx(n), p0:p1], nxt[p0:p1, p0:p1], reads=[nxtb], writes=[Sbb_b])
        cur, curb = nxt, nxtb
    return sf_last, (cur, curb)


def build_mod():
    nc = bass.Bass("TRN2", target_bir_lowering=False)
    d_c = nc.dram_tensor("cvec", [128, 16], F32, kind="ExternalInput")
    d_w = nc.dram_tensor("wmod", [DEPTH, 128, 8, 768], F32, kind="ExternalInput")
    d_b = nc.dram_tensor("bmod", [2, DEPTH * 768], F32, kind="ExternalInput")
    d_o = nc.dram_tensor("modv", [2, DEPTH * 768], F32, kind="ExternalOutput")
    with ExitStack() as es:
        S = Sched(nc, es)
        cv, cvb = load_const(S, nc, es, "cv", d_c, [128, 16])
        bt, btb = load_const(S, nc, es, "bt", d_b, [2, DEPTH * 768])
        sv = _sb(nc, es, "sv", [128, 16], F32)
        svb = Buf()
        S.act(sv[:], cv[:], AF.Silu, reads=[cvb], writes=[svb])
        wr = Ring(nc, es, "w", [128, 8, 768], F32, 2)
        pr = Ring(nc, es, "ps", [128, 1024], F32, 2, psum=True)
        ot = _sb(nc, es, "ot", [2, DEPTH * 768], F32)
        otb = Buf()
        for l in range(DEPTH):
            w, wb = wr.next()
            S.dma("sp", w[:], d_w[l], writes=[wb])
            ps, psb = pr.next()
            for (a, b_, o) in ((0, 512, 0), (512, 768, 512)):
                for k in range(8):
                    S.mm(ps[0:2, o:o + (b_ - a)], sv[:, 2 * k:2 * k + 2], w[:, k, a:b_], start=(k == 0), stop=(k == 7),
                         reads=[svb, wb], writes=[psb])
            S.tt("dve", ot[:, l * 768:(l + 1) * 768], ps[0:2, 0:768], bt[:, l * 768:(l + 1) * 768], ALU.add,
                 reads=[psb, btb], writes=[otb])
        S.dma("sp", d_o[:], ot[:], reads=[otb])
        S.emit()
    return nc


def to_fm(a):
    T, F = a.shape
    return np.ascontiguousarray(a.T.reshape(F // 128, 128, T).transpose(1, 0, 2))


def from_fm(a):
    P, K, T = a.shape
    return np.ascontiguousarray(a.transpose(1, 0, 2).reshape(K * 128, T).T)


def vec_pk(v):
    return np.ascontiguousarray(v.reshape(-1, 128).T)


_PROGS = {}


def _prog(name, builder):
    if name not in _PROGS:
        _PROGS[name] = builder()
    return _PROGS[name]


def run_mod(c, c_ctx, w_mod, b_mod, cores=NCORE):
    nc = _prog("mod", build_mod)
    cvec = np.zeros((128, 16), np.float32)
    cvec[:, 0::2] = vec_pk(c.reshape(-1))
    cvec[:, 1::2] = vec_pk(c_ctx.reshape(-1))
    maps = []
    for i in range(cores):
        cols = slice(i * 768, (i + 1) * 768)
        w = w_mod[:, :, cols]
        w = np.ascontiguousarray(w.reshape(DEPTH, 8, 128, 768).transpose(0, 2, 1, 3))
        b = np.ascontiguousarray(np.broadcast_to(b_mod[:, cols].reshape(1, DEPTH * 768), (2, DEPTH * 768)))
        maps.append({"cvec": cvec, "wmod": w, "bmod": b})
    res = run_bass_kernel_spmd(nc, maps, core_ids=list(range(cores)))
    out = np.zeros((DEPTH, 2, 6 * D), np.float32)
    for i in range(cores):
        o = res.results[i]["modv"].reshape(2, DEPTH, 768)
        out[:, :, i * 768:(i + 1) * 768] = o.transpose(1, 0, 2)
    return out


NH2 = 2308


def build_m2(final):
    nc = bass.Bass("TRN2", target_bir_lowering=False)
    ncols = NL if final else NT
    d_hm = nc.dram_tensor("hmid", [128, 8, NT], F32, kind="ExternalInput")
    d_h2 = nc.dram_tensor("h2e", [128, 8, NH2], BF16, kind="ExternalInput")
    d_wu = nc.dram_tensor("wup", [44, 128, 8, 128], F32, kind="ExternalInput")
    d_cw = nc.dram_tensor("cw", [128, 44, 4], F32, kind="ExternalInput")
    d_wd = nc.dram_tensor("wdn", [8, 128, 22, 128], F32, kind="ExternalInput")
    d_g2 = nc.dram_tensor("g2", [128, 2, 8], F32, kind="ExternalInput")
    d_cst = nc.dram_tensor("cst", [128, 385], F32, kind="ExternalInput")
    if final:
        d_fg = nc.dram_tensor("fg", [128, 8], F32, kind="ExternalInput")
        d_hn = nc.dram_tensor("hn_scratch", [128, 8, NL], F32)
        d_out = nc.dram_tensor("outT", [128, 8, NL], F32, kind="ExternalOutput")
    else:
        d_hn = nc.dram_tensor("hn", [128, 8, NT], F32, kind="ExternalOutput")
    if final:
        up_tiles = [(0, 512), (510, 1022), (1020, 1532), (1530, 2042), (2040, 2050)]
        dn_tiles = [(1, 513, 0), (513, 1025, 512), (1025, 1537, 1024), (1537, 2049, 1536)]
    else:
        up_tiles = [(0, 512), (510, 1022), (1020, 1532), (1530, 2042), (2040, 2308)]
        dn_tiles = [(1, 513, 0), (513, 1025, 512), (1025, 1537, 1024), (1537, 2049, 1536), (2051, 2307, 2048)]
    with ExitStack() as es0:
        S = Sched(nc, es0)
        cst = emit_consts(S, nc, es0, d_cst)
        with ExitStack() as es:
            h2, h2b = load_const(S, nc, es, "h2_sb", d_h2, [128, 8, NH2], BF16)
            cw, cwb = load_const(S, nc, es, "cw_sb", d_cw, [128, 44, 4])
            g2, g2b = load_const(S, nc, es, "g2_sb", d_g2, [128, 2, 8])
            g = _sb(nc, es, "g", [128, 22, NH2], BF16)
            gb = [Buf() for _ in range(22)]
            wr = Ring(nc, es, "wu", [128, 8, 128], BF16, 4)
            pa = Ring(nc, es, "pa", [128, 512], F32, 2, psum=True)
            pb = Ring(nc, es, "pb", [128, 512], F32, 2, psum=True)
            pd = Ring(nc, es, "pd", [128, 512], F32, 3, psum=True)
            tar = Ring(nc, es, "ta", [128, 512], F32, 2)
            tbr = Ring(nc, es, "tb", [128, 512], F32, 2)
            sar = Ring(nc, es, "sa", [128, 512], F32, 2)
            wdr = Ring(nc, es, "wd", [128, 22, 128], BF16, 2)
            hmr = Ring(nc, es, "hm", [128, 512], F32, 3)
            hor = Ring(nc, es, "ho", [128, 512], F32, 3)
            for f in range(22):
                wa, wab = wr.next()
                S.dma("pool", wa[:], d_wu[f], writes=[wab])
                wb_, wbb = wr.next()
                S.dma("pool", wb_[:], d_wu[22 + f], writes=[wbb])
                for (c0, c1) in up_tiles:
                    w = c1 - c0
                    psa, psab = pa.next()
                    for k in range(8):
                        S.mm(psa[:, 0:w], wa[:, k, :], h2[:, k, c0:c1], start=(k == 0), stop=(k == 7),
                             reads=[wab, h2b], writes=[psab])
                    psb, psbb = pb.next()
                    for k in range(8):
                        S.mm(psb[:, 0:w], wb_[:, k, :], h2[:, k, c0:c1], start=(k == 0), stop=(k == 7),
                             reads=[wbb, h2b], writes=[psbb])
                    n = w - 2
                    outs = []
                    for (ps_, psb_, ring, j) in ((psa, psab, tar, f), (psb, psbb, tbr, 22 + f)):
                        t, tb = ring.next()
                        S.act(t[:, 0:n], ps_[:, 1:1 + n], AF.Identity, bias=cw[:, j, 3:4], scale=cw[:, j, 1:2],
                              reads=[psb_, cwb], writes=[tb])
                        S.stt("dve", t[:, 0:n], ps_[:, 0:n], cw[:, j, 0:1], t[:, 0:n], ALU.mult, ALU.add,
                              reads=[psb_, cwb, tb], writes=[tb])
                        S.stt("dve", t[:, 0:n], ps_[:, 2:2 + n], cw[:, j, 2:3], t[:, 0:n], ALU.mult, ALU.add,
                              reads=[psb_, cwb, tb], writes=[tb])
                        outs.append((t, tb))
                    (ta, tab), (tb2, tb2b) = outs
                    sa, sab = sar.next()
                    S.act(sa[:, 0:n], ta[:, 0:n], AF.Silu, reads=[tab], writes=[sab])
                    S.tt("pool", g[:, f, c0 + 1:c0 + 1 + n], sa[:, 0:n], tb2[:, 0:n], ALU.mult,
                         reads=[sab, tb2b], writes=[gb[f]])
            for m in range(8):
                wd, wdb = wdr.next()
                S.dma("pool", wd[:], d_wd[m], writes=[wdb])
                for (c0, c1, o0) in dn_tiles:
                    w = c1 - c0
                    ps, psb = pd.next()
                    for f in range(22):
                        S.mm(ps[:, 0:w], wd[:, f, :], g[:, f, c0:c1], start=(f == 0), stop=(f == 21),
                             reads=[wdb, gb[f]], writes=[psb])
                    hm, hmb = hmr.next()
                    S.dma("sp", hm[:, 0:w], d_hm[:, m, o0:o0 + w], writes=[hmb])
                    ho, hob = hor.next()
                    s = 1 if o0 >= NL else 0
                    S.stt("dve", ho[:, 0:w], ps[:, 0:w], g2[:, s, m:m + 1], hm[:, 0:w], ALU.mult, ALU.add,
                          reads=[psb, hmb, g2b], writes=[hob])
                    S.dma("act", d_hn[:, m, o0:o0 + w], ho[:, 0:w], reads=[hob])
            S.barrier()
        if final:
            with ExitStack() as es:
                fg, fgb = load_const(S, nc, es, "fg_sb", d_fg, [128, 8])
                htr = Ring(nc, es, "ht", [128, 8, 256], F32, 2)
                sqr = Ring(nc, es, "sq", [128, 8, 256], F32, 1)
                psr = Ring(nc, es, "nps", [128, 512], F32, 2, psum=True)
                smr = Ring(nc, es, "sd", [128, 256], F32, 2)
                otr = Ring(nc, es, "ot", [128, 8, 256], F32, 2)
                for (c0, c1) in _split(0, NL, 256):
                    w = c1 - c0
                    ht, htb = htr.next()
                    S.dma("sp", ht[:, :, 0:w], d_hn[:, :, c0:c1], writes=[htb])
                    ot, otb = otr.next()
                    emit_norm_tile(S, cst, ht[:, :, 0:w], htb, w, fg, None, [fgb],
                                   lambda k, ot=ot, otb=otb, w=w: (ot[:, k, 0:w], otb), sqr, psr, smr, None)
                    S.dma("act", d_out[:, :, c0:c1], ot[:, :, 0:w], reads=[otb])
        S.emit()
    return nc


def prep_m2_weights(w_up_l, conv_w_l, conv_b_l, w_down_l):
    wup = np.ascontiguousarray(w_up_l.reshape(8, 128, 44, 128).transpose(2, 1, 0, 3))
    cw = np.zeros((128, 44, 4), np.float32)
    cw[:, :, 0:3] = conv_w_l.T.reshape(44, 128, 3).transpose(1, 0, 2)
    cw[:, :, 3] = conv_b_l.reshape(44, 128).T
    wdn = np.ascontiguousarray(w_down_l.reshape(22, 128, 8, 128).transpose(2, 1, 0, 3))
    return wup, cw, wdn


def make_h2e(h2_loc, h2_prev_last, h2_next_first, h2_ctx):
    e = np.zeros((128, 8, NH2), NPBF)
    if h2_prev_last is not None:
        e[:, :, 0] = h2_prev_last
    e[:, :, 1:2049] = h2_loc
    if h2_next_first is not None:
        e[:, :, 2049] = h2_next_first
    e[:, :, 2051:2307] = h2_ctx
    return e


FM_TILES_ALL = [(0, 512), (512, 1024), (1024, 1536), (1536, 2048), (2048, 2496), (2496, 2752)]
FM_TILES_LOC = [(256, 768), (768, 1280), (1280, 1792), (1792, 2304), (2496, 2752)]
FM_TILES_POOL = [(248, 760), (760, 1272), (1272, 1784), (1784, 2296), (2296, 2312), (2496, 2752)]
NPP = 2336


def loc_col(s):
    return s - HB if s < NS else NL + (s - NS)


def emit_norm_phase(S, nc, cst, d_h, ncols_list, ax, axb_of, scal_x, scal_c, which):
    with ExitStack() as es:
        htr = Ring(nc, es, "nht", [128, 8, 256], F32, 2)
        sqr = Ring(nc, es, "nsq", [128, 8, 256], F32, 2)
        psr = Ring(nc, es, "nps", [128, 512], F32, 2, psum=True)
        smr = Ring(nc, es, "nsd", [128, 256], F32, 2)
        tmr = Ring(nc, es, "ntm", [128, 256], F32, 3)
        for (c0, c1, isc) in ncols_list:
            w = c1 - c0
            ht, htb = htr.next()
            S.dma("sp", ht[:, :, 0:w], d_h[:, :, c0:c1], writes=[htb])
            sc = scal_c if isc else scal_x
            b = axb_of(c0)
            emit_norm_tile(S, cst, ht[:, :, 0:w], htb, w, sc["A" + which], sc["B" + which], sc["bufs"],
                           lambda k, c0=c0, c1=c1, b=b: (ax[:, k, c0:c1], b), sqr, psr, smr, tmr)
        S.barrier()


def emit_decays(S, nc, es, d_dk, d_dm, d_lgc, d_cexp, need_q):
    dk, dkb = load_const(S, nc, es, "dk_sb", d_dk, [128, 2, 1536])
    tab = _sb(nc, es, "dectab", [128, 1536], F32)
    b = Buf()
    S.tt("dve", tab[:], dk[:, 0, :], dk[:, 1, :], ALU.mult, reads=[dkb], writes=[b])
    S.act(tab[:], tab[:], AF.Exp, reads=[b], writes=[b])
    ce, ceb = load_const(S, nc, es, "ce_sb", d_cexp, [128, 2, 3])
    cfb = _sb(nc, es, "cfb", [128, 2, 3], F32)
    S.act(cfb[:], ce[:], AF.Exp, scale=128.0, reads=[ceb], writes=[b])
    sl = lambda t, c: tab[:, (t * 3 + c) * 128:(t * 3 + c + 1) * 128]
    dec = dict(ktab_f=[sl(0, c) for c in range(3)], ktab_b=[sl(1, c) for c in range(3)],
               qtab_f=[sl(2, c) for c in range(3)], qtab_b=[sl(3, c) for c in range(3)], cfb=cfb, b=b)
    if need_q:
        dm, dmb = load_const(S, nc, es, "dm_sb", d_dm, [128, 4, 128])
        lgc, lgcb = load_const(S, nc, es, "lgc_sb", d_lgc, [128, 2, 6])
        mc = _sb(nc, es, "mcomb", [128, 6, 128], F32)
        tmp = _sb(nc, es, "mctmp", [128, 2, 128], F32)
        tb = Buf()
        for h in range(6):
            S.act(tmp[:, 0, :], dm[:, 0, :], AF.Exp, scale=lgc[:, 0, h:h + 1], reads=[dmb, lgcb], writes=[tb])
            S.act(tmp[:, 1, :], dm[:, 2, :], AF.Exp, scale=lgc[:, 1, h:h + 1], reads=[dmb, lgcb], writes=[tb])
            S.tt("dve", tmp[:, 0, :], tmp[:, 0, :], dm[:, 1, :], ALU.mult, reads=[tb, dmb], writes=[tb])
            S.tt("dve", tmp[:, 1, :], tmp[:, 1, :], dm[:, 3, :], ALU.mult, reads=[tb, dmb], writes=[tb])
            S.tt("dve", mc[:, h, :], tmp[:, 0, :], tmp[:, 1, :], ALU.add, reads=[tb], writes=[b])
        dec["mcomb"] = mc
    return dec


def chain_rings(nc, es, deep=False):
    return dict(
        tps=Ring(nc, es, "c_tps", [128, 128], BF16, 2 if deep else 1, psum=True),
        dps=Ring(nc, es, "c_dps", [128, 128], F32, 4 if deep else 2, psum=True),
        kd=Ring(nc, es, "c_kd", [128, 128], BF16, 6),
        sf32=Ring(nc, es, "c_sf", [128, 128], F32, 3),
        sb32=Ring(nc, es, "c_sb", [128, 128], F32, 3),
        dbs=(_sb(nc, es, "c_dbs", [128, 16, 128], F32), [Buf() for _ in range(16)]),
        dfs=(_sb(nc, es, "c_dfs", [128, 16, 128], F32), [Buf() for _ in range(16)]),
    )


def build_r(stop=99):
    nc = bass.Bass("TRN2", target_bir_lowering=False)
    d_h = nc.dram_tensor("hT", [128, 8, NT], F32, kind="ExternalInput")
    d_m6 = nc.dram_tensor("mod6", [128, 2, 6, 8], F32, kind="ExternalInput")
    d_g = nc.dram_tensor("g12", [128, 2, 8], F32, kind="ExternalInput")
    d_wk = nc.dram_tensor("wk", [3, 128, 8, 128], F32, kind="ExternalInput")
    d_wv = nc.dram_tensor("wv", [128, 8, 384], F32, kind="ExternalInput")
    d_rope = nc.dram_tensor("rope", [128, 2 * NT + 128], BF16, kind="ExternalInput")
    d_dk = nc.dram_tensor("dk", [128, 2, 1536], F32, kind="ExternalInput")
    d_ce = nc.dram_tensor("cexp", [128, 2, 3], F32, kind="ExternalInput")
    d_cst = nc.dram_tensor("cst", [128, 385], F32, kind="ExternalInput")
    d_out = nc.dram_tensor("sl", [128, 4, 3, 128], F32, kind="ExternalOutput")
    with ExitStack() as es0:
        S = Sched(nc, es0)
        cst = emit_consts(S, nc, es0, d_cst)
        m6, m6b = load_const(S, nc, es0, "m6_sb", d_m6, [128, 2, 6, 8])
        g12, g12b = load_const(S, nc, es0, "g12_sb", d_g, [128, 2, 8])
        sx = emit_mod_scalars(S, nc, es0, (m6[:, 0], m6b), (g12[:, 0, :], g12b), (g12[:, 1, :], g12b), "sx")
        sc = emit_mod_scalars(S, nc, es0, (m6[:, 1], m6b), (g12[:, 0, :], g12b), (g12[:, 1, :], g12b), "sc")
        rope, ropeb = load_const(S, nc, es0, "rope_sb", d_rope, [128, 2 * NT + 128], BF16)
        crot = (rope[:, 0:NT], rope[:, NT:2 * NT], rope[:, 2 * NT:2 * NT + 128], ropeb)
        dec = emit_decays(S, nc, es0, d_dk, None, None, d_ce, False)
        rk = _sb(nc, es0, "rk", [128, 3, NT], BF16)
        rkb = Buf()
        vret = _sb(nc, es0, "vret", [128, 18, 384], BF16)
        vretb = Buf()
        with ExitStack() as es1:
            ax = _sb(nc, es1, "ax", [128, 8, NT], BF16)
            axb = Buf()
            tiles = [(a, b, False) for (a, b) in _split(0, NL, 256)] + [(NL, NT, True)]
            emit_norm_phase(S, nc, cst, d_h, tiles, ax, lambda c0: axb, sx, sc, "1")
            if stop <= 1:
                S.emit()
                return nc
            with ExitStack() as es:
                wr = Ring(nc, es, "wfm", [128, 8, 128], BF16, 2)
                accr = Ring(nc, es, "acc", [128, 512], F32, 3, psum=True)
                rr = dict(raw=Ring(nc, es, "r_raw", [128, 512], BF16, 2), t1=Ring(nc, es, "r_t1", [128, 512], F32, 2),
                          t2=Ring(nc, es, "r_t2", [128, 512], F32, 2), pp=Ring(nc, es, "r_pp", [128, 512], F32, 2, psum=True))
                for c in range(3):
                    w_, wb = wr.next()
                    S.dma("pool", w_[:], d_wk[c], writes=[wb])
                    for (c0, c1) in _split(0, NL, 512) + [(NL, NT)]:
                        w = c1 - c0
                        ps, psb = accr.next()
                        for k in range(8):
                            S.mm(ps[:, 0:w], w_[:, k, :], ax[:, k, c0:c1], start=(k == 0), stop=(k == 7),
                                 reads=[wb, axb], writes=[psb])
                        emit_rotary(S, crot, ps, psb, w, c0, 0.125, rk[:, c, c0:c1], rkb, rr)
                if stop <= 2:
                    S.emit()
                    return nc
                wv, wvb = load_const(S, nc, es, "wv_sb", d_wv, [128, 8, 384], BF16, q="pool")
                tmr = Ring(nc, es, "tm", [128, 512], F32, 2, psum=True)
                for blk in range(18):
                    c0 = blk * 128
                    ps, psb = tmr.next()
                    for k in range(8):
                        S.mm(ps[:, 0:384], ax[:, k, c0:c0 + 128], wv[:, k, :], start=(k == 0), stop=(k == 7),
                             reads=[wvb, axb], writes=[psb])
                    S.cp("act" if blk % 2 else "dve", vret[:, blk, :], ps[:, 0:384], reads=[psb], writes=[vretb])
                S.barrier()
        if stop <= 3:
            S.emit()
            return nc
        with ExitStack() as es:
            rings = chain_rings(nc, es, deep=True)
            Sfb = _sb(nc, es, "Sfb", [128, 17, 128], BF16)
            Sbb = _sb(nc, es, "Sbb", [128, 17, 128], BF16)
            sbuf_b = Buf()
            S.memset("pool", Sfb[:], 0.0, writes=[sbuf_b])
            S.memset("pool", Sbb[:], 0.0, writes=[sbuf_b])
            outt = _sb(nc, es, "outt", [128, 4, 3, 128], F32)
            outb = Buf()
            S.memset("dve", outt[:], 0.0, writes=[outb])
            for c in range(3):
                for (si, blocks) in ((0, list(range(16))), (1, [16, 17])):
                    kcols = [b * 128 for b in blocks]
                    (sf, sfb_), (sb_, sbb_) = emit_ret_chain(
                        S, nc, rk, rkb, vret, vretb, 0, c, blocks, kcols, blocks, None, None, dec, cst,
                        Sfb, Sbb, sbuf_b, sbuf_b, rings, lambda i: i)
                    for (p0, p1) in ((0, 64), (64, 128)):
                        S.cp("dve", outt[p0:p1, 2 * si, c, p0:p1], sf[p0:p1, p0:p1], reads=[sfb_], writes=[outb])
                        S.cp("dve", outt[p0:p1, 2 * si + 1, c, p0:p1], sb_[p0:p1, p0:p1], reads=[sbb_], writes=[outb])
            S.dma("sp", d_out[:], outt[:], reads=[outb])
        S.emit()
    return nc


def rope_tables(core):
    inv = (10000.0 ** (-np.arange(16, dtype=np.float32) / 16)).astype(np.float32)
    t = np.arange(NL) + core * NL
    row = (t // GW).astype(np.float32)
    col = (t % GW).astype(np.float32)
    p = np.arange(128)
    f = p % 64
    part = f // 32
    j = f % 16
    u = (f % 32) // 16
    pos = np.where(part[:, None] == 0, row[None, :], col[None, :]).astype(np.float32)
    ang = (pos * inv[j][:, None]).astype(np.float32)
    C = np.ones((128, NT), np.float32)
    Sg = np.zeros((128, NT), np.float32)
    C[:, :NL] = np.cos(ang)
    Sg[:, :NL] = np.where(u[:, None] == 0, -np.sin(ang), np.sin(ang))
    P = np.zeros((128, 128), np.float32)
    partner = np.where((f % 32) < 16, p + 16, p - 16)
    P[partner, p] = 1.0
    return np.concatenate([C, Sg, P], 1).astype(NPBF)


def decay_inputs(lg_f, lg_b):
    dk = np.zeros((128, 2, 4, 3, 128), np.float32)
    j = np.arange(128, dtype=np.float32)
    for c in range(3):
        for col in range(128):
            h = 2 * c + col // 64
            dk[:, 0, 0, c, col] = 127 - j
            dk[:, 1, 0, c, col] = lg_f[h]
            dk[:, 0, 1, c, col] = j
            dk[:, 1, 1, c, col] = lg_b[h]
        for p in range(128):
            h = 2 * c + p // 64
            dk[p, 0, 2, c, :] = j + 1
            dk[p, 1, 2, c, :] = lg_f[h]
            dk[p, 0, 3, c, :] = 128 - j
            dk[p, 1, 3, c, :] = lg_b[h]
    cexp = np.zeros((128, 2, 3), np.float32)
    for c in range(3):
        for half in range(2):
            cexp[half * 64:(half + 1) * 64, 0, c] = lg_f[2 * c + half]
            cexp[half * 64:(half + 1) * 64, 1, c] = lg_b[2 * c + half]
    lgc = np.zeros((128, 2, 6), np.float32)
    lgc[:, 0, :] = lg_f[None, :]
    lgc[:, 1, :] = lg_b[None, :]
    return dk.reshape(128, 2, 1536), cexp, lgc


def mod6_layout(mv_l):
    return np.ascontiguousarray(mv_l.reshape(2, 6, 8, 128).transpose(3, 0, 1, 2))


def prep_r_weights(w_in_l):
    wk = w_in_l[:, 1792:2176]
    wk = np.ascontiguousarray(wk.reshape(8, 128, 3, 128).transpose(2, 1, 0, 3))
    wv = np.ascontiguousarray(w_in_l[:, 2176:2560].reshape(8, 128, 384).transpose(1, 0, 2))
    return wk, wv


def build_m1(stop=99):
    nc = bass.Bass("TRN2", target_bir_lowering=False)
    d_h = nc.dram_tensor("hT", [128, 8, NT1], F32, kind="ExternalInput")
    d_m6 = nc.dram_tensor("mod6", [128, 2, 6, 8], F32, kind="ExternalInput")
    d_g = nc.dram_tensor("g12", [128, 2, 8], F32, kind="ExternalInput")
    d_wf = nc.dram_tensor("wfm", [17, 128, 8, 128], F32, kind="ExternalInput")
    d_wt = nc.dram_tensor("wtm", [128, 8, 768], F32, kind="ExternalInput")
    d_rope = nc.dram_tensor("rope", [128, 2 * NT + 128], BF16, kind="ExternalInput")
    d_gt = nc.dram_tensor("gtab", [128, 6, 5, 128], BF16, kind="ExternalInput")
    d_st = nc.dram_tensor("stab", [4, 128, 6, 6, 128], BF16, kind="ExternalInput")
    d_wbd = nc.dram_tensor("wbd", [128, 2, 128], F32, kind="ExternalInput")
    d_pv = nc.dram_tensor("pvec", [128, 2, 3], F32, kind="ExternalInput")
    d_ic = nc.dram_tensor("invcnt", [128, 2, 4, 8], F32, kind="ExternalInput")
    d_dk = nc.dram_tensor("dk", [128, 2, 1536], F32, kind="ExternalInput")
    d_dm = nc.dram_tensor("dm", [128, 4, 128], F32, kind="ExternalInput")
    d_lgc = nc.dram_tensor("lgc", [128, 2, 6], F32, kind="ExternalInput")
    d_ce = nc.dram_tensor("cexp", [128, 2, 3], F32, kind="ExternalInput")
    d_cf = nc.dram_tensor("coef", [128, 3, 54], F32, kind="ExternalInput")
    d_sa = nc.dram_tensor("sall", [128, 2, 3, 9, 128], F32, kind="ExternalInput")
    d_gn = nc.dram_tensor("gng", [128, 3], F32, kind="ExternalInput")
    d_wo = nc.dram_tensor("wout", [128, 8, 1024], F32, kind="ExternalInput")
    d_cst = nc.dram_tensor("cst", [128, 385], F32, kind="ExternalInput")
    d_hm = nc.dram_tensor("hmid", [128, 8, NT], F32, kind="ExternalOutput")
    d_h2 = nc.dram_tensor("h2", [128, 8, NT], BF16, kind="ExternalOutput")
    d_yx = nc.dram_tensor("yxs", [128, 8, NT], BF16)
    d_pp = nc.dram_tensor("pps", [128, 2, NPP], F32)
    dbg = {}
    with ExitStack() as es0:
        S = Sched(nc, es0)
        cst = emit_consts(S, nc, es0, d_cst)
        m6, m6b = load_const(S, nc, es0, "m6_sb", d_m6, [128, 2, 6, 8])
        g12, g12b = load_const(S, nc, es0, "g12_sb", d_g, [128, 2, 8])
        sx = emit_mod_scalars(S, nc, es0, (m6[:, 0], m6b), (g12[:, 0, :], g12b), (g12[:, 1, :], g12b), "sx")
        sc = emit_mod_scalars(S, nc, es0, (m6[:, 1], m6b), (g12[:, 0, :], g12b), (g12[:, 1, :], g12b), "sc")
        yxb = Buf()
        with ExitStack() as esP:
            naq = _sb(nc, esP, "naq", [128, 3, NT], BF16)
            nak = _sb(nc, esP, "nak", [128, 3, NT1], BF16)
            vna = _sb(nc, esP, "vna", [128, 22, 6, 65], BF16)
            rq = _sb(nc, esP, "rq", [128, 3, NT], BF16)
            rk = _sb(nc, esP, "rk", [128, 3, NT], BF16)
            gs = _sb(nc, esP, "gs", [128, 3, NT], BF16)
            vret = _sb(nc, esP, "vret", [128, 18, 384], BF16)
            naqb, nakb, vnab, rqb, rkb, gsb, vretb, ppb = (Buf() for _ in range(8))
            S.memset("pool", vna[:], 1.0, writes=[vnab])
            with ExitStack() as esA:
                ax = _sb(nc, esA, "ax", [128, 8, NT1], BF16)
                axb = Buf()
                tiles = [(a, b, False) for (a, b) in _split(0, NS, 256)] + [(NS, NT1, True)]
                emit_norm_phase(S, nc, cst, d_h, tiles, ax, lambda c0: axb, sx, sc, "1")
                with ExitStack() as es:
                    rope, ropeb = load_const(S, nc, es, "rope_sb", d_rope, [128, 2 * NT + 128], BF16)
                    crot = (rope[:, 0:NT], rope[:, NT:2 * NT], rope[:, 2 * NT:2 * NT + 128], ropeb)
                    wr = Ring(nc, es, "wfm_sb", [128, 8, 128], BF16, 4)
                    accr = Ring(nc, es, "acc", [128, 512], F32, 3, psum=True)
                    rr = dict(raw=Ring(nc, es, "r_raw", [128, 512], BF16, 2), t1=Ring(nc, es, "r_t1", [128, 512], F32, 2),
                              t2=Ring(nc, es, "r_t2", [128, 512], F32, 2),
                              pp=Ring(nc, es, "r_pp", [128, 512], F32, 2, psum=True))
                    ptr = Ring(nc, es, "ptmp", [128, 512], F32, 2)
                    zt = _sb(nc, es, "zt", [128, 2, 8], F32)
                    ztb = Buf()
                    S.memset("dve", zt[:], 0.0, writes=[ztb])
                    S.dma("sp", d_pp[:, :, 2064:2072], zt[:], reads=[ztb], writes=[ppb])
                    S.dma("sp", d_pp[:, :, 2328:2336], zt[:], reads=[ztb], writes=[ppb])
                    for j in range(17):
                        w_, wb = wr.next()
                        S.dma("pool", w_[:], d_wf[j], writes=[wb])
                        kind = ("pool", "pool", "q", "q", "q", "k", "k", "k", "rq", "rq", "rq", "rk", "rk", "rk",
                                "g", "g", "g")[j]
                        tl = FM_TILES_POOL if kind == "pool" else (FM_TILES_ALL if kind == "k" else FM_TILES_LOC)
                        for (c0, c1) in tl:
                            w = c1 - c0
                            ps, psb = accr.next()
                            for k in range(8):
                                S.mm(ps[:, 0:w], w_[:, k, :], ax[:, k, c0:c1], start=(k == 0), stop=(k == 7),
                                     reads=[wb, axb], writes=[psb])
                            lc = loc_col(c0)
                            if kind == "pool":
                                t, tb = ptr.next()
                                S.cp("act", t[:, 0:w], ps[:, 0:w], reads=[psb], writes=[tb])
                                dc = (c0 - 248) if c0 < NS else 2072 + (c0 - NS)
                                S.dma("sp", d_pp[:, j, dc:dc + w], t[:, 0:w], reads=[tb], writes=[ppb])
                            elif kind == "q":
                                S.act(naq[:, j - 2, lc:lc + w], ps[:, 0:w], AF.Copy, scale=0.125, reads=[psb], writes=[naqb])
                            elif kind == "k":
                                S.cp("dve", nak[:, j - 5, c0:c1], ps[:, 0:w], reads=[psb], writes=[nakb])
                            elif kind == "rq":
                                emit_rotary(S, crot, ps, psb, w, lc, 1.0, rq[:, j - 8, lc:lc + w], rqb, rr)
                            elif kind == "rk":
                                emit_rotary(S, crot, ps, psb, w, lc, 0.125, rk[:, j - 11, lc:lc + w], rkb, rr)
                            else:
                                S.act(gs[:, j - 14, lc:lc + w], ps[:, 0:w], AF.Silu, reads=[psb], writes=[gsb])
                    wv, wvb = load_const(S, nc, es, "wv_sb", d_wt, [128, 8, 768], BF16, q="pool")
                    tmr = Ring(nc, es, "tm", [128, 1024], F32, 1, psum=True)
                    for blk in range(22):
                        c0 = blk * 128 if blk < 20 else NS + (blk - 20) * 128
                        ri = blk - 2 if 2 <= blk < 18 else (16 + blk - 20 if blk >= 20 else None)
                        ps, psb = tmr.next()
                        for k in range(8):
                            S.mm(ps[:, 0:384], ax[:, k, c0:c0 + 128], wv[:, k, 0:384], start=(k == 0), stop=(k == 7),
                                 reads=[wvb, axb], writes=[psb])
                        if ri is not None:
                            for k in range(8):
                                S.mm(ps[:, 512:896], ax[:, k, c0:c0 + 128], wv[:, k, 384:768], start=(k == 0),
                                     stop=(k == 7), reads=[wvb, axb], writes=[psb])
                        for h in range(6):
                            S.cp("dve" if h % 2 else "act", vna[:, blk, h, 0:64], ps[:, h * 64:(h + 1) * 64],
                                 reads=[psb], writes=[vnab])
                        if ri is not None:
                            S.cp("act", vret[:, ri, :], ps[:, 512:896], reads=[psb], writes=[vretb])
                    S.barrier()
            if stop <= 1:
                for (nm, t) in (("naq", naq), ("nak", nak), ("rq", rq), ("rk", rk), ("gs", gs)):
                    dd = nc.dram_tensor("dbg_" + nm, list(t.shape), BF16, kind="ExternalOutput")
                    S.dma("sp", dd[:], t[:])
                dd = nc.dram_tensor("dbg_vna", [128, 22, 6, 65], BF16, kind="ExternalOutput")
                S.dma("sp", dd[:], vna[:])
                dd = nc.dram_tensor("dbg_vret", [128, 18, 384], BF16, kind="ExternalOutput")
                S.dma("sp", dd[:], vret[:])
                dd = nc.dram_tensor("dbg_pp", [128, 2, NPP], F32, kind="ExternalOutput")
                S.dma("sp", dd[:], d_pp[:])
                S.emit()
                return nc
            with ExitStack() as es:
                gt, gtb = load_const(S, nc, es, "gt_sb", d_gt, [128, 6, 5, 128], BF16)
                str_ = Ring(nc, es, "st_sb", [128, 6, 6, 128], BF16, 1)
                psS = Ring(nc, es, "psS", [128, 1024], F32, 3, psum=True)
                psO = Ring(nc, es, "psO", [128, 6, 65], F32, 1, psum=True)
                psT = Ring(nc, es, "psT", [128, 3, 128], BF16, 1, psum=True)
                pr = Ring(nc, es, "Pt", [128, 1024], BF16, 4)
                otr = Ring(nc, es, "otok", [128, 384], BF16, 2)
                recr = Ring(nc, es, "rec", [128, 6], F32, 2)
                yr = Ring(nc, es, "yxn", [128, 3, 128], BF16, 2)
                spec = {p: (i, b0) for i, (p, b0) in enumerate(SPECIAL)}
                for pi in range(18):
                    if pi < 16:
                        q0 = 128 * pi
                        if pi in spec:
                            si, b0 = spec[pi]
                            st, stb = str_.next()
                            S.dma("sp", st[:], d_st[si], writes=[stb])
                            kb = [(128 * (b0 + s), (st, stb, s), b0 + s) for s in range(6)]
                        else:
                            kb = [(128 * (pi + s), (gt, gtb, s), pi + s) for s in range(5)]
                    else:
                        q0 = NL + 128 * (pi - 16)
                        kb = []
                    kb = kb + [(NS, None, 20), (NS + 128, None, 21)]
                    nb = len(kb)
                    po, pob = psO.next()
                    for h in range(6):
                        c = h // 2
                        p0 = 64 * (h % 2)
                        sp_, spb = psS.next()
                        for j, (kc0, tab, vb) in enumerate(kb):
                            reg = sp_[:, j * 128:(j + 1) * 128]
                            S.mm(reg, nak[p0:p0 + 64, c, kc0:kc0 + 128], naq[p0:p0 + 64, c, q0:q0 + 128],
                                 start=True, stop=(tab is None), reads=[nakb, naqb], writes=[spb])
                            if tab is not None:
                                tt_, ttb, s = tab
                                S.mm(reg, tt_[:, h, s, :], cst["identb"][:], start=False, stop=True,
                                     reads=[ttb, cst["identb_b"]], writes=[spb])
                        pt, ptb = pr.next()
                        n1 = min(nb, 4) * 128
                        S.act(pt[:, 0:n1], sp_[:, 0:n1], AF.Exp, reads=[spb], writes=[ptb])
                        if nb > 4:
                            S.act(pt[:, 512:nb * 128], sp_[:, 512:nb * 128], AF.Exp, reads=[spb], writes=[ptb])
                        for j, (kc0, tab, vb) in enumerate(kb):
                            S.mm(po[:, h, :], pt[:, j * 128:(j + 1) * 128], vna[:, vb, h, :], start=(j == 0),
                                 stop=(j == nb - 1), reads=[ptb, vnab], writes=[pob])
                    rec, recb = recr.next()
                    S.recip(rec[:], po[:, :, 64], reads=[pob], writes=[recb])
                    ot, otb = otr.next()
                    for h in range(6):
                        S.ts("dve", ot[:, h * 64:(h + 1) * 64], po[:, h, 0:64], rec[:, h:h + 1], ALU.mult,
                             reads=[pob, recb], writes=[otb])
                    pT, pTb = psT.next()
                    for j in range(3):
                        S.tr(pT[:, j, :], ot[:, j * 128:(j + 1) * 128], cst["identb"][:],
                             reads=[otb, cst["identb_b"]], writes=[pTb])
                    yt, ytb = yr.next()
                    S.cp("act", yt[:], pT[:], reads=[pTb], writes=[ytb])
                    S.dma("sp", d_yx[:, 2:5, q0:q0 + 128], yt[:], reads=[ytb], writes=[yxb])
                S.barrier()
            if stop <= 2:
                dd = nc.dram_tensor("dbg_yx", [128, 8, NT], BF16, kind="ExternalOutput")
                S.dma("sp", dd[:, 2:5, :], d_yx[:, 2:5, :])
                S.emit()
                return nc
            with ExitStack() as es:
                dec = emit_decays(S, nc, es, d_dk, d_dm, d_lgc, d_ce, True)
                decb = dec["b"]
                mcomb = dec["mcomb"]
                gng, gngb = load_const(S, nc, es, "gng_sb", d_gn, [128, 3])
                cf, cfb_ = load_const(S, nc, es, "coef_sb", d_cf, [128, 3, 54])
                coef = _sb(nc, es, "coef_e", [128, 54], F32)
                coefb = Buf()
                S.tt("dve", coef[:], cf[:, 0, :], cf[:, 1, :], ALU.mult, reads=[cfb_], writes=[coefb])
                S.act(coef[:], coef[:], AF.Exp, reads=[coefb], writes=[coefb])
                S.tt("dve", coef[:], coef[:], cf[:, 2, :], ALU.mult, reads=[coefb, cfb_], writes=[coefb])
                rings = chain_rings(nc, es)
                sar = Ring(nc, es, "sall_sb", [128, 9, 128], F32, 2)
                sinr = Ring(nc, es, "sin", [128, 128], F32, 4)
                Sfb = _sb(nc, es, "Sfb", [128, 20, 128], BF16)
                Sbb = _sb(nc, es, "Sbb", [128, 20, 128], BF16)
                stb_ = Buf()
                psA = Ring(nc, es, "psA", [128, 128], F32, 2, psum=True)
                psY = Ring(nc, es, "psY", [128, 128], F32, 1, psum=True)
                psG = Ring(nc, es, "psG", [128, 512], F32, 2, psum=True)
                atr = Ring(nc, es, "at", [128, 128], BF16, 4)
                qdr = Ring(nc, es, "qd", [128, 128], BF16, 4)
                ytr = Ring(nc, es, "ytile", [128, 512], F32, 2)
                gnt = Ring(nc, es, "gnt", [128, 512], F32, 6)
                gno = Ring(nc, es, "gno", [128, 512], BF16, 2)
                for c in range(3):
                    S.memset("pool", Sfb[:], 0.0, writes=[stb_])
                    S.memset("pool", Sbb[:], 0.0, writes=[stb_])
                    sins = []
                    for d in range(2):
                        sa, sab = sar.next()
                        S.dma("sp", sa[:], d_sa[:, d, c], writes=[sab])
                        si_, sib = sinr.next()
                        for src in range(9):
                            col = (d * 3 + c) * 9 + src
                            if src == 0:
                                S.ts("dve", si_[:], sa[:, src, :], coef[:, col:col + 1], ALU.mult,
                                     reads=[sab, coefb], writes=[sib])
                            else:
                                S.stt("dve", si_[:], sa[:, src, :], coef[:, col:col + 1], si_[:], ALU.mult, ALU.add,
                                      reads=[sab, coefb, sib], writes=[sib])
                        sins.append((si_, sib))
                    segs = (((list(range(16))), sins[0], sins[1], 0), ([16, 17], None, None, 17))
                    for (blocks, fi, bi, s0) in segs:
                        kcols = [b * 128 for b in blocks]
                        emit_ret_chain(S, nc, rk, rkb, vret, vretb, 0, c, blocks, kcols, blocks,
                                       None if fi is None else (fi[0][:], fi[1]), None if bi is None else (bi[0][:], bi[1]),
                                       dec, cst, Sfb, Sbb, stb_, stb_, rings, lambda i, s0=s0: s0 + i)
                    for (blocks, fi, bi, s0) in segs:
                        nbk = len(blocks)
                        for t0 in range(0, nbk, 4):
                            tb_ = blocks[t0:t0 + 4]
                            wt = 128 * len(tb_)
                            yt, ytb = ytr.next()
                            for ii, blk in enumerate(tb_):
                                n = t0 + ii
                                k0 = blk * 128
                                ats = []
                                for hh in range(2):
                                    p0 = 64 * hh
                                    pa_, pab = psA.next()
                                    S.mm(pa_[:], rk[p0:p0 + 64, c, k0:k0 + 128], rq[p0:p0 + 64, c, k0:k0 + 128],
                                         reads=[rkb, rqb], writes=[pab])
                                    at, atb = atr.next()
                                    S.tt("dve", at[:], pa_[:], mcomb[:, 2 * c + hh, :], ALU.mult, reads=[pab, decb],
                                         writes=[atb])
                                    ats.append((at, atb))
                                qf, qfb = qdr.next()
                                S.tt("pool", qf[:], rq[:, c, k0:k0 + 128], dec["qtab_f"][c], ALU.mult, reads=[rqb, decb],
                                     writes=[qfb])
                                qb_, qbb = qdr.next()
                                S.tt("pool", qb_[:], rq[:, c, k0:k0 + 128], dec["qtab_b"][c], ALU.mult, reads=[rqb, decb],
                                     writes=[qbb])
                                py, pyb = psY.next()
                                S.mm(py[:], Sfb[:, s0 + n, :], qf[:], start=True, stop=False, reads=[stb_, qfb], writes=[pyb])
                                S.mm(py[:], Sbb[:, s0 + n + 1, :], qb_[:], start=False, stop=False, reads=[stb_, qbb],
                                     writes=[pyb])
                                S.mm(py[0:64, :], vret[:, blk, (2 * c) * 64:(2 * c + 1) * 64], ats[0][0][:], start=False,
                                     stop=True, reads=[vretb, ats[0][1]], writes=[pyb])
                                S.mm(py[64:128, :], vret[:, blk, (2 * c + 1) * 64:(2 * c + 2) * 64], ats[1][0][:],
                                     start=False, stop=True, reads=[vretb, ats[1][1]], writes=[pyb], tp=(0, 64))
                                S.cp("act", yt[:, ii * 128:(ii + 1) * 128], py[:], reads=[pyb], writes=[ytb])
                            col0 = blocks[t0] * 128
                            gq = [gnt.next() for _ in range(5)]
                            (ysq, ysqb), (mean, meanb), (m2, m2b), (var, varb), (yc, ycb) = gq
                            S.act(ysq[:, 0:wt], yt[:, 0:wt], AF.Square, reads=[ytb], writes=[ysqb])
                            p1, p1b = psG.next()
                            S.mm(p1[:, 0:wt], cst["bd64"][:], yt[:, 0:wt], reads=[ytb, cst["b"]], writes=[p1b])
                            p2, p2b = psG.next()
                            S.mm(p2[:, 0:wt], cst["bd64"][:], ysq[:, 0:wt], reads=[ysqb, cst["b"]], writes=[p2b])
                            S.cp("act", mean[:, 0:wt], p1[:, 0:wt], reads=[p1b], writes=[meanb])
                            S.tt("pool", m2[:, 0:wt], mean[:, 0:wt], mean[:, 0:wt], ALU.mult, reads=[meanb], writes=[m2b])
                            S.tt("dve", var[:, 0:wt], p2[:, 0:wt], m2[:, 0:wt], ALU.subtract, reads=[p2b, m2b], writes=[varb])
                            S.act(var[:, 0:wt], var[:, 0:wt], AF.Sqrt, bias=cst["eps"][:], scale=1.0, reads=[varb, cst["b"]],
                                  writes=[varb])
                            S.recip(var[:, 0:wt], var[:, 0:wt], reads=[varb], writes=[varb])
                            S.tt("pool", yc[:, 0:wt], yt[:, 0:wt], mean[:, 0:wt], ALU.subtract, reads=[ytb, meanb], writes=[ycb])
                            S.tt("dve", yc[:, 0:wt], yc[:, 0:wt], var[:, 0:wt], ALU.mult, reads=[ycb, varb], writes=[ycb])
                            go, gob = gno.next()
                            S.ts("dve", yc[:, 0:wt], yc[:, 0:wt], gng[:, c:c + 1], ALU.mult, reads=[ycb, gngb], writes=[ycb])
                            S.tt("pool", go[:, 0:wt], yc[:, 0:wt], gs[:, c, col0:col0 + wt], ALU.mult,
                                 reads=[ycb, gsb], writes=[gob])
                            S.dma("sp", d_yx[:, 5 + c, col0:col0 + wt], go[:, 0:wt], reads=[gob], writes=[yxb])
                S.barrier()
            with ExitStack() as es:
                pv, pvb = load_const(S, nc, es, "pv_sb", d_pv, [128, 2, 3])
                ic, icb = load_const(S, nc, es, "ic_sb", d_ic, [128, 2, 4, 8])
                wbd, wbdb = load_const(S, nc, es, "wbd_sb", d_wbd, [128, 2, 128], BF16, q="pool")
                X = _sb(nc, es, "pX", [128, NPP], F32)
                A_ = _sb(nc, es, "pA", [128, NPP], F32)
                B_ = _sb(nc, es, "pB", [128, NPP], F32)
                md = _sb(nc, es, "pmd", [128, NT], BF16)
                t8 = _sb(nc, es, "pt8", [128, 8], F32)
                Xb, Ab, Bb, mdb, t8b = (Buf() for _ in range(5))
                psP = Ring(nc, es, "psP", [128, 512], F32, 2, psum=True)
                ypr = Ring(nc, es, "ypool", [128, 512], BF16, 2)
                N = NPP
                for g2 in range(2):
                    S.dma("sp", X[:], d_pp[:, g2, :], reads=[ppb], writes=[Xb])
                    S.ts("dve", X[:, 0:8], X[:, 0:8], pv[:, 0, 2:3], ALU.mult, reads=[Xb, pvb], writes=[Xb])
                    S.ts("dve", X[:, 2056:2064], X[:, 2056:2064], pv[:, 1, 2:3], ALU.mult, reads=[Xb, pvb], writes=[Xb])
                    S.tt("dve", A_[:, 1:N], X[:, 0:N - 1], X[:, 1:N], ALU.add, reads=[Xb], writes=[Ab])
                    S.tt("pool", B_[:, 2:N - 1], A_[:, 1:N - 2], A_[:, 3:N], ALU.add, reads=[Ab], writes=[Bb])
                    if g2 == 1:
                        S.tt("dve", A_[:, 4:N - 3], B_[:, 2:N - 5], B_[:, 6:N - 1], ALU.add, reads=[Bb], writes=[Ab])
                        S.tt("pool", B_[:, 8:N - 7], A_[:, 4:N - 11], A_[:, 12:N - 3], ALU.add, reads=[Ab], writes=[Bb])
                    for half, src, srcb in ((0, A_, Ab), (1, B_, Bb)):
                        p0, p1 = 64 * half, 64 * half + 64
                        for (xc, mc_, wd_) in ((8, 0, NL), (2072, NL, NCTX)):
                            S.stt("dve", md[p0:p1, mc_:mc_ + wd_], src[p0:p1, xc:xc + wd_], pv[p0:p1, g2, 1:2],
                                  X[p0:p1, xc:xc + wd_], ALU.mult, ALU.subtract, reads=[srcb, Xb, pvb], writes=[mdb])
                        for r, (xc, mc_) in enumerate(((8, 0), (2048, 2040), (2072, 2048), (2320, 2296))):
                            S.tt("dve", t8[p0:p1, :], src[p0:p1, xc:xc + 8], ic[p0:p1, g2, r, :], ALU.mult,
                                 reads=[srcb, icb], writes=[t8b])
                            S.tt("dve", md[p0:p1, mc_:mc_ + 8], t8[p0:p1, :], X[p0:p1, xc:xc + 8], ALU.subtract,
                                 reads=[t8b, Xb, mdb], writes=[mdb])
                    for (c0, c1) in _split(0, NL, 512) + [(NL, NT)]:
                        w = c1 - c0
                        ps, psb = psP.next()
                        S.mm(ps[:, 0:w], wbd[:, g2, :], md[:, c0:c1], reads=[wbdb, mdb], writes=[psb])
                        yp, ypb = ypr.next()
                        S.act(yp[:, 0:w], ps[:, 0:w], AF.Copy, scale=pv[:, g2, 0:1], reads=[psb, pvb], writes=[ypb])
                        S.dma("sp", d_yx[:, g2, c0:c1], yp[:, 0:w], reads=[ypb], writes=[yxb])
                S.barrier()
        if stop <= 3:
            dd = nc.dram_tensor("dbg_yx", [128, 8, NT], BF16, kind="ExternalOutput")
            S.dma("sp", dd[:], d_yx[:])
            S.emit()
            return nc
        with ExitStack() as es:
            wo, wob = load_const(S, nc, es, "wo_sb", d_wo, [128, 8, 1024], BF16, q="pool")
            yxr = Ring(nc, es, "yxt", [128, 8, 256], BF16, 2)
            htr = Ring(nc, es, "o_ht", [128, 8, 256], F32, 2)
            hmr = Ring(nc, es, "o_hm", [128, 8, 256], F32, 2)
            h2r = Ring(nc, es, "o_h2", [128, 8, 256], BF16, 2)
            accr = Ring(nc, es, "o_acc", [128, 256], F32, 3, psum=True)
            sqr = Ring(nc, es, "o_sq", [128, 8, 256], F32, 1)
            psr = Ring(nc, es, "o_nps", [128, 512], F32, 2, psum=True)
            smr = Ring(nc, es, "o_sd", [128, 256], F32, 2)
            tmr = Ring(nc, es, "o_tm", [128, 256], F32, 3)
            for (c0, c1) in _split(0, NL, 256) + [(NL, NT)]:
                w = c1 - c0
                isc = c0 >= NL
                sc_ = sc if isc else sx
                s0 = NS + (c0 - NL) if isc else HB + c0
                yt, ytb = yxr.next()
                S.dma("sp", yt[:, :, 0:w], d_yx[:, :, c0:c1], reads=[yxb], writes=[ytb])
                ht, htb = htr.next()
                S.dma("sp", ht[:, :, 0:w], d_h[:, :, s0:s0 + w], writes=[htb])
                hm, hmb = hmr.next()
                for m in range(8):
                    ps, psb = accr.next()
                    for k in range(8):
                        S.mm(ps[:, 0:w], wo[:, k, m * 128:(m + 1) * 128], yt[:, k, 0:w], start=(k == 0), stop=(k == 7),
                             reads=[wob, ytb], writes=[psb])
                    S.stt("dve", hm[:, m, 0:w], ps[:, 0:w], sc_["G1"][:, m:m + 1], ht[:, m, 0:w], ALU.mult, ALU.add,
                          reads=[psb, htb] + sc_["bufs"], writes=[hmb])
                S.dma("sp", d_hm[:, :, c0:c1], hm[:, :, 0:w], reads=[hmb])
                h2, h2b = h2r.next()
                emit_norm_tile(S, cst, hm[:, :, 0:w], hmb, w, sc_["A2"], sc_["B2"], sc_["bufs"],
                               lambda k, h2=h2, h2b=h2b, w=w: (h2[:, k, 0:w], h2b), sqr, psr, smr, tmr)
                S.dma("sp", d_h2[:, :, c0:c1], h2[:, :, 0:w], reads=[h2b])
        S.emit()
    return nc


FM_COLS = list(range(0, 1024)) + list(range(1408, 2176)) + list(range(2560, 2944))
TM_COLS = list(range(1024, 1408)) + list(range(2176, 2560))


def prep_m1_weights(w_in_l, w_out_l, pool_w_l):
    wf = np.ascontiguousarray(w_in_l[:, FM_COLS].reshape(8, 128, 17, 128).transpose(2, 1, 0, 3))
    wt = np.ascontiguousarray(w_in_l[:, TM_COLS].reshape(8, 128, 768).transpose(1, 0, 2))
    wo = np.ascontiguousarray(w_out_l.reshape(8, 128, 1024).transpose(1, 0, 2))
    wbd = np.zeros((128, 2, 128), np.float32)
    for g2 in range(2):
        for gh in range(2):
            wbd[gh * 64:(gh + 1) * 64, g2, gh * 64:(gh + 1) * 64] = pool_w_l[2 * g2 + gh]
    return wf, wt, wo, wbd


def pool_consts(core, pool_scale_l):
    pv = np.zeros((128, 2, 3), np.float32)
    ic = np.ones((128, 2, 4, 8), np.float32)
    for g2 in range(2):
        pv[:, g2, 0] = pool_scale_l[g2 * 128:(g2 + 1) * 128]
        for gh in range(2):
            w = POOL_WINDOWS[2 * g2 + gh]
            sl = slice(gh * 64, (gh + 1) * 64)
            pv[sl, g2, 1] = 1.0 / w
            regs = ((core * NL + np.arange(8), L), (core * NL + NL - 8 + np.arange(8), L),
                    (np.arange(8), NCTX), (NCTX - 8 + np.arange(8), NCTX))
            for r, (t, ln) in enumerate(regs):
                lo = np.clip(t - w // 2, 0, ln)
                hi = np.clip(t - w // 2 + w, 0, ln)
                ic[sl, g2, r, :] = (1.0 / (hi - lo).astype(np.float32))[None, :]
    pv[:, 0, 2] = 1.0 if core > 0 else 0.0
    pv[:, 1, 2] = 1.0 if core < NCORE - 1 else 0.0
    return pv, ic


def _na_block(rpb_h, qrow0, krow0):
    a = np.repeat(np.arange(2), 64)
    qc = np.tile(np.arange(64), 2)
    qrow = (qrow0 + a)[:, None]
    krow = (krow0 + a)[None, :]
    kc = qc[None, :]
    qcc = qc[:, None]
    rs = np.clip(qrow - 4, 0, ROWS - 8)
    vrow = (krow >= 0) & (krow < ROWS) & (krow >= rs) & (krow < rs + 8)
    ws = np.clip(qcc - 8, 0, GW - 16)
    vcol = (kc >= ws) & (kc < ws + 16)
    dr = np.clip(krow - qrow + 7, 0, 14)
    dc = np.clip(kc - qcc + 15, 0, 30)
    val = rpb_h[dr, dc]
    return np.where(vrow & vcol, val, np.float32(NEG)).astype(np.float32)


def na_tables(rpb_l, core):
    gt = np.zeros((128, 6, 5, 128), np.float32)
    st = np.zeros((4, 128, 6, 6, 128), np.float32)
    for h in range(6):
        for s in range(5):
            gt[:, h, s, :] = _na_block(rpb_l[h], 32 + 8, 32 + 2 * (4 + s) - 4)
        for si, (pi, b0) in enumerate(SPECIAL):
            for s in range(6):
                st[si, :, h, s, :] = _na_block(rpb_l[h], RPC * core + 2 * pi, RPC * core + 2 * (b0 + s) - 4)
    return gt.astype(NPBF), st.astype(NPBF)


def dm_table():
    j = np.arange(128)[:, None]
    i = np.arange(128)[None, :]
    dm = np.zeros((128, 4, 128), np.float32)
    dm[:, 0, :] = np.maximum(i - j, 0)
    dm[:, 1, :] = (i >= j)
    dm[:, 2, :] = np.maximum(j - i, 0)
    dm[:, 3, :] = (j >= i)
    return dm


def coef_inputs(core, lg_f, lg_b):
    cf = np.zeros((128, 3, 2, 3, 9), np.float32)
    for c in range(3):
        for half in range(2):
            sl = slice(half * 64, (half + 1) * 64)
            h = 2 * c + half
            cf[sl, 1, 0, c, :] = lg_f[h]
            cf[sl, 1, 1, c, :] = lg_b[h]
    for j in range(NCORE):
        if j < core:
            cf[:, 0, 0, :, j] = NL * (core - 1 - j)
            cf[:, 2, 0, :, j] = 1.0
        if j > core:
            cf[:, 0, 1, :, j] = NL * (j - core - 1)
            cf[:, 2, 1, :, j] = 1.0
    cf[:, 0, 0, :, 8] = NL * core
    cf[:, 2, 0, :, 8] = 1.0
    cf[:, 0, 1, :, 8] = NL * (NCORE - 1 - core)
    cf[:, 2, 1, :, 8] = 1.0
    return cf.reshape(128, 3, 54)


def assemble_sall(sls):
    sa = np.zeros((128, 2, 3, 9, 128), np.float32)
    for j in range(NCORE):
        sa[:, 0, :, j, :] = sls[j][:, 0]
        sa[:, 1, :, j, :] = sls[j][:, 1]
    sa[:, 0, :, 8, :] = sls[0][:, 2]
    sa[:, 1, :, 8, :] = sls[0][:, 3]
    return sa


def slab_with_halo(h_full, core):
    out = np.zeros((NS, h_full.shape[1]), h_full.dtype)
    t0 = core * NL - HB
    a = max(t0, 0)
    b = min(t0 + NS, L)
    out[a - t0:b - t0] = h_full[a:b]
    return out


def _run(nc, maps):
    res = run_bass_kernel_spmd(nc, maps, core_ids=list(range(NCORE)))
    return res.results


def kernel(x, c, ctx, c_ctx, w_mod, b_mod, norm1_g, w_in, pool_w, pool_scale, na_rpb, ret_decay_fwd,
           ret_decay_bwd, ret_gn_g, w_out, norm2_g, w_up, conv_w, conv_b, w_down, final_g):
    f = lambda a: np.ascontiguousarray(np.asarray(a, dtype=np.float32))
    x, c, ctx, c_ctx, w_mod, b_mod, norm1_g, w_in, pool_w, pool_scale, na_rpb, ret_decay_fwd, ret_decay_bwd, \
        ret_gn_g, w_out, norm2_g, w_up, conv_w, conv_b, w_down, final_g = map(f, (
            x, c, ctx, c_ctx, w_mod, b_mod, norm1_g, w_in, pool_w, pool_scale, na_rpb, ret_decay_fwd, ret_decay_bwd,
            ret_gn_g, w_out, norm2_g, w_up, conv_w, conv_b, w_down, final_g))
    mv = run_mod(c, c_ctx, w_mod, b_mod)
    h = x[0]
    hc = ctx[0]
    cst = make_cst()
    ropes = [rope_tables(i) for i in range(NCORE)]
    dm = dm_table()
    out = None
    for l in range(DEPTH):
        last = l == DEPTH - 1
        mod6 = mod6_layout(mv[l])
        g12 = np.ascontiguousarray(np.stack([vec_pk(norm1_g[l]), vec_pk(norm2_g[l])], 1))
        dk, cexp, lgc = decay_inputs(ret_decay_fwd[l], ret_decay_bwd[l])
        hcT = to_fm(hc)
        wk, wv = prep_r_weights(w_in[l])
        maps = []
        for i in range(NCORE):
            hT = np.concatenate([to_fm(h[i * NL:(i + 1) * NL]), hcT], 2)
            maps.append({"hT": hT, "mod6": mod6, "g12": g12, "wk": wk, "wv": wv, "rope": ropes[i], "dk": dk,
                         "cexp": cexp, "cst": cst})
        res = _run(_prog("r", build_r), maps)
        sall = assemble_sall([res[i]["sl"] for i in range(NCORE)])
        wf, wt, wo, wbd = prep_m1_weights(w_in[l], w_out[l], pool_w[l])
        gng = vec_pk(ret_gn_g[l])
        maps = []
        for i in range(NCORE):
            hT = np.concatenate([to_fm(slab_with_halo(h, i)), hcT], 2)
            pv, ic = pool_consts(i, pool_scale[l])
            gt, st = na_tables(na_rpb[l], i)
            maps.append({"hT": hT, "mod6": mod6, "g12": g12, "wfm": wf, "wtm": wt, "rope": ropes[i], "gtab": gt,
                         "stab": st, "wbd": wbd, "pvec": pv, "invcnt": ic, "dk": dk, "dm": dm, "lgc": lgc,
                         "cexp": cexp, "coef": coef_inputs(i, ret_decay_fwd[l], ret_decay_bwd[l]), "sall": sall,
                         "gng": gng, "wout": wo, "cst": cst})
        res = _run(_prog("m1", build_m1), maps)
        hmid = [res[i]["hmid"] for i in range(NCORE)]
        h2 = [np.asarray(res[i]["h2"]) for i in range(NCORE)]
        wup, cw, wdn = prep_m2_weights(w_up[l], conv_w[l], conv_b[l], w_down[l])
        g2 = np.ascontiguousarray(np.stack([vec_pk(mv[l, 0, 5 * D:]), vec_pk(mv[l, 1, 5 * D:])], 1))
        maps = []
        for i in range(NCORE):
            prev = h2[i - 1][:, :, NL - 1] if i > 0 else None
            nxt = h2[i + 1][:, :, 0] if i < NCORE - 1 else None
            h2e = make_h2e(h2[i][:, :, :NL], prev, nxt, h2[i][:, :, NL:])
            m = {"hmid": hmid[i], "h2e": h2e, "wup": wup, "cw": cw, "wdn": wdn, "g2": g2, "cst": cst}
            if last:
                m["fg"] = vec_pk(final_g)
            maps.append(m)
        if last:
            res = _run(_prog("m2f", lambda: build_m2(True)), maps)
            out = np.concatenate([from_fm(res[i]["outT"]) for i in range(NCORE)], 0)
        else:
            res = _run(_prog("m2", lambda: build_m2(False)), maps)
            h = np.concatenate([from_fm(res[i]["hn"][:, :, :NL]) for i in range(NCORE)], 0)
            hc = from_fm(res[0]["hn"][:, :, NL:])
    return out.reshape(1, L, D).astype(np.float32)
```

```python
import numpy as np
from contextlib import ExitStack
import ml_dtypes
import concourse.bass as bass
import concourse.mybir as mybir
from concourse.bass_utils import run_bass_kernel_spmd

F32 = mybir.dt.float32
BF16 = mybir.dt.bfloat16
ALU = mybir.AluOpType
AF = mybir.ActivationFunctionType
NPBF = ml_dtypes.bfloat16

NCORE = 8
D = 1024
L = 16384
GW = 64
ROWS = 256
RPC = 32
NL = 2048
NCTX = 256
NT = NL + NCTX
HB = 256
HA = 192
NS = HB + NL + HA
NT1 = NS + NCTX
DEPTH = 4
DFF = 2816
EPS = 1e-6
POOL_WINDOWS = (2, 4, 8, 16)
NEG = -1e30
SPECIAL = ((0, 0), (1, 1), (14, 13), (15, 14))


class Buf:
    __slots__ = ("w", "r")

    def __init__(self):
        self.w = None
        self.r = {}


class Sched:
    def __init__(self, nc, es, n_dma_sems=10):
        self.nc = nc
        self.names = ("pe", "act", "dve", "pool", "sp")
        self.sems = []
        self.cnt = []
        self._es = es
        self.esem = {e: self._new_sem("s_" + e) for e in ("pe", "act", "dve", "pool")}
        self.dq = {q: [self._new_sem(f"d_{q}{i}") for i in range(n_dma_sems)] for q in ("sp", "pool", "act")}
        self.dq_next = {q: 0 for q in self.dq}
        self.obs = {e: {} for e in self.names}
        self.prog = {e: [] for e in self.names}

    def _new_sem(self, name):
        s = self._es.enter_context(self.nc.semaphore(name))
        self.sems.append(s)
        self.cnt.append(0)
        return len(self.sems) - 1

    def _wait(self, e, deps):
        best = {}
        for d in deps:
            if d is None:
                continue
            s, v = d
            if best.get(s, 0) < v:
                best[s] = v
        for s, v in best.items():
            if e == "pe" and s == self.esem["pe"]:
                continue
            if self.obs[e].get(s, 0) >= v:
                continue
            self.prog[e].append((0, s, v))
            self.obs[e][s] = v

    def _deps(self, reads, writes, own=None):
        deps = []
        for b in reads:
            deps.append(b.w)
        for b in writes:
            if b.w is not None and b.w[0] != own:
                deps.append(b.w)
            for s, v in b.r.items():
                if s != own:
                    deps.append((s, v))
        return deps

    def _commit(self, ev, reads, writes):
        s, v = ev
        for b in reads:
            if b.r.get(s, 0) < v:
                b.r[s] = v
        for b in writes:
            b.w = ev
            b.r = {}

    def op(self, e, fn, reads=(), writes=()):
        self._wait(e, self._deps(reads, writes))
        s = self.esem[e]
        self.prog[e].append((1, fn, s, 1))
        self.cnt[s] += 1
        ev = (s, self.cnt[s])
        self._commit(ev, reads, writes)
        return ev

    def dma(self, q, out, in_, reads=(), writes=()):
        pool = self.dq[q]
        i = self.dq_next[q]
        self.dq_next[q] = (i + 1) % len(pool)
        s = pool[i]
        deps = self._deps(reads, writes)
        deps.append((s, self.cnt[s]))
        self._wait(q, deps)
        self.prog[q].append((1, (lambda e, out=out, in_=in_: e.dma_start(out=out, in_=in_)), s, 16))
        self.cnt[s] += 16
        ev = (s, self.cnt[s])
        self._commit(ev, reads, writes)
        return ev

    def mm(self, out, lhsT, rhs, start=True, stop=True, reads=(), writes=(), tp=None):
        kw = {} if tp is None else {"tile_position": tp}
        return self.op("pe", lambda e: e.matmul(out, lhsT, rhs, start=start, stop=stop, **kw), reads, writes)

    def tr(self, out, in_, ident, reads=(), writes=()):
        return self.op("pe", lambda e: e.transpose(out, in_, ident), reads, writes)

    def act(self, out, in_, func, bias=None, scale=None, reads=(), writes=()):
        kw = {}
        if bias is not None:
            kw["bias"] = bias
        if scale is not None:
            kw["scale"] = scale
        return self.op("act", lambda e: e.activation(out, in_, func, **kw), reads, writes)

    def tt(self, eng, out, a, b, op, reads=(), writes=()):
        return self.op(eng, lambda e: e.tensor_tensor(out, a, b, op), reads, writes)

    def ts(self, eng, out, a, s1, op0, s2=None, op1=None, reads=(), writes=()):
        if op1 is None:
            return self.op(eng, lambda e: e.tensor_scalar(out, a, s1, None, op0), reads, writes)
        return self.op(eng, lambda e: e.tensor_scalar(out, a, s1, s2, op0, op1), reads, writes)

    def stt(self, eng, out, in0, scalar, in1, op0, op1, reads=(), writes=()):
        return self.op(eng, lambda e: e.scalar_tensor_tensor(out, in0, scalar, in1, op0, op1), reads, writes)

    def cp(self, eng, out, in_, reads=(), writes=()):
        if eng == "act":
            return self.op("act", lambda e: e.copy(out, in_), reads, writes)
        return self.op(eng, lambda e: e.tensor_copy(out, in_), reads, writes)

    def recip(self, out, in_, reads=(), writes=()):
        return self.op("dve", lambda e: e.reciprocal(out, in_), reads, writes)

    def memset(self, eng, ap, val, writes=()):
        return self.op(eng, lambda e: e.memset(ap, val), (), writes)

    def barrier(self):
        deps = [(s, c) for s, c in enumerate(self.cnt) if c > 0]
        for e in self.names:
            self._wait(e, deps)

    def emit(self):
        self.barrier()
        nc = self.nc
        sems = self.sems
        prog = self.prog

        def replay(name):
            def f(eng):
                for it in prog[name]:
                    if it[0] == 0:
                        eng.wait_ge(sems[it[1]], it[2])
                    else:
                        it[1](eng).then_inc(sems[it[2]], it[3])
            return f
        with nc.Block() as block:
            block.sync(replay("sp"))
            block.scalar(replay("act"))
            block.vector(replay("dve"))
            block.gpsimd(replay("pool"))
            block.tensor(replay("pe"))


class Ring:
    def __init__(self, nc, es, name, shape, dt, n, psum=False):
        alloc = nc.psum_tensor if psum else nc.sbuf_tensor
        self.t = [es.enter_context(alloc(f"{name}{i}", shape, dt)) for i in range(n)]
        self.b = [Buf() for _ in range(n)]
        self.i = 0

    def next(self):
        i = self.i
        self.i = (i + 1) % len(self.t)
        return self.t[i], self.b[i]


def _sb(nc, es, name, shape, dt):
    return es.enter_context(nc.sbuf_tensor(name, shape, dt))


def _ps(nc, es, name, shape, dt):
    return es.enter_context(nc.psum_tensor(name, shape, dt))


def _split(c0, c1, step):
    out = []
    while c0 < c1:
        out.append((c0, min(c0 + step, c1)))
        c0 += step
    return out


def load_const(S, nc, es, name, dram, shape, dt=F32, q="sp"):
    t = _sb(nc, es, name, shape, dt)
    b = Buf()
    S.dma(q, t[:], dram[:], writes=[b])
    return t, b


def emit_mod_scalars(S, nc, es, mod6, g1, g2, pfx):
    (m6, m6b), (g1t, g1b), (g2t, g2b) = mod6, g1, g2
    A = _sb(nc, es, pfx + "A", [128, 2, 8], F32)
    b = Buf()
    S.ts("dve", A[:, 0, :], m6[:, 1, :], 1.0, ALU.add, reads=[m6b], writes=[b])
    S.tt("dve", A[:, 0, :], A[:, 0, :], g1t[:], ALU.mult, reads=[b, g1b], writes=[b])
    S.ts("dve", A[:, 1, :], m6[:, 4, :], 1.0, ALU.add, reads=[m6b, b], writes=[b])
    S.tt("dve", A[:, 1, :], A[:, 1, :], g2t[:], ALU.mult, reads=[b, g2b], writes=[b])
    return dict(A1=A[:, 0, :], B1=m6[:, 0, :], G1=m6[:, 2, :], A2=A[:, 1, :], B2=m6[:, 3, :], G2=m6[:, 5, :],
                bufs=[b, m6b])


def emit_norm_stats(S, cst, ht, htb, w, ring_sq, ring_ps):
    sq, sqb = ring_sq.next()
    S.act(sq[:, :, 0:w], ht, AF.Square, reads=[htb], writes=[sqb])
    ps, psb = ring_ps.next()
    for k in range(8):
        S.mm(ps[:, 0:w], cst["ones"][:], sq[:, k, 0:w], start=(k == 0), stop=(k == 7),
             reads=[sqb, cst["b"]], writes=[psb])
    return ps, psb


def emit_norm_apply(S, cst, stats, ht, htb, w, Asc, Bsc, scb, out_fn, ring_small, ring_tmp):
    ps, psb = stats
    sd, sdb = ring_small.next()
    S.act(sd[:, 0:w], ps[:, 0:w], AF.Sqrt, bias=cst["eps"][:], scale=1.0, reads=[psb, cst["b"]], writes=[sdb])
    S.recip(sd[:, 0:w], sd[:, 0:w], reads=[sdb], writes=[sdb])
    for k in range(8):
        dst, dstb = out_fn(k)
        if Bsc is None:
            S.stt("dve", dst, ht[:, k, :], Asc[:, k:k + 1], sd[:, 0:w], ALU.mult, ALU.mult,
                  reads=[htb, sdb] + scb, writes=[dstb])
        else:
            tmp, tmpb = ring_tmp.next()
            S.stt("dve", tmp[:, 0:w], ht[:, k, :], Asc[:, k:k + 1], sd[:, 0:w], ALU.mult, ALU.mult,
                  reads=[htb, sdb] + scb, writes=[tmpb])
            S.act(dst, tmp[:, 0:w], AF.Identity, bias=Bsc[:, k:k + 1], scale=1.0, reads=[tmpb] + scb, writes=[dstb])


def emit_norm_tile(S, cst, ht, htb, w, Asc, Bsc, scb, out_fn, ring_sq, ring_ps, ring_small, ring_tmp):
    st = emit_norm_stats(S, cst, ht, htb, w, ring_sq, ring_ps)
    emit_norm_apply(S, cst, st, ht, htb, w, Asc, Bsc, scb, out_fn, ring_small, ring_tmp)


def emit_consts(S, nc, es, d_cst):
    t, b = load_const(S, nc, es, "cst_sb", d_cst, [128, 385], F32)
    identb = _sb(nc, es, "identb", [128, 128], BF16)
    bb = Buf()
    S.cp("dve", identb[:], t[:, 128:256], reads=[b], writes=[bb])
    return dict(ones=t[:, 0:128], identf=t[:, 128:256], bd64=t[:, 256:384], eps=t[:, 384:385], b=b,
                identb=identb, identb_b=bb)


def make_cst():
    c = np.zeros((128, 385), np.float32)
    c[:, 0:128] = 1.0 / 1024.0
    c[:, 128:256] = np.eye(128, dtype=np.float32)
    bd = np.zeros((128, 128), np.float32)
    bd[:64, :64] = 1.0 / 64
    bd[64:, 64:] = 1.0 / 64
    c[:, 256:384] = bd
    c[:, 384] = EPS
    return c


def emit_rotary(S, cst_rot, ps, psb, w, tcol0, scale, dst, dstb, rings):
    C, Sg, P, rb = cst_rot
    rawb, rawbb = rings["raw"].next()
    S.act(rawb[:, 0:w], ps[:, 0:w], AF.Copy, scale=scale, reads=[psb], writes=[rawbb])
    t1, t1b = rings["t1"].next()
    S.tt("dve", t1[:, 0:w], rawb[:, 0:w], C[:, tcol0:tcol0 + w], ALU.mult, reads=[rawbb, rb], writes=[t1b])
    pp, ppb = rings["pp"].next()
    S.mm(pp[:, 0:w], P[:], rawb[:, 0:w], reads=[rawbb, rb], writes=[ppb])
    t2, t2b = rings["t2"].next()
    S.tt("dve", t2[:, 0:w], pp[:, 0:w], Sg[:, tcol0:tcol0 + w], ALU.mult, reads=[ppb, rb], writes=[t2b])
    S.tt("pool", dst, t1[:, 0:w], t2[:, 0:w], ALU.add, reads=[t1b, t2b], writes=[dstb])


def emit_ret_chain(S, nc, rk, rkb, vret, vretb, vcol0, c, blocks, kcols, vblks, Sf_init, Sb_init,
                   dec, cst, Sfb, Sbb, Sfb_b, Sbb_b, rings, sidx):
    nb = len(blocks)
    ktf, ktb_, cfb, decb = dec["ktab_f"], dec["ktab_b"], dec["cfb"], dec["b"]
    Dbs, Dbsb = rings["dbs"]
    Dfs, Dfsb = rings["dfs"]
    halves = ((0, 64), (64, 128))
    for n in range(nb):
        kc0 = kcols[n]
        tp_, tpb = rings["tps"].next()
        S.tr(tp_[:], rk[:, c, kc0:kc0 + 128], cst["identb"][:], reads=[rkb, cst["identb_b"]], writes=[tpb])
        kdf, kdfb = rings["kd"].next()
        S.tt("dve", kdf[:], tp_[:], ktf[c], ALU.mult, reads=[tpb, decb], writes=[kdfb])
        kdb, kdbb = rings["kd"].next()
        S.tt("dve", kdb[:], tp_[:], ktb_[c], ALU.mult, reads=[tpb, decb], writes=[kdbb])
        vv = vret[:, vblks[n], vcol0 + c * 128: vcol0 + (c + 1) * 128]
        dfp, dfpb = rings["dps"].next()
        S.mm(dfp[:], kdf[:], vv, reads=[kdfb, vretb], writes=[dfpb])
        dbp, dbpb = rings["dps"].next()
        S.mm(dbp[:], kdb[:], vv, reads=[kdbb, vretb], writes=[dbpb])
        S.cp("act", Dfs[:, n, :], dfp[:], reads=[dfpb], writes=[Dfsb[n]])
        S.cp("act", Dbs[:, n, :], dbp[:], reads=[dbpb], writes=[Dbsb[n]])
    f32a, f32ab = rings["sf32"].next()
    if Sf_init is None:
        S.memset("pool", f32a[:], 0.0, writes=[f32ab])
    else:
        S.cp("pool", f32a[:], Sf_init[0], reads=[Sf_init[1]], writes=[f32ab])
    for (p0, p1) in halves:
        S.cp("pool", Sfb[p0:p1, sidx(0), p0:p1], f32a[p0:p1, p0:p1], reads=[f32ab], writes=[Sfb_b])
    cur, curb = f32a, f32ab
    for n in range(nb):
        nxt, nxtb = rings["sf32"].next()
        for (p0, p1) in halves:
            S.stt("dve", nxt[p0:p1, p0:p1], cur[p0:p1, p0:p1], cfb[p0:p1, 0, c:c + 1], Dfs[p0:p1, n, p0:p1],
                  ALU.mult, ALU.add, reads=[curb, Dfsb[n], decb], writes=[nxtb])
            S.cp("pool", Sfb[p0:p1, sidx(n + 1), p0:p1], nxt[p0:p1, p0:p1], reads=[nxtb], writes=[Sfb_b])
        cur, curb = nxt, nxtb
    sf_last = (cur, curb)
    g32, g32b = rings["sb32"].next()
    if Sb_init is None:
        S.memset("pool", g32[:], 0.0, writes=[g32b])
    else:
        S.cp("pool", g32[:], Sb_init[0], reads=[Sb_init[1]], writes=[g32b])
    for (p0, p1) in halves:
        S.cp("pool", Sbb[p0:p1, sidx(nb), p0:p1], g32[p0:p1, p0:p1], reads=[g32b], writes=[Sbb_b])
    cur, curb = g32, g32b
    for n in range(nb - 1, -1, -1):
        nxt, nxtb = rings["sb32"].next()
        for (p0, p1) in halves:
            S.stt("dve", nxt[p0:p1, p0:p1], cur[p0:p1, p0:p1], cfb[p0:p1, 1, c:c + 1], Dbs[p0:p1, n, p0:p1],
                  ALU.mult, ALU.add, reads=[curb, Dbsb[n], decb], writes=[nxtb])
            S.cp("pool", Sbb[p0:p1, sidx(n), p0:p1], nxt[p0:p1, p0:p1], reads=[nxtb], writes=[Sbb_b])
        cur, curb = nxt, nxtb
    return sf_last, (cur, curb)


def build_mod():
    nc = bass.Bass("TRN2", target_bir_lowering=False)
    d_c = nc.dram_tensor("cvec", [128, 16], F32, kind="ExternalInput")
    d_w = nc.dram_tensor("wmod", [DEPTH, 128, 8, 768], F32, kind="ExternalInput")
    d_b = nc.dram_tensor("bmod", [2, DEPTH * 768], F32, kind="ExternalInput")
    d_o = nc.dram_tensor("modv", [2, DEPTH * 768], F32, kind="ExternalOutput")
    with ExitStack() as es:
        S = Sched(nc, es)
        cv, cvb = load_const(S, nc, es, "cv", d_c, [128, 16])
        bt, btb = load_const(S, nc, es, "bt", d_b, [2, DEPTH * 768])
        sv = _sb(nc, es, "sv", [128, 16], F32)
        svb = Buf()
        S.act(sv[:], cv[:], AF.Silu, reads=[cvb], writes=[svb])
        wr = Ring(nc, es, "w", [128, 8, 768], F32, 2)
        pr = Ring(nc, es, "ps", [128, 1024], F32, 2, psum=True)
        ot = _sb(nc, es, "ot", [2, DEPTH * 768], F32)
        otb = Buf()
        for l in range(DEPTH):
            w, wb = wr.next()
            S.dma("sp", w[:], d_w[l], writes=[wb])
            ps, psb = pr.next()
            for (a, b_, o) in ((0, 512, 0), (512, 768, 512)):
                for k in range(8):
                    S.mm(ps[0:2, o:o + (b_ - a)], sv[:, 2 * k:2 * k + 2], w[:, k, a:b_], start=(k == 0), stop=(k == 7),
                         reads=[svb, wb], writes=[psb])
            S.tt("dve", ot[:, l * 768:(l + 1) * 768], ps[0:2, 0:768], bt[:, l * 768:(l + 1) * 768], ALU.add,
                 reads=[psb, btb], writes=[otb])
        S.dma("sp", d_o[:], ot[:], reads=[otb])
        S.emit()
    return nc


def to_fm(a):
    T, F = a.shape
    return np.ascontiguousarray(a.T.reshape(F // 128, 128, T).transpose(1, 0, 2))


def from_fm(a):
    P, K, T = a.shape
    return np.ascontiguousarray(a.transpose(1, 0, 2).reshape(K * 128, T).T)


def vec_pk(v):
    return np.ascontiguousarray(v.reshape(-1, 128).T)


_PROGS = {}


def _prog(name, builder):
    if name not in _PROGS:
        _PROGS[name] = builder()
    return _PROGS[name]


def run_mod(c, c_ctx, w_mod, b_mod, cores=NCORE):
    nc = _prog("mod", build_mod)
    cvec = np.zeros((128, 16), np.float32)
    cvec[:, 0::2] = vec_pk(c.reshape(-1))
    cvec[:, 1::2] = vec_pk(c_ctx.reshape(-1))
    maps = []
    for i in range(cores):
        cols = slice(i * 768, (i + 1) * 768)
        w = w_mod[:, :, cols]
        w = np.ascontiguousarray(w.reshape(DEPTH, 8, 128, 768).transpose(0, 2, 1, 3))
        b = np.ascontiguousarray(np.broadcast_to(b_mod[:, cols].reshape(1, DEPTH * 768), (2, DEPTH * 768)))
        maps.append({"cvec": cvec, "wmod": w, "bmod": b})
    res = run_bass_kernel_spmd(nc, maps, core_ids=list(range(cores)))
    out = np.zeros((DEPTH, 2, 6 * D), np.float32)
    for i in range(cores):
        o = res.results[i]["modv"].reshape(2, DEPTH, 768)
        out[:, :, i * 768:(i + 1) * 768] = o.transpose(1, 0, 2)
    return out


NH2 = 2308


def build_m2(final):
    nc = bass.Bass("TRN2", target_bir_lowering=False)
    ncols = NL if final else NT
    d_hm = nc.dram_tensor("hmid", [128, 8, NT], F32, kind="ExternalInput")
    d_h2 = nc.dram_tensor("h2e", [128, 8, NH2], BF16, kind="ExternalInput")
    d_wu = nc.dram_tensor("wup", [44, 128, 8, 128], F32, kind="ExternalInput")
    d_cw = nc.dram_tensor("cw", [128, 44, 4], F32, kind="ExternalInput")
    d_wd = nc.dram_tensor("wdn", [8, 128, 22, 128], F32, kind="ExternalInput")
    d_g2 = nc.dram_tensor("g2", [128, 2, 8], F32, kind="ExternalInput")
    d_cst = nc.dram_tensor("cst", [128, 385], F32, kind="ExternalInput")
    if final:
        d_fg = nc.dram_tensor("fg", [128, 8], F32, kind="ExternalInput")
        d_hn = nc.dram_tensor("hn_scratch", [128, 8, NL], F32)
        d_out = nc.dram_tensor("outT", [128, 8, NL], F32, kind="ExternalOutput")
    else:
        d_hn = nc.dram_tensor("hn", [128, 8, NT], F32, kind="ExternalOutput")
    if final:
        up_tiles = [(0, 512), (510, 1022), (1020, 1532), (1530, 2042), (2040, 2050)]
        dn_tiles = [(1, 513, 0), (513, 1025, 512), (1025, 1537, 1024), (1537, 2049, 1536)]
    else:
        up_tiles = [(0, 512), (510, 1022), (1020, 1532), (1530, 2042), (2040, 2308)]
        dn_tiles = [(1, 513, 0), (513, 1025, 512), (1025, 1537, 1024), (1537, 2049, 1536), (2051, 2307, 2048)]
    with ExitStack() as es0:
        S = Sched(nc, es0)
        cst = emit_consts(S, nc, es0, d_cst)
        with ExitStack() as es:
            h2, h2b = load_const(S, nc, es, "h2_sb", d_h2, [128, 8, NH2], BF16)
            cw, cwb = load_const(S, nc, es, "cw_sb", d_cw, [128, 44, 4])
            g2, g2b = load_const(S, nc, es, "g2_sb", d_g2, [128, 2, 8])
            g = _sb(nc, es, "g", [128, 22, NH2], BF16)
            gb = [Buf() for _ in range(22)]
            wr = Ring(nc, es, "wu", [128, 8, 128], BF16, 4)
            pa = Ring(nc, es, "pa", [128, 512], F32, 2, psum=True)
            pb = Ring(nc, es, "pb", [128, 512], F32, 2, psum=True)
            pd = Ring(nc, es, "pd", [128, 512], F32, 3, psum=True)
            tar = Ring(nc, es, "ta", [128, 512], F32, 2)
            tbr = Ring(nc, es, "tb", [128, 512], F32, 2)
            sar = Ring(nc, es, "sa", [128, 512], F32, 2)
            wdr = Ring(nc, es, "wd", [128, 22, 128], BF16, 2)
            hmr = Ring(nc, es, "hm", [128, 512], F32, 3)
            hor = Ring(nc, es, "ho", [128, 512], F32, 3)
            for f in range(22):
                wa, wab = wr.next()
                S.dma("pool", wa[:], d_wu[f], writes=[wab])
                wb_, wbb = wr.next()
                S.dma("pool", wb_[:], d_wu[22 + f], writes=[wbb])
                for (c0, c1) in up_tiles:
                    w = c1 - c0
                    psa, psab = pa.next()
                    for k in range(8):
                        S.mm(psa[:, 0:w], wa[:, k, :], h2[:, k, c0:c1], start=(k == 0), stop=(k == 7),
                             reads=[wab, h2b], writes=[psab])
                    psb, psbb = pb.next()
                    for k in range(8):
                        S.mm(psb[:, 0:w], wb_[:, k, :], h2[:, k, c0:c1], start=(k == 0), stop=(k == 7),
                             reads=[wbb, h2b], writes=[psbb])
                    n = w - 2
                    outs = []
                    for (ps_, psb_, ring, j) in ((psa, psab, tar, f), (psb, psbb, tbr, 22 + f)):
                        t, tb = ring.next()
                        S.act(t[:, 0:n], ps_[:, 1:1 + n], AF.Identity, bias=cw[:, j, 3:4], scale=cw[:, j, 1:2],
                              reads=[psb_, cwb], writes=[tb])
                        S.stt("dve", t[:, 0:n], ps_[:, 0:n], cw[:, j, 0:1], t[:, 0:n], ALU.mult, ALU.add,
                              reads=[psb_, cwb, tb], writes=[tb])
                        S.stt("dve", t[:, 0:n], ps_[:, 2:2 + n], cw[:, j, 2:3], t[:, 0:n], ALU.mult, ALU.add,
                              reads=[psb_, cwb, tb], writes=[tb])
                        outs.append((t, tb))
                    (ta, tab), (tb2, tb2b) = outs
                    sa, sab = sar.next()
                    S.act(sa[:, 0:n], ta[:, 0:n], AF.Silu, reads=[tab], writes=[sab])
                    S.tt("pool", g[:, f, c0 + 1:c0 + 1 + n], sa[:, 0:n], tb2[:, 0:n], ALU.mult,
                         reads=[sab, tb2b], writes=[gb[f]])
            for m in range(8):
                wd, wdb = wdr.next()
                S.dma("pool", wd[:], d_wd[m], writes=[wdb])
                for (c0, c1, o0) in dn_tiles:
                    w = c1 - c0
                    ps, psb = pd.next()
                    for f in range(22):
                        S.mm(ps[:, 0:w], wd[:, f, :], g[:, f, c0:c1], start=(f == 0), stop=(f == 21),
                             reads=[wdb, gb[f]], writes=[psb])
                    hm, hmb = hmr.next()
                    S.dma("sp", hm[:, 0:w], d_hm[:, m, o0:o0 + w], writes=[hmb])
                    ho, hob = hor.next()
                    s = 1 if o0 >= NL else 0
                    S.stt("dve", ho[:, 0:w], ps[:, 0:w], g2[:, s, m:m + 1], hm[:, 0:w], ALU.mult, ALU.add,
                          reads=[psb, hmb, g2b], writes=[hob])
                    S.dma("act", d_hn[:, m, o0:o0 + w], ho[:, 0:w], reads=[hob])
            S.barrier()
        if final:
            with ExitStack() as es:
                fg, fgb = load_const(S, nc, es, "fg_sb", d_fg, [128, 8])
                htr = Ring(nc, es, "ht", [128, 8, 256], F32, 2)
                sqr = Ring(nc, es, "sq", [128, 8, 256], F32, 1)
                psr = Ring(nc, es, "nps", [128, 512], F32, 2, psum=True)
                smr = Ring(nc, es, "sd", [128, 256], F32, 2)
                otr = Ring(nc, es, "ot", [128, 8, 256], F32, 2)
                for (c0, c1) in _split(0, NL, 256):
                    w = c1 - c0
                    ht, htb = htr.next()
                    S.dma("sp", ht[:, :, 0:w], d_hn[:, :, c0:c1], writes=[htb])
                    ot, otb = otr.next()
                    emit_norm_tile(S, cst, ht[:, :, 0:w], htb, w, fg, None, [fgb],
                                   lambda k, ot=ot, otb=otb, w=w: (ot[:, k, 0:w], otb), sqr, psr, smr, None)
                    S.dma("act", d_out[:, :, c0:c1], ot[:, :, 0:w], reads=[otb])
        S.emit()
    return nc


def prep_m2_weights(w_up_l, conv_w_l, conv_b_l, w_down_l):
    wup = np.ascontiguousarray(w_up_l.reshape(8, 128, 44, 128).transpose(2, 1, 0, 3))
    cw = np.zeros((128, 44, 4), np.float32)
    cw[:, :, 0:3] = conv_w_l.T.reshape(44, 128, 3).transpose(1, 0, 2)
    cw[:, :, 3] = conv_b_l.reshape(44, 128).T
    wdn = np.ascontiguousarray(w_down_l.reshape(22, 128, 8, 128).transpose(2, 1, 0, 3))
    return wup, cw, wdn


def make_h2e(h2_loc, h2_prev_last, h2_next_first, h2_ctx):
    e = np.zeros((128, 8, NH2), NPBF)
    if h2_prev_last is not None:
        e[:, :, 0] = h2_prev_last
    e[:, :, 1:2049] = h2_loc
    if h2_next_first is not None:
        e[:, :, 2049] = h2_next_first
    e[:, :, 2051:2307] = h2_ctx
    return e


FM_TILES_ALL = [(0, 512), (512, 1024), (1024, 1536), (1536, 2048), (2048, 2496), (2496, 2752)]
FM_TILES_LOC = [(256, 768), (768, 1280), (1280, 1792), (1792, 2304), (2496, 2752)]
FM_TILES_POOL = [(248, 760), (760, 1272), (1272, 1784), (1784, 2296), (2296, 2312), (2496, 2752)]
NPP = 2336


def loc_col(s):
    return s - HB if s < NS else NL + (s - NS)


def emit_norm_phase(S, nc, cst, d_h, ncols_list, ax, axb_of, scal_x, scal_c, which):
    with ExitStack() as es:
        htr = Ring(nc, es, "nht", [128, 8, 256], F32, 3)
        sqr = Ring(nc, es, "nsq", [128, 8, 256], F32, 2)
        psr = Ring(nc, es, "nps", [128, 512], F32, 3, psum=True)
        smr = Ring(nc, es, "nsd", [128, 256], F32, 2)
        tmr = Ring(nc, es, "ntm", [128, 256], F32, 4)
        pend = None

        def apply(p):
            (c0, c1, isc, ht, htb, w, st) = p
            sc = scal_c if isc else scal_x
            b = axb_of(c0)
            emit_norm_apply(S, cst, st, ht[:, :, 0:w], htb, w, sc["A" + which], sc["B" + which], sc["bufs"],
                            lambda k, c0=c0, c1=c1, b=b: (ax[:, k, c0:c1], b), smr, tmr)
        for (c0, c1, isc) in ncols_list:
            w = c1 - c0
            ht, htb = htr.next()
            S.dma("sp", ht[:, :, 0:w], d_h[:, :, c0:c1], writes=[htb])
            st = emit_norm_stats(S, cst, ht[:, :, 0:w], htb, w, sqr, psr)
            if pend is not None:
                apply(pend)
            pend = (c0, c1, isc, ht, htb, w, st)
        apply(pend)
        S.barrier()


def emit_decays(S, nc, es, d_dk, d_dm, d_lgc, d_cexp, need_q):
    dk, dkb = load_const(S, nc, es, "dk_sb", d_dk, [128, 2, 1536])
    tab = _sb(nc, es, "dectab", [128, 1536], F32)
    b = Buf()
    S.tt("dve", tab[:], dk[:, 0, :], dk[:, 1, :], ALU.mult, reads=[dkb], writes=[b])
    S.act(tab[:], tab[:], AF.Exp, reads=[b], writes=[b])
    ce, ceb = load_const(S, nc, es, "ce_sb", d_cexp, [128, 2, 3])
    cfb = _sb(nc, es, "cfb", [128, 2, 3], F32)
    S.act(cfb[:], ce[:], AF.Exp, scale=128.0, reads=[ceb], writes=[b])
    sl = lambda t, c: tab[:, (t * 3 + c) * 128:(t * 3 + c + 1) * 128]
    dec = dict(ktab_f=[sl(0, c) for c in range(3)], ktab_b=[sl(1, c) for c in range(3)],
               qtab_f=[sl(2, c) for c in range(3)], qtab_b=[sl(3, c) for c in range(3)], cfb=cfb, b=b)
    if need_q:
        dm, dmb = load_const(S, nc, es, "dm_sb", d_dm, [128, 4, 128])
        lgc, lgcb = load_const(S, nc, es, "lgc_sb", d_lgc, [128, 2, 6])
        mc = _sb(nc, es, "mcomb", [128, 6, 128], F32)
        tmp = _sb(nc, es, "mctmp", [128, 2, 128], F32)
        tb = Buf()
        for h in range(6):
            S.act(tmp[:, 0, :], dm[:, 0, :], AF.Exp, scale=lgc[:, 0, h:h + 1], reads=[dmb, lgcb], writes=[tb])
            S.act(tmp[:, 1, :], dm[:, 2, :], AF.Exp, scale=lgc[:, 1, h:h + 1], reads=[dmb, lgcb], writes=[tb])
            S.tt("dve", tmp[:, 0, :], tmp[:, 0, :], dm[:, 1, :], ALU.mult, reads=[tb, dmb], writes=[tb])
            S.tt("dve", tmp[:, 1, :], tmp[:, 1, :], dm[:, 3, :], ALU.mult, reads=[tb, dmb], writes=[tb])
            S.tt("dve", mc[:, h, :], tmp[:, 0, :], tmp[:, 1, :], ALU.add, reads=[tb], writes=[b])
        dec["mcomb"] = mc
    return dec


def chain_rings(nc, es, deep=False):
    return dict(
        tps=Ring(nc, es, "c_tps", [128, 128], BF16, 2 if deep else 1, psum=True),
        dps=Ring(nc, es, "c_dps", [128, 128], F32, 4 if deep else 2, psum=True),
        kd=Ring(nc, es, "c_kd", [128, 128], BF16, 6),
        sf32=Ring(nc, es, "c_sf", [128, 128], F32, 3),
        sb32=Ring(nc, es, "c_sb", [128, 128], F32, 3),
        dbs=(_sb(nc, es, "c_dbs", [128, 16, 128], F32), [Buf() for _ in range(16)]),
        dfs=(_sb(nc, es, "c_dfs", [128, 16, 128], F32), [Buf() for _ in range(16)]),
    )


def build_r(stop=99):
    nc = bass.Bass("TRN2", target_bir_lowering=False)
    d_h = nc.dram_tensor("hT", [128, 8, NT], F32, kind="ExternalInput")
    d_m6 = nc.dram_tensor("mod6", [128, 2, 6, 8], F32, kind="ExternalInput")
    d_g = nc.dram_tensor("g12", [128, 2, 8], F32, kind="ExternalInput")
    d_wk = nc.dram_tensor("wk", [3, 128, 8, 128], F32, kind="ExternalInput")
    d_wv = nc.dram_tensor("wv", [128, 8, 384], F32, kind="ExternalInput")
    d_rope = nc.dram_tensor("rope", [128, 2 * NT + 128], BF16, kind="ExternalInput")
    d_dk = nc.dram_tensor("dk", [128, 2, 1536], F32, kind="ExternalInput")
    d_ce = nc.dram_tensor("cexp", [128, 2, 3], F32, kind="ExternalInput")
    d_cst = nc.dram_tensor("cst", [128, 385], F32, kind="ExternalInput")
    d_out = nc.dram_tensor("sl", [128, 4, 3, 128], F32, kind="ExternalOutput")
    with ExitStack() as es0:
        S = Sched(nc, es0)
        cst = emit_consts(S, nc, es0, d_cst)
        m6, m6b = load_const(S, nc, es0, "m6_sb", d_m6, [128, 2, 6, 8])
        g12, g12b = load_const(S, nc, es0, "g12_sb", d_g, [128, 2, 8])
        sx = emit_mod_scalars(S, nc, es0, (m6[:, 0], m6b), (g12[:, 0, :], g12b), (g12[:, 1, :], g12b), "sx")
        sc = emit_mod_scalars(S, nc, es0, (m6[:, 1], m6b), (g12[:, 0, :], g12b), (g12[:, 1, :], g12b), "sc")
        rope, ropeb = load_const(S, nc, es0, "rope_sb", d_rope, [128, 2 * NT + 128], BF16)
        crot = (rope[:, 0:NT], rope[:, NT:2 * NT], rope[:, 2 * NT:2 * NT + 128], ropeb)
        dec = emit_decays(S, nc, es0, d_dk, None, None, d_ce, False)
        rk = _sb(nc, es0, "rk", [128, 3, NT], BF16)
        rkb = Buf()
        vret = _sb(nc, es0, "vret", [128, 18, 384], BF16)
        vretb = Buf()
        with ExitStack() as es1:
            ax = _sb(nc, es1, "ax", [128, 8, NT], BF16)
            axb = Buf()
            tiles = [(a, b, False) for (a, b) in _split(0, NL, 256)] + [(NL, NT, True)]
            emit_norm_phase(S, nc, cst, d_h, tiles, ax, lambda c0: axb, sx, sc, "1")
            if stop <= 1:
                S.emit()
                return nc
            with ExitStack() as es:
                wr = Ring(nc, es, "wfm", [128, 8, 128], BF16, 2)
                accr = Ring(nc, es, "acc", [128, 512], F32, 3, psum=True)
                rr = dict(raw=Ring(nc, es, "r_raw", [128, 512], BF16, 2), t1=Ring(nc, es, "r_t1", [128, 512], F32, 2),
                          t2=Ring(nc, es, "r_t2", [128, 512], F32, 2), pp=Ring(nc, es, "r_pp", [128, 512], F32, 2, psum=True))
                for c in range(3):
                    w_, wb = wr.next()
                    S.dma("pool", w_[:], d_wk[c], writes=[wb])
                    for (c0, c1) in _split(0, NL, 512) + [(NL, NT)]:
                        w = c1 - c0
                        ps, psb = accr.next()
                        for k in range(8):
                            S.mm(ps[:, 0:w], w_[:, k, :], ax[:, k, c0:c1], start=(k == 0), stop=(k == 7),
                                 reads=[wb, axb], writes=[psb])
                        emit_rotary(S, crot, ps, psb, w, c0, 0.125, rk[:, c, c0:c1], rkb, rr)
                if stop <= 2:
                    S.emit()
                    return nc
                wv, wvb = load_const(S, nc, es, "wv_sb", d_wv, [128, 8, 384], BF16, q="pool")
                tmr = Ring(nc, es, "tm", [128, 512], F32, 2, psum=True)
                for blk in range(18):
                    c0 = blk * 128
                    ps, psb = tmr.next()
                    for k in range(8):
                        S.mm(ps[:, 0:384], ax[:, k, c0:c0 + 128], wv[:, k, :], start=(k == 0), stop=(k == 7),
                             reads=[wvb, axb], writes=[psb])
                    S.cp("act" if blk % 2 else "dve", vret[:, blk, :], ps[:, 0:384], reads=[psb], writes=[vretb])
                S.barrier()
        if stop <= 3:
            S.emit()
            return nc
        with ExitStack() as es:
            rings = chain_rings(nc, es, deep=True)
            Sfb = _sb(nc, es, "Sfb", [128, 17, 128], BF16)
            Sbb = _sb(nc, es, "Sbb", [128, 17, 128], BF16)
            sbuf_b = Buf()
            S.memset("pool", Sfb[:], 0.0, writes=[sbuf_b])
            S.memset("pool", Sbb[:], 0.0, writes=[sbuf_b])
            outt = _sb(nc, es, "outt", [128, 4, 3, 128], F32)
            outb = Buf()
            S.memset("dve", outt[:], 0.0, writes=[outb])
            for c in range(3):
                for (si, blocks) in ((0, list(range(16))), (1, [16, 17])):
                    kcols = [b * 128 for b in blocks]
                    (sf, sfb_), (sb_, sbb_) = emit_ret_chain(
                        S, nc, rk, rkb, vret, vretb, 0, c, blocks, kcols, blocks, None, None, dec, cst,
                        Sfb, Sbb, sbuf_b, sbuf_b, rings, lambda i: i)
                    for (p0, p1) in ((0, 64), (64, 128)):
                        S.cp("dve", outt[p0:p1, 2 * si, c, p0:p1], sf[p0:p1, p0:p1], reads=[sfb_], writes=[outb])
                        S.cp("dve", outt[p0:p1, 2 * si + 1, c, p0:p1], sb_[p0:p1, p0:p1], reads=[sbb_], writes=[outb])
            S.dma("sp", d_out[:], outt[:], reads=[outb])
        S.emit()
    return nc


def rope_tables(core):
    inv = (10000.0 ** (-np.arange(16, dtype=np.float32) / 16)).astype(np.float32)
    t = np.arange(NL) + core * NL
    row = (t // GW).astype(np.float32)
    col = (t % GW).astype(np.float32)
    p = np.arange(128)
    f = p % 64
    part = f // 32
    j = f % 16
    u = (f % 32) // 16
    pos = np.where(part[:, None] == 0, row[None, :], col[None, :]).astype(np.float32)
    ang = (pos * inv[j][:, None]).astype(np.float32)
    C = np.ones((128, NT), np.float32)
    Sg = np.zeros((128, NT), np.float32)
    C[:, :NL] = np.cos(ang)
    Sg[:, :NL] = np.where(u[:, None] == 0, -np.sin(ang), np.sin(ang))
    P = np.zeros((128, 128), np.float32)
    partner = np.where((f % 32) < 16, p + 16, p - 16)
    P[partner, p] = 1.0
    return np.concatenate([C, Sg, P], 1).astype(NPBF)


def decay_inputs(lg_f, lg_b):
    dk = np.zeros((128, 2, 4, 3, 128), np.float32)
    j = np.arange(128, dtype=np.float32)
    for c in range(3):
        for col in range(128):
            h = 2 * c + col // 64
            dk[:, 0, 0, c, col] = 127 - j
            dk[:, 1, 0, c, col] = lg_f[h]
            dk[:, 0, 1, c, col] = j
            dk[:, 1, 1, c, col] = lg_b[h]
        for p in range(128):
            h = 2 * c + p // 64
            dk[p, 0, 2, c, :] = j + 1
            dk[p, 1, 2, c, :] = lg_f[h]
            dk[p, 0, 3, c, :] = 128 - j
            dk[p, 1, 3, c, :] = lg_b[h]
    cexp = np.zeros((128, 2, 3), np.float32)
    for c in range(3):
        for half in range(2):
            cexp[half * 64:(half + 1) * 64, 0, c] = lg_f[2 * c + half]
            cexp[half * 64:(half + 1) * 64, 1, c] = lg_b[2 * c + half]
    lgc = np.zeros((128, 2, 6), np.float32)
    lgc[:, 0, :] = lg_f[None, :]
    lgc[:, 1, :] = lg_b[None, :]
    return dk.reshape(128, 2, 1536), cexp, lgc


def mod6_layout(mv_l):
    return np.ascontiguousarray(mv_l.reshape(2, 6, 8, 128).transpose(3, 0, 1, 2))


def prep_r_weights(w_in_l):
    wk = w_in_l[:, 1792:2176]
    wk = np.ascontiguousarray(wk.reshape(8, 128, 3, 128).transpose(2, 1, 0, 3))
    wv = np.ascontiguousarray(w_in_l[:, 2176:2560].reshape(8, 128, 384).transpose(1, 0, 2))
    return wk, wv


def build_m1(stop=99):
    nc = bass.Bass("TRN2", target_bir_lowering=False)
    d_h = nc.dram_tensor("hT", [128, 8, NT1], F32, kind="ExternalInput")
    d_m6 = nc.dram_tensor("mod6", [128, 2, 6, 8], F32, kind="ExternalInput")
    d_g = nc.dram_tensor("g12", [128, 2, 8], F32, kind="ExternalInput")
    d_wf = nc.dram_tensor("wfm", [17, 128, 8, 128], F32, kind="ExternalInput")
    d_wt = nc.dram_tensor("wtm", [128, 8, 768], F32, kind="ExternalInput")
    d_rope = nc.dram_tensor("rope", [128, 2 * NT + 128], BF16, kind="ExternalInput")
    d_gt = nc.dram_tensor("gtab", [128, 6, 640], BF16, kind="ExternalInput")
    d_st = nc.dram_tensor("stab", [4, 128, 6, 768], BF16, kind="ExternalInput")
    d_wbd = nc.dram_tensor("wbd", [128, 2, 128], F32, kind="ExternalInput")
    d_pv = nc.dram_tensor("pvec", [128, 2, 3], F32, kind="ExternalInput")
    d_ic = nc.dram_tensor("invcnt", [128, 2, 4, 8], F32, kind="ExternalInput")
    d_dk = nc.dram_tensor("dk", [128, 2, 1536], F32, kind="ExternalInput")
    d_dm = nc.dram_tensor("dm", [128, 4, 128], F32, kind="ExternalInput")
    d_lgc = nc.dram_tensor("lgc", [128, 2, 6], F32, kind="ExternalInput")
    d_ce = nc.dram_tensor("cexp", [128, 2, 3], F32, kind="ExternalInput")
    d_cf = nc.dram_tensor("coef", [128, 3, 54], F32, kind="ExternalInput")
    d_sa = nc.dram_tensor("sall", [128, 2, 3, 9, 128], F32, kind="ExternalInput")
    d_gn = nc.dram_tensor("gng", [128, 3], F32, kind="ExternalInput")
    d_wo = nc.dram_tensor("wout", [128, 8, 1024], F32, kind="ExternalInput")
    d_cst = nc.dram_tensor("cst", [128, 385], F32, kind="ExternalInput")
    d_hm = nc.dram_tensor("hmid", [128, 8, NT], F32, kind="ExternalOutput")
    d_h2 = nc.dram_tensor("h2", [128, 8, NT], BF16, kind="ExternalOutput")
    d_yx = nc.dram_tensor("yxs", [128, 8, NT], BF16)
    d_pp = nc.dram_tensor("pps", [128, 2, NPP], F32)
    dbg = {}
    with ExitStack() as es0:
        S = Sched(nc, es0)
        cst = emit_consts(S, nc, es0, d_cst)
        m6, m6b = load_const(S, nc, es0, "m6_sb", d_m6, [128, 2, 6, 8])
        g12, g12b = load_const(S, nc, es0, "g12_sb", d_g, [128, 2, 8])
        sx = emit_mod_scalars(S, nc, es0, (m6[:, 0], m6b), (g12[:, 0, :], g12b), (g12[:, 1, :], g12b), "sx")
        sc = emit_mod_scalars(S, nc, es0, (m6[:, 1], m6b), (g12[:, 0, :], g12b), (g12[:, 1, :], g12b), "sc")
        yxb = Buf()
        with ExitStack() as esP:
            naq = _sb(nc, esP, "naq", [128, 3, NT], BF16)
            nak = _sb(nc, esP, "nak", [128, 3, NT1], BF16)
            vna = _sb(nc, esP, "vna", [128, 22, 6, 65], BF16)
            rq = _sb(nc, esP, "rq", [128, 3, NT], BF16)
            rk = _sb(nc, esP, "rk", [128, 3, NT], BF16)
            gs = _sb(nc, esP, "gs", [128, 3, NT], BF16)
            vret = _sb(nc, esP, "vret", [128, 18, 384], BF16)
            naqb, nakb, vnab, rqb, rkb, gsb, vretb, ppb = (Buf() for _ in range(8))
            S.memset("pool", vna[:], 1.0, writes=[vnab])
            with ExitStack() as esA:
                ax = _sb(nc, esA, "ax", [128, 8, NT1], BF16)
                axb = Buf()
                tiles = [(a, b, False) for (a, b) in _split(0, NS, 256)] + [(NS, NT1, True)]
                emit_norm_phase(S, nc, cst, d_h, tiles, ax, lambda c0: axb, sx, sc, "1")
                with ExitStack() as es:
                    rope, ropeb = load_const(S, nc, es, "rope_sb", d_rope, [128, 2 * NT + 128], BF16)
                    crot = (rope[:, 0:NT], rope[:, NT:2 * NT], rope[:, 2 * NT:2 * NT + 128], ropeb)
                    wr = Ring(nc, es, "wfm_sb", [128, 8, 128], BF16, 4)
                    accr = Ring(nc, es, "acc", [128, 512], F32, 3, psum=True)
                    rr = dict(raw=Ring(nc, es, "r_raw", [128, 512], BF16, 2), t1=Ring(nc, es, "r_t1", [128, 512], F32, 2),
                              t2=Ring(nc, es, "r_t2", [128, 512], F32, 2),
                              pp=Ring(nc, es, "r_pp", [128, 512], F32, 2, psum=True))
                    ptr = Ring(nc, es, "ptmp", [128, 512], F32, 2)
                    zt = _sb(nc, es, "zt", [128, 2, 8], F32)
                    ztb = Buf()
                    S.memset("dve", zt[:], 0.0, writes=[ztb])
                    S.dma("sp", d_pp[:, :, 2064:2072], zt[:], reads=[ztb], writes=[ppb])
                    S.dma("sp", d_pp[:, :, 2328:2336], zt[:], reads=[ztb], writes=[ppb])
                    for j in range(17):
                        w_, wb = wr.next()
                        S.dma("pool", w_[:], d_wf[j], writes=[wb])
                        kind = ("pool", "pool", "q", "q", "q", "k", "k", "k", "rq", "rq", "rq", "rk", "rk", "rk",
                                "g", "g", "g")[j]
                        tl = FM_TILES_POOL if kind == "pool" else (FM_TILES_ALL if kind == "k" else FM_TILES_LOC)
                        for (c0, c1) in tl:
                            w = c1 - c0
                            ps, psb = accr.next()
                            for k in range(8):
                                S.mm(ps[:, 0:w], w_[:, k, :], ax[:, k, c0:c1], start=(k == 0), stop=(k == 7),
                                     reads=[wb, axb], writes=[psb])
                            lc = loc_col(c0)
                            if kind == "pool":
                                t, tb = ptr.next()
                                S.cp("act", t[:, 0:w], ps[:, 0:w], reads=[psb], writes=[tb])
                                dc = (c0 - 248) if c0 < NS else 2072 + (c0 - NS)
                                S.dma("sp", d_pp[:, j, dc:dc + w], t[:, 0:w], reads=[tb], writes=[ppb])
                            elif kind == "q":
                                S.act(naq[:, j - 2, lc:lc + w], ps[:, 0:w], AF.Copy, scale=0.125, reads=[psb], writes=[naqb])
                            elif kind == "k":
                                S.cp("dve", nak[:, j - 5, c0:c1], ps[:, 0:w], reads=[psb], writes=[nakb])
                            elif kind == "rq":
                                emit_rotary(S, crot, ps, psb, w, lc, 1.0, rq[:, j - 8, lc:lc + w], rqb, rr)
                            elif kind == "rk":
                                emit_rotary(S, crot, ps, psb, w, lc, 0.125, rk[:, j - 11, lc:lc + w], rkb, rr)
                            else:
                                S.act(gs[:, j - 14, lc:lc + w], ps[:, 0:w], AF.Silu, reads=[psb], writes=[gsb])
                    wv, wvb = load_const(S, nc, es, "wv_sb", d_wt, [128, 8, 768], BF16, q="pool")
                    tmr = Ring(nc, es, "tm", [128, 1024], F32, 1, psum=True)
                    for blk in range(22):
                        c0 = blk * 128 if blk < 20 else NS + (blk - 20) * 128
                        ri = blk - 2 if 2 <= blk < 18 else (16 + blk - 20 if blk >= 20 else None)
                        ps, psb = tmr.next()
                        for k in range(8):
                            S.mm(ps[:, 0:384], ax[:, k, c0:c0 + 128], wv[:, k, 0:384], start=(k == 0), stop=(k == 7),
                                 reads=[wvb, axb], writes=[psb])
                        if ri is not None:
                            for k in range(8):
                                S.mm(ps[:, 512:896], ax[:, k, c0:c0 + 128], wv[:, k, 384:768], start=(k == 0),
                                     stop=(k == 7), reads=[wvb, axb], writes=[psb])
                        for h in range(6):
                            S.cp("dve" if h % 2 else "act", vna[:, blk, h, 0:64], ps[:, h * 64:(h + 1) * 64],
                                 reads=[psb], writes=[vnab])
                        if ri is not None:
                            S.cp("act", vret[:, ri, :], ps[:, 512:896], reads=[psb], writes=[vretb])
                    S.barrier()
            if stop <= 1:
                for (nm, t) in (("naq", naq), ("nak", nak), ("rq", rq), ("rk", rk), ("gs", gs)):
                    dd = nc.dram_tensor("dbg_" + nm, list(t.shape), BF16, kind="ExternalOutput")
                    S.dma("sp", dd[:], t[:])
                dd = nc.dram_tensor("dbg_vna", [128, 22, 6, 65], BF16, kind="ExternalOutput")
                S.dma("sp", dd[:], vna[:])
                dd = nc.dram_tensor("dbg_vret", [128, 18, 384], BF16, kind="ExternalOutput")
                S.dma("sp", dd[:], vret[:])
                dd = nc.dram_tensor("dbg_pp", [128, 2, NPP], F32, kind="ExternalOutput")
                S.dma("sp", dd[:], d_pp[:])
                S.emit()
                return nc
            with ExitStack() as es:
                gt, gtb = load_const(S, nc, es, "gt_sb", d_gt, [128, 6, 640], BF16)
                str_ = Ring(nc, es, "st_sb", [128, 6, 768], BF16, 1)
                psS = Ring(nc, es, "psS", [128, 1024], F32, 3, psum=True)
                psO = Ring(nc, es, "psO", [128, 6, 65], F32, 1, psum=True)
                psT = Ring(nc, es, "psT", [128, 3, 128], BF16, 1, psum=True)
                pr = Ring(nc, es, "Pt", [128, 1024], BF16, 4)
                otr = Ring(nc, es, "otok", [128, 384], BF16, 2)
                recr = Ring(nc, es, "rec", [128, 6], F32, 2)
                yr = Ring(nc, es, "yxn", [128, 3, 128], BF16, 2)
                spec = {p: (i, b0) for i, (p, b0) in enumerate(SPECIAL)}

                def na_pv(st_):
                    (po, pob, h, pt, ptb, kb) = st_
                    nb = len(kb)
                    for j, (kc0, vb) in enumerate(kb):
                        S.mm(po[:, h, :], pt[:, j * 128:(j + 1) * 128], vna[:, vb, h, :], start=(j == 0),
                             stop=(j == nb - 1), reads=[ptb, vnab], writes=[pob])

                def na_epilogue(po, pob, q0):
                    rec, recb = recr.next()
                    S.recip(rec[:], po[:, :, 64], reads=[pob], writes=[recb])
                    ot, otb = otr.next()
                    for h in range(6):
                        S.ts("dve", ot[:, h * 64:(h + 1) * 64], po[:, h, 0:64], rec[:, h:h + 1], ALU.mult,
                             reads=[pob, recb], writes=[otb])
                    pT, pTb = psT.next()
                    for j in range(3):
                        S.tr(pT[:, j, :], ot[:, j * 128:(j + 1) * 128], cst["identb"][:],
                             reads=[otb, cst["identb_b"]], writes=[pTb])
                    yt, ytb = yr.next()
                    S.cp("act", yt[:], pT[:], reads=[pTb], writes=[ytb])
                    S.dma("sp", d_yx[:, 2:5, q0:q0 + 128], yt[:], reads=[ytb], writes=[yxb])

                pend = None
                for pi in range(18):
                    tab = None
                    if pi < 16:
                        q0 = 128 * pi
                        if pi in spec:
                            si, b0 = spec[pi]
                            st, stb = str_.next()
                            S.dma("sp", st[:], d_st[si], writes=[stb])
                            tab = (st, stb)
                            kb = [(128 * (b0 + s), b0 + s) for s in range(6)]
                        else:
                            tab = (gt, gtb)
                            kb = [(128 * (pi + s), pi + s) for s in range(5)]
                    else:
                        q0 = NL + 128 * (pi - 16)
                        kb = []
                    nbl = len(kb)
                    kb = kb + [(NS, 20), (NS + 128, 21)]
                    nb = len(kb)
                    po, pob = psO.next()
                    for h in range(6):
                        c = h // 2
                        p0 = 64 * (h % 2)
                        sp_, spb = psS.next()
                        for j, (kc0, vb) in enumerate(kb):
                            S.mm(sp_[:, j * 128:(j + 1) * 128], nak[p0:p0 + 64, c, kc0:kc0 + 128],
                                 naq[p0:p0 + 64, c, q0:q0 + 128], reads=[nakb, naqb], writes=[spb])
                        if nbl:
                            S.tt("dve", sp_[:, 0:nbl * 128], sp_[:, 0:nbl * 128], tab[0][:, h, 0:nbl * 128], ALU.add,
                                 reads=[spb, tab[1]], writes=[spb])
                        pt, ptb = pr.next()
                        n1 = min(nb, 4) * 128
                        S.act(pt[:, 0:n1], sp_[:, 0:n1], AF.Exp, reads=[spb], writes=[ptb])
                        if nb > 4:
                            S.act(pt[:, 512:nb * 128], sp_[:, 512:nb * 128], AF.Exp, reads=[spb], writes=[ptb])
                        if pend is not None:
                            na_pv(pend[0])
                            if pend[1] is not None:
                                na_epilogue(*pend[1])
                        pend = ((po, pob, h, pt, ptb, kb), (po, pob, q0) if h == 5 else None)
                na_pv(pend[0])
                na_epilogue(*pend[1])
                S.barrier()
            if stop <= 2:
                dd = nc.dram_tensor("dbg_yx", [128, 8, NT], BF16, kind="ExternalOutput")
                S.dma("sp", dd[:, 2:5, :], d_yx[:, 2:5, :])
                S.emit()
                return nc
            with ExitStack() as es:
                dec = emit_decays(S, nc, es, d_dk, d_dm, d_lgc, d_ce, True)
                decb = dec["b"]
                mcomb = dec["mcomb"]
                gng, gngb = load_const(S, nc, es, "gng_sb", d_gn, [128, 3])
                cf, cfb_ = load_const(S, nc, es, "coef_sb", d_cf, [128, 3, 54])
                coef = _sb(nc, es, "coef_e", [128, 54], F32)
                coefb = Buf()
                S.tt("dve", coef[:], cf[:, 0, :], cf[:, 1, :], ALU.mult, reads=[cfb_], writes=[coefb])
                S.act(coef[:], coef[:], AF.Exp, reads=[coefb], writes=[coefb])
                S.tt("dve", coef[:], coef[:], cf[:, 2, :], ALU.mult, reads=[coefb, cfb_], writes=[coefb])
                rings = chain_rings(nc, es)
                sar = Ring(nc, es, "sall_sb", [128, 9, 128], F32, 2)
                sinr = Ring(nc, es, "sin", [128, 128], F32, 4)
                Sfb = _sb(nc, es, "Sfb", [128, 20, 128], BF16)
                Sbb = _sb(nc, es, "Sbb", [128, 20, 128], BF16)
                stb_ = Buf()
                psA = Ring(nc, es, "psA", [128, 128], F32, 2, psum=True)
                psY = Ring(nc, es, "psY", [128, 128], F32, 1, psum=True)
                psG = Ring(nc, es, "psG", [128, 512], F32, 2, psum=True)
                atr = Ring(nc, es, "at", [128, 128], BF16, 4)
                qdr = Ring(nc, es, "qd", [128, 128], BF16, 4)
                ytr = Ring(nc, es, "ytile", [128, 512], F32, 2)
                gnt = Ring(nc, es, "gnt", [128, 512], F32, 6)
                gno = Ring(nc, es, "gno", [128, 512], BF16, 2)
                for c in range(3):
                    S.memset("pool", Sfb[:], 0.0, writes=[stb_])
                    S.memset("pool", Sbb[:], 0.0, writes=[stb_])
                    sins = []
                    for d in range(2):
                        sa, sab = sar.next()
                        S.dma("sp", sa[:], d_sa[:, d, c], writes=[sab])
                        si_, sib = sinr.next()
                        for src in range(9):
                            col = (d * 3 + c) * 9 + src
                            if src == 0:
                                S.ts("dve", si_[:], sa[:, src, :], coef[:, col:col + 1], ALU.mult,
                                     reads=[sab, coefb], writes=[sib])
                            else:
                                S.stt("dve", si_[:], sa[:, src, :], coef[:, col:col + 1], si_[:], ALU.mult, ALU.add,
                                      reads=[sab, coefb, sib], writes=[sib])
                        sins.append((si_, sib))
                    segs = (((list(range(16))), sins[0], sins[1], 0), ([16, 17], None, None, 17))
                    for (blocks, fi, bi, s0) in segs:
                        kcols = [b * 128 for b in blocks]
                        emit_ret_chain(S, nc, rk, rkb, vret, vretb, 0, c, blocks, kcols, blocks,
                                       None if fi is None else (fi[0][:], fi[1]), None if bi is None else (bi[0][:], bi[1]),
                                       dec, cst, Sfb, Sbb, stb_, stb_, rings, lambda i, s0=s0: s0 + i)
                    for (blocks, fi, bi, s0) in segs:
                        nbk = len(blocks)
                        for t0 in range(0, nbk, 4):
                            tb_ = blocks[t0:t0 + 4]
                            wt = 128 * len(tb_)
                            yt, ytb = ytr.next()
                            for ii, blk in enumerate(tb_):
                                n = t0 + ii
                                k0 = blk * 128
                                ats = []
                                for hh in range(2):
                                    p0 = 64 * hh
                                    pa_, pab = psA.next()
                                    S.mm(pa_[:], rk[p0:p0 + 64, c, k0:k0 + 128], rq[p0:p0 + 64, c, k0:k0 + 128],
                                         reads=[rkb, rqb], writes=[pab])
                                    at, atb = atr.next()
                                    S.tt("dve", at[:], pa_[:], mcomb[:, 2 * c + hh, :], ALU.mult, reads=[pab, decb],
                                         writes=[atb])
                                    ats.append((at, atb))
                                qf, qfb = qdr.next()
                                S.tt("pool", qf[:], rq[:, c, k0:k0 + 128], dec["qtab_f"][c], ALU.mult, reads=[rqb, decb],
                                     writes=[qfb])
                                qb_, qbb = qdr.next()
                                S.tt("pool", qb_[:], rq[:, c, k0:k0 + 128], dec["qtab_b"][c], ALU.mult, reads=[rqb, decb],
                                     writes=[qbb])
                                py, pyb = psY.next()
                                S.mm(py[:], Sfb[:, s0 + n, :], qf[:], start=True, stop=False, reads=[stb_, qfb], writes=[pyb])
                                S.mm(py[:], Sbb[:, s0 + n + 1, :], qb_[:], start=False, stop=False, reads=[stb_, qbb],
                                     writes=[pyb])
                                S.mm(py[0:64, :], vret[:, blk, (2 * c) * 64:(2 * c + 1) * 64], ats[0][0][:], start=False,
                                     stop=True, reads=[vretb, ats[0][1]], writes=[pyb])
                                S.mm(py[64:128, :], vret[:, blk, (2 * c + 1) * 64:(2 * c + 2) * 64], ats[1][0][:],
                                     start=False, stop=True, reads=[vretb, ats[1][1]], writes=[pyb], tp=(0, 64))
                                S.cp("act", yt[:, ii * 128:(ii + 1) * 128], py[:], reads=[pyb], writes=[ytb])
                            col0 = blocks[t0] * 128
                            gq = [gnt.next() for _ in range(5)]
                            (ysq, ysqb), (mean, meanb), (m2, m2b), (var, varb), (yc, ycb) = gq
                            S.act(ysq[:, 0:wt], yt[:, 0:wt], AF.Square, reads=[ytb], writes=[ysqb])
                            p1, p1b = psG.next()
                            S.mm(p1[:, 0:wt], cst["bd64"][:], yt[:, 0:wt], reads=[ytb, cst["b"]], writes=[p1b])
                            p2, p2b = psG.next()
                            S.mm(p2[:, 0:wt], cst["bd64"][:], ysq[:, 0:wt], reads=[ysqb, cst["b"]], writes=[p2b])
                            S.cp("act", mean[:, 0:wt], p1[:, 0:wt], reads=[p1b], writes=[meanb])
                            S.tt("pool", m2[:, 0:wt], mean[:, 0:wt], mean[:, 0:wt], ALU.mult, reads=[meanb], writes=[m2b])
                            S.tt("dve", var[:, 0:wt], p2[:, 0:wt], m2[:, 0:wt], ALU.subtract, reads=[p2b, m2b], writes=[varb])
                            S.act(var[:, 0:wt], var[:, 0:wt], AF.Sqrt, bias=cst["eps"][:], scale=1.0, reads=[varb, cst["b"]],
                                  writes=[varb])
                            S.recip(var[:, 0:wt], var[:, 0:wt], reads=[varb], writes=[varb])
                            S.tt("pool", yc[:, 0:wt], yt[:, 0:wt], mean[:, 0:wt], ALU.subtract, reads=[ytb, meanb], writes=[ycb])
                            S.tt("dve", yc[:, 0:wt], yc[:, 0:wt], var[:, 0:wt], ALU.mult, reads=[ycb, varb], writes=[ycb])
                            go, gob = gno.next()
                            S.ts("dve", yc[:, 0:wt], yc[:, 0:wt], gng[:, c:c + 1], ALU.mult, reads=[ycb, gngb], writes=[ycb])
                            S.tt("pool", go[:, 0:wt], yc[:, 0:wt], gs[:, c, col0:col0 + wt], ALU.mult,
                                 reads=[ycb, gsb], writes=[gob])
                            S.dma("sp", d_yx[:, 5 + c, col0:col0 + wt], go[:, 0:wt], reads=[gob], writes=[yxb])
                S.barrier()
            with ExitStack() as es:
                pv, pvb = load_const(S, nc, es, "pv_sb", d_pv, [128, 2, 3])
                ic, icb = load_const(S, nc, es, "ic_sb", d_ic, [128, 2, 4, 8])
                wbd, wbdb = load_const(S, nc, es, "wbd_sb", d_wbd, [128, 2, 128], BF16, q="pool")
                X = _sb(nc, es, "pX", [128, NPP], F32)
                A_ = _sb(nc, es, "pA", [128, NPP], F32)
                B_ = _sb(nc, es, "pB", [128, NPP], F32)
                md = _sb(nc, es, "pmd", [128, NT], BF16)
                t8 = _sb(nc, es, "pt8", [128, 8], F32)
                Xb, Ab, Bb, mdb, t8b = (Buf() for _ in range(5))
                psP = Ring(nc, es, "psP", [128, 512], F32, 2, psum=True)
                ypr = Ring(nc, es, "ypool", [128, 512], BF16, 2)
                N = NPP
                for g2 in range(2):
                    S.dma("sp", X[:], d_pp[:, g2, :], reads=[ppb], writes=[Xb])
                    S.ts("dve", X[:, 0:8], X[:, 0:8], pv[:, 0, 2:3], ALU.mult, reads=[Xb, pvb], writes=[Xb])
                    S.ts("dve", X[:, 2056:2064], X[:, 2056:2064], pv[:, 1, 2:3], ALU.mult, reads=[Xb, pvb], writes=[Xb])
                    S.tt("dve", A_[:, 1:N], X[:, 0:N - 1], X[:, 1:N], ALU.add, reads=[Xb], writes=[Ab])
                    S.tt("pool", B_[:, 2:N - 1], A_[:, 1:N - 2], A_[:, 3:N], ALU.add, reads=[Ab], writes=[Bb])
                    if g2 == 1:
                        S.tt("dve", A_[:, 4:N - 3], B_[:, 2:N - 5], B_[:, 6:N - 1], ALU.add, reads=[Bb], writes=[Ab])
                        S.tt("pool", B_[:, 8:N - 7], A_[:, 4:N - 11], A_[:, 12:N - 3], ALU.add, reads=[Ab], writes=[Bb])
                    for half, src, srcb in ((0, A_, Ab), (1, B_, Bb)):
                        p0, p1 = 64 * half, 64 * half + 64
                        for (xc, mc_, wd_) in ((8, 0, NL), (2072, NL, NCTX)):
                            S.stt("dve", md[p0:p1, mc_:mc_ + wd_], src[p0:p1, xc:xc + wd_], pv[p0:p1, g2, 1:2],
                                  X[p0:p1, xc:xc + wd_], ALU.mult, ALU.subtract, reads=[srcb, Xb, pvb], writes=[mdb])
                        for r, (xc, mc_) in enumerate(((8, 0), (2048, 2040), (2072, 2048), (2320, 2296))):
                            S.tt("dve", t8[p0:p1, :], src[p0:p1, xc:xc + 8], ic[p0:p1, g2, r, :], ALU.mult,
                                 reads=[srcb, icb], writes=[t8b])
                            S.tt("dve", md[p0:p1, mc_:mc_ + 8], t8[p0:p1, :], X[p0:p1, xc:xc + 8], ALU.subtract,
                                 reads=[t8b, Xb, mdb], writes=[mdb])
                    for (c0, c1) in _split(0, NL, 512) + [(NL, NT)]:
                        w = c1 - c0
                        ps, psb = psP.next()
                        S.mm(ps[:, 0:w], wbd[:, g2, :], md[:, c0:c1], reads=[wbdb, mdb], writes=[psb])
                        yp, ypb = ypr.next()
                        S.act(yp[:, 0:w], ps[:, 0:w], AF.Copy, scale=pv[:, g2, 0:1], reads=[psb, pvb], writes=[ypb])
                        S.dma("sp", d_yx[:, g2, c0:c1], yp[:, 0:w], reads=[ypb], writes=[yxb])
                S.barrier()
        if stop <= 3:
            dd = nc.dram_tensor("dbg_yx", [128, 8, NT], BF16, kind="ExternalOutput")
            S.dma("sp", dd[:], d_yx[:])
            S.emit()
            return nc
        with ExitStack() as es:
            wo, wob = load_const(S, nc, es, "wo_sb", d_wo, [128, 8, 1024], BF16, q="pool")
            yxr = Ring(nc, es, "yxt", [128, 8, 256], BF16, 2)
            htr = Ring(nc, es, "o_ht", [128, 8, 256], F32, 2)
            hmr = Ring(nc, es, "o_hm", [128, 8, 256], F32, 2)
            h2r = Ring(nc, es, "o_h2", [128, 8, 256], BF16, 2)
            accr = Ring(nc, es, "o_acc", [128, 256], F32, 3, psum=True)
            sqr = Ring(nc, es, "o_sq", [128, 8, 256], F32, 1)
            psr = Ring(nc, es, "o_nps", [128, 512], F32, 2, psum=True)
            smr = Ring(nc, es, "o_sd", [128, 256], F32, 2)
            tmr = Ring(nc, es, "o_tm", [128, 256], F32, 3)
            for (c0, c1) in _split(0, NL, 256) + [(NL, NT)]:
                w = c1 - c0
                isc = c0 >= NL
                sc_ = sc if isc else sx
                s0 = NS + (c0 - NL) if isc else HB + c0
                yt, ytb = yxr.next()
                S.dma("sp", yt[:, :, 0:w], d_yx[:, :, c0:c1], reads=[yxb], writes=[ytb])
                ht, htb = htr.next()
                S.dma("sp", ht[:, :, 0:w], d_h[:, :, s0:s0 + w], writes=[htb])
                hm, hmb = hmr.next()
                for m in range(8):
                    ps, psb = accr.next()
                    for k in range(8):
                        S.mm(ps[:, 0:w], wo[:, k, m * 128:(m + 1) * 128], yt[:, k, 0:w], start=(k == 0), stop=(k == 7),
                             reads=[wob, ytb], writes=[psb])
                    S.stt("dve", hm[:, m, 0:w], ps[:, 0:w], sc_["G1"][:, m:m + 1], ht[:, m, 0:w], ALU.mult, ALU.add,
                          reads=[psb, htb] + sc_["bufs"], writes=[hmb])
                S.dma("sp", d_hm[:, :, c0:c1], hm[:, :, 0:w], reads=[hmb])
                h2, h2b = h2r.next()
                emit_norm_tile(S, cst, hm[:, :, 0:w], hmb, w, sc_["A2"], sc_["B2"], sc_["bufs"],
                               lambda k, h2=h2, h2b=h2b, w=w: (h2[:, k, 0:w], h2b), sqr, psr, smr, tmr)
                S.dma("sp", d_h2[:, :, c0:c1], h2[:, :, 0:w], reads=[h2b])
        S.emit()
    return nc


FM_COLS = list(range(0, 1024)) + list(range(1408, 2176)) + list(range(2560, 2944))
TM_COLS = list(range(1024, 1408)) + list(range(2176, 2560))


def prep_m1_weights(w_in_l, w_out_l, pool_w_l):
    wf = np.ascontiguousarray(w_in_l[:, FM_COLS].reshape(8, 128, 17, 128).transpose(2, 1, 0, 3))
    wt = np.ascontiguousarray(w_in_l[:, TM_COLS].reshape(8, 128, 768).transpose(1, 0, 2))
    wo = np.ascontiguousarray(w_out_l.reshape(8, 128, 1024).transpose(1, 0, 2))
    wbd = np.zeros((128, 2, 128), np.float32)
    for g2 in range(2):
        for gh in range(2):
            wbd[gh * 64:(gh + 1) * 64, g2, gh * 64:(gh + 1) * 64] = pool_w_l[2 * g2 + gh]
    return wf, wt, wo, wbd


def pool_consts(core, pool_scale_l):
    pv = np.zeros((128, 2, 3), np.float32)
    ic = np.ones((128, 2, 4, 8), np.float32)
    for g2 in range(2):
        pv[:, g2, 0] = pool_scale_l[g2 * 128:(g2 + 1) * 128]
        for gh in range(2):
            w = POOL_WINDOWS[2 * g2 + gh]
            sl = slice(gh * 64, (gh + 1) * 64)
            pv[sl, g2, 1] = 1.0 / w
            regs = ((core * NL + np.arange(8), L), (core * NL + NL - 8 + np.arange(8), L),
                    (np.arange(8), NCTX), (NCTX - 8 + np.arange(8), NCTX))
            for r, (t, ln) in enumerate(regs):
                lo = np.clip(t - w // 2, 0, ln)
                hi = np.clip(t - w // 2 + w, 0, ln)
                ic[sl, g2, r, :] = (1.0 / (hi - lo).astype(np.float32))[None, :]
    pv[:, 0, 2] = 1.0 if core > 0 else 0.0
    pv[:, 1, 2] = 1.0 if core < NCORE - 1 else 0.0
    return pv, ic


def _na_block(rpb_h, qrow0, krow0):
    a = np.repeat(np.arange(2), 64)
    qc = np.tile(np.arange(64), 2)
    qrow = (qrow0 + a)[:, None]
    krow = (krow0 + a)[None, :]
    kc = qc[None, :]
    qcc = qc[:, None]
    rs = np.clip(qrow - 4, 0, ROWS - 8)
    vrow = (krow >= 0) & (krow < ROWS) & (krow >= rs) & (krow < rs + 8)
    ws = np.clip(qcc - 8, 0, GW - 16)
    vcol = (kc >= ws) & (kc < ws + 16)
    dr = np.clip(krow - qrow + 7, 0, 14)
    dc = np.clip(kc - qcc + 15, 0, 30)
    val = rpb_h[dr, dc]
    return np.where(vrow & vcol, val, np.float32(NEG)).astype(np.float32)


def na_tables(rpb_l, core):
    gt = np.zeros((128, 6, 5, 128), np.float32)
    st = np.zeros((4, 128, 6, 6, 128), np.float32)
    for h in range(6):
        for s in range(5):
            gt[:, h, s, :] = _na_block(rpb_l[h], 32 + 8, 32 + 2 * (4 + s) - 4).T
        for si, (pi, b0) in enumerate(SPECIAL):
            for s in range(6):
                st[si, :, h, s, :] = _na_block(rpb_l[h], RPC * core + 2 * pi, RPC * core + 2 * (b0 + s) - 4).T
    return gt.reshape(128, 6, 640).astype(NPBF), st.reshape(4, 128, 6, 768).astype(NPBF)


def dm_table():
    j = np.arange(128)[:, None]
    i = np.arange(128)[None, :]
    dm = np.zeros((128, 4, 128), np.float32)
    dm[:, 0, :] = np.maximum(i - j, 0)
    dm[:, 1, :] = (i >= j)
    dm[:, 2, :] = np.maximum(j - i, 0)
    dm[:, 3, :] = (j >= i)
    return dm


def coef_inputs(core, lg_f, lg_b):
    cf = np.zeros((128, 3, 2, 3, 9), np.float32)
    for c in range(3):
        for half in range(2):
            sl = slice(half * 64, (half + 1) * 64)
            h = 2 * c + half
            cf[sl, 1, 0, c, :] = lg_f[h]
            cf[sl, 1, 1, c, :] = lg_b[h]
    for j in range(NCORE):
        if j < core:
            cf[:, 0, 0, :, j] = NL * (core - 1 - j)
            cf[:, 2, 0, :, j] = 1.0
        if j > core:
            cf[:, 0, 1, :, j] = NL * (j - core - 1)
            cf[:, 2, 1, :, j] = 1.0
    cf[:, 0, 0, :, 8] = NL * core
    cf[:, 2, 0, :, 8] = 1.0
    cf[:, 0, 1, :, 8] = NL * (NCORE - 1 - core)
    cf[:, 2, 1, :, 8] = 1.0
    return cf.reshape(128, 3, 54)


def assemble_sall(sls):
    sa = np.zeros((128, 2, 3, 9, 128), np.float32)
    for j in range(NCORE):
        sa[:, 0, :, j, :] = sls[j][:, 0]
        sa[:, 1, :, j, :] = sls[j][:, 1]
    sa[:, 0, :, 8, :] = sls[0][:, 2]
    sa[:, 1, :, 8, :] = sls[0][:, 3]
    return sa


def slab_with_halo(h_full, core):
    out = np.zeros((NS, h_full.shape[1]), h_full.dtype)
    t0 = core * NL - HB
    a = max(t0, 0)
    b = min(t0 + NS, L)
    out[a - t0:b - t0] = h_full[a:b]
    return out


def _run(nc, maps):
    res = run_bass_kernel_spmd(nc, maps, core_ids=list(range(NCORE)))
    return res.results


def kernel(x, c, ctx, c_ctx, w_mod, b_mod, norm1_g, w_in, pool_w, pool_scale, na_rpb, ret_decay_fwd,
           ret_decay_bwd, ret_gn_g, w_out, norm2_g, w_up, conv_w, conv_b, w_down, final_g):
    f = lambda a: np.ascontiguousarray(np.asarray(a, dtype=np.float32))
    x, c, ctx, c_ctx, w_mod, b_mod, norm1_g, w_in, pool_w, pool_scale, na_rpb, ret_decay_fwd, ret_decay_bwd, \
        ret_gn_g, w_out, norm2_g, w_up, conv_w, conv_b, w_down, final_g = map(f, (
            x, c, ctx, c_ctx, w_mod, b_mod, norm1_g, w_in, pool_w, pool_scale, na_rpb, ret_decay_fwd, ret_decay_bwd,
            ret_gn_g, w_out, norm2_g, w_up, conv_w, conv_b, w_down, final_g))
    mv = run_mod(c, c_ctx, w_mod, b_mod)
    h = x[0]
    hc = ctx[0]
    cst = make_cst()
    ropes = [rope_tables(i) for i in range(NCORE)]
    dm = dm_table()
    out = None
    for l in range(DEPTH):
        last = l == DEPTH - 1
        mod6 = mod6_layout(mv[l])
        g12 = np.ascontiguousarray(np.stack([vec_pk(norm1_g[l]), vec_pk(norm2_g[l])], 1))
        dk, cexp, lgc = decay_inputs(ret_decay_fwd[l], ret_decay_bwd[l])
        hcT = to_fm(hc)
        wk, wv = prep_r_weights(w_in[l])
        maps = []
        for i in range(NCORE):
            hT = np.concatenate([to_fm(h[i * NL:(i + 1) * NL]), hcT], 2)
            maps.append({"hT": hT, "mod6": mod6, "g12": g12, "wk": wk, "wv": wv, "rope": ropes[i], "dk": dk,
                         "cexp": cexp, "cst": cst})
        res = _run(_prog("r", build_r), maps)
        sall = assemble_sall([res[i]["sl"] for i in range(NCORE)])
        wf, wt, wo, wbd = prep_m1_weights(w_in[l], w_out[l], pool_w[l])
        gng = vec_pk(ret_gn_g[l])
        maps = []
        for i in range(NCORE):
            hT = np.concatenate([to_fm(slab_with_halo(h, i)), hcT], 2)
            pv, ic = pool_consts(i, pool_scale[l])
            gt, st = na_tables(na_rpb[l], i)
            maps.append({"hT": hT, "mod6": mod6, "g12": g12, "wfm": wf, "wtm": wt, "rope": ropes[i], "gtab": gt,
                         "stab": st, "wbd": wbd, "pvec": pv, "invcnt": ic, "dk": dk, "dm": dm, "lgc": lgc,
                         "cexp": cexp, "coef": coef_inputs(i, ret_decay_fwd[l], ret_decay_bwd[l]), "sall": sall,
                         "gng": gng, "wout": wo, "cst": cst})
        res = _run(_prog("m1", build_m1), maps)
        hmid = [res[i]["hmid"] for i in range(NCORE)]
        h2 = [np.asarray(res[i]["h2"]) for i in range(NCORE)]
        wup, cw, wdn = prep_m2_weights(w_up[l], conv_w[l], conv_b[l], w_down[l])
        g2 = np.ascontiguousarray(np.stack([vec_pk(mv[l, 0, 5 * D:]), vec_pk(mv[l, 1, 5 * D:])], 1))
        maps = []
        for i in range(NCORE):
            prev = h2[i - 1][:, :, NL - 1] if i > 0 else None
            nxt = h2[i + 1][:, :, 0] if i < NCORE - 1 else None
            h2e = make_h2e(h2[i][:, :, :NL], prev, nxt, h2[i][:, :, NL:])
            m = {"hmid": hmid[i], "h2e": h2e, "wup": wup, "cw": cw, "wdn": wdn, "g2": g2, "cst": cst}
            if last:
                m["fg"] = vec_pk(final_g)
            maps.append(m)
        if last:
            res = _run(_prog("m2f", lambda: build_m2(True)), maps)
            out = np.concatenate([from_fm(res[i]["outT"]) for i in range(NCORE)], 0)
        else:
            res = _run(_prog("m2", lambda: build_m2(False)), maps)
            h = np.concatenate([from_fm(res[i]["hn"][:, :, :NL]) for i in range(NCORE)], 0)
            hc = from_fm(res[0]["hn"][:, :, NL:])
    return out.reshape(1, L, D).astype(np.float32)
```

```python
import numpy as np
from contextlib import ExitStack
import ml_dtypes
import concourse.bass as bass
import concourse.mybir as mybir
from concourse.bass_utils import run_bass_kernel_spmd

F32 = mybir.dt.float32
BF16 = mybir.dt.bfloat16
ALU = mybir.AluOpType
AF = mybir.ActivationFunctionType
NPBF = ml_dtypes.bfloat16

NCORE = 8
D = 1024
L = 16384
GW = 64
ROWS = 256
RPC = 32
NL = 2048
NCTX = 256
NT = NL + NCTX
HB = 256
HA = 192
NS = HB + NL + HA
NT1 = NS + NCTX
DEPTH = 4
DFF = 2816
EPS = 1e-6
POOL_WINDOWS = (2, 4, 8, 16)
NEG = -1e30
SPECIAL = ((0, 0), (1, 1), (14, 13), (15, 14))


class Buf:
    __slots__ = ("w", "r")

    def __init__(self):
        self.w = None
        self.r = {}


class Sched:
    def __init__(self, nc, es, n_dma_sems=10):
        self.nc = nc
        self.names = ("pe", "act", "dve", "pool", "sp")
        self.sems = []
        self.cnt = []
        self._es = es
        self.esem = {e: self._new_sem("s_" + e) for e in ("pe", "act", "dve", "pool")}
        self.dq = {q: [self._new_sem(f"d_{q}{i}") for i in range(n_dma_sems)] for q in ("sp", "pool", "act")}
        self.dq_next = {q: 0 for q in self.dq}
        self.obs = {e: {} for e in self.names}
        self.prog = {e: [] for e in self.names}

    def _new_sem(self, name):
        s = self._es.enter_context(self.nc.semaphore(name))
        self.sems.append(s)
        self.cnt.append(0)
        return len(self.sems) - 1

    def _wait(self, e, deps):
        best = {}
        for d in deps:
            if d is None:
                continue
            s, v = d
            if best.get(s, 0) < v:
                best[s] = v
        for s, v in best.items():
            if e == "pe" and s == self.esem["pe"]:
                continue
            if self.obs[e].get(s, 0) >= v:
                continue
            self.prog[e].append((0, s, v))
            self.obs[e][s] = v

    def _deps(self, reads, writes, own=None):
        deps = []
        for b in reads:
            deps.append(b.w)
        for b in writes:
            if b.w is not None and b.w[0] != own:
                deps.append(b.w)
            for s, v in b.r.items():
                if s != own:
                    deps.append((s, v))
        return deps

    def _commit(self, ev, reads, writes):
        s, v = ev
        for b in reads:
            if b.r.get(s, 0) < v:
                b.r[s] = v
        for b in writes:
            b.w = ev
            b.r = {}

    def op(self, e, fn, reads=(), writes=()):
        self._wait(e, self._deps(reads, writes))
        s = self.esem[e]
        self.prog[e].append((1, fn, s, 1))
        self.cnt[s] += 1
        ev = (s, self.cnt[s])
        self._commit(ev, reads, writes)
        return ev

    def dma(self, q, out, in_, reads=(), writes=()):
        pool = self.dq[q]
        i = self.dq_next[q]
        self.dq_next[q] = (i + 1) % len(pool)
        s = pool[i]
        deps = self._deps(reads, writes)
        deps.append((s, self.cnt[s]))
        self._wait(q, deps)
        self.prog[q].append((1, (lambda e, out=out, in_=in_: e.dma_start(out=out, in_=in_)), s, 16))
        self.cnt[s] += 16
        ev = (s, self.cnt[s])
        self._commit(ev, reads, writes)
        return ev

    def mm(self, out, lhsT, rhs, start=True, stop=True, reads=(), writes=(), tp=None):
        kw = {} if tp is None else {"tile_position": tp}
        return self.op("pe", lambda e: e.matmul(out, lhsT, rhs, start=start, stop=stop, **kw), reads, writes)

    def tr(self, out, in_, ident, reads=(), writes=()):
        return self.op("pe", lambda e: e.transpose(out, in_, ident), reads, writes)

    def act(self, out, in_, func, bias=None, scale=None, reads=(), writes=()):
        kw = {}
        if bias is not None:
            kw["bias"] = bias
        if scale is not None:
            kw["scale"] = scale
        return self.op("act", lambda e: e.activation(out, in_, func, **kw), reads, writes)

    def tt(self, eng, out, a, b, op, reads=(), writes=()):
        return self.op(eng, lambda e: e.tensor_tensor(out, a, b, op), reads, writes)

    def ts(self, eng, out, a, s1, op0, s2=None, op1=None, reads=(), writes=()):
        if op1 is None:
            return self.op(eng, lambda e: e.tensor_scalar(out, a, s1, None, op0), reads, writes)
        return self.op(eng, lambda e: e.tensor_scalar(out, a, s1, s2, op0, op1), reads, writes)

    def stt(self, eng, out, in0, scalar, in1, op0, op1, reads=(), writes=()):
        return self.op(eng, lambda e: e.scalar_tensor_tensor(out, in0, scalar, in1, op0, op1), reads, writes)

    def cp(self, eng, out, in_, reads=(), writes=()):
        if eng == "act":
            return self.op("act", lambda e: e.copy(out, in_), reads, writes)
        return self.op(eng, lambda e: e.tensor_copy(out, in_), reads, writes)

    def recip(self, out, in_, reads=(), writes=()):
        return self.op("dve", lambda e: e.reciprocal(out, in_), reads, writes)

    def memset(self, eng, ap, val, writes=()):
        return self.op(eng, lambda e: e.memset(ap, val), (), writes)

    def barrier(self):
        deps = [(s, c) for s, c in enumerate(self.cnt) if c > 0]
        for e in self.names:
            self._wait(e, deps)

    def emit(self):
        self.barrier()
        nc = self.nc
        sems = self.sems
        prog = self.prog

        def replay(name):
            def f(eng):
                for it in prog[name]:
                    if it[0] == 0:
                        eng.wait_ge(sems[it[1]], it[2])
                    else:
                        it[1](eng).then_inc(sems[it[2]], it[3])
            return f
        with nc.Block() as block:
            block.sync(replay("sp"))
            block.scalar(replay("act"))
            block.vector(replay("dve"))
            block.gpsimd(replay("pool"))
            block.tensor(replay("pe"))


class Ring:
    def __init__(self, nc, es, name, shape, dt, n, psum=False):
        alloc = nc.psum_tensor if psum else nc.sbuf_tensor
        self.t = [es.enter_context(alloc(f"{name}{i}", shape, dt)) for i in range(n)]
        self.b = [Buf() for _ in range(n)]
        self.i = 0

    def next(self):
        i = self.i
        self.i = (i + 1) % len(self.t)
        return self.t[i], self.b[i]


def _sb(nc, es, name, shape, dt):
    return es.enter_context(nc.sbuf_tensor(name, shape, dt))


def _ps(nc, es, name, shape, dt):
    return es.enter_context(nc.psum_tensor(name, shape, dt))


def _split(c0, c1, step):
    out = []
    while c0 < c1:
        out.append((c0, min(c0 + step, c1)))
        c0 += step
    return out


def load_const(S, nc, es, name, dram, shape, dt=F32, q="sp"):
    t = _sb(nc, es, name, shape, dt)
    b = Buf()
    S.dma(q, t[:], dram[:], writes=[b])
    return t, b


def emit_mod_scalars(S, nc, es, mod6, g1, g2, pfx):
    (m6, m6b), (g1t, g1b), (g2t, g2b) = mod6, g1, g2
    A = _sb(nc, es, pfx + "A", [128, 2, 8], F32)
    b = Buf()
    S.ts("dve", A[:, 0, :], m6[:, 1, :], 1.0, ALU.add, reads=[m6b], writes=[b])
    S.tt("dve", A[:, 0, :], A[:, 0, :], g1t[:], ALU.mult, reads=[b, g1b], writes=[b])
    S.ts("dve", A[:, 1, :], m6[:, 4, :], 1.0, ALU.add, reads=[m6b, b], writes=[b])
    S.tt("dve", A[:, 1, :], A[:, 1, :], g2t[:], ALU.mult, reads=[b, g2b], writes=[b])
    return dict(A1=A[:, 0, :], B1=m6[:, 0, :], G1=m6[:, 2, :], A2=A[:, 1, :], B2=m6[:, 3, :], G2=m6[:, 5, :],
                bufs=[b, m6b])


def emit_norm_stats(S, cst, ht, htb, w, ring_sq, ring_ps):
    sq, sqb = ring_sq.next()
    S.act(sq[:, :, 0:w], ht, AF.Square, reads=[htb], writes=[sqb])
    ps, psb = ring_ps.next()
    for k in range(8):
        S.mm(ps[:, 0:w], cst["ones"][:], sq[:, k, 0:w], start=(k == 0), stop=(k == 7),
             reads=[sqb, cst["b"]], writes=[psb])
    return ps, psb


def emit_norm_apply(S, cst, stats, ht, htb, w, Asc, Bsc, scb, out_fn, ring_small, ring_tmp):
    ps, psb = stats
    sd, sdb = ring_small.next()
    S.act(sd[:, 0:w], ps[:, 0:w], AF.Sqrt, bias=cst["eps"][:], scale=1.0, reads=[psb, cst["b"]], writes=[sdb])
    S.recip(sd[:, 0:w], sd[:, 0:w], reads=[sdb], writes=[sdb])
    for k in range(8):
        dst, dstb = out_fn(k)
        if Bsc is None:
            S.stt("dve", dst, ht[:, k, :], Asc[:, k:k + 1], sd[:, 0:w], ALU.mult, ALU.mult,
                  reads=[htb, sdb] + scb, writes=[dstb])
        else:
            tmp, tmpb = ring_tmp.next()
            S.stt("dve", tmp[:, 0:w], ht[:, k, :], Asc[:, k:k + 1], sd[:, 0:w], ALU.mult, ALU.mult,
                  reads=[htb, sdb] + scb, writes=[tmpb])
            S.act(dst, tmp[:, 0:w], AF.Identity, bias=Bsc[:, k:k + 1], scale=1.0, reads=[tmpb] + scb, writes=[dstb])


def emit_norm_tile(S, cst, ht, htb, w, Asc, Bsc, scb, out_fn, ring_sq, ring_ps, ring_small, ring_tmp):
    st = emit_norm_stats(S, cst, ht, htb, w, ring_sq, ring_ps)
    emit_norm_apply(S, cst, st, ht, htb, w, Asc, Bsc, scb, out_fn, ring_small, ring_tmp)


def emit_consts(S, nc, es, d_cst):
    t, b = load_const(S, nc, es, "cst_sb", d_cst, [128, 385], F32)
    identb = _sb(nc, es, "identb", [128, 128], BF16)
    bb = Buf()
    S.cp("dve", identb[:], t[:, 128:256], reads=[b], writes=[bb])
    return dict(ones=t[:, 0:128], identf=t[:, 128:256], bd64=t[:, 256:384], eps=t[:, 384:385], b=b,
                identb=identb, identb_b=bb)


def make_cst():
    c = np.zeros((128, 385), np.float32)
    c[:, 0:128] = 1.0 / 1024.0
    c[:, 128:256] = np.eye(128, dtype=np.float32)
    bd = np.zeros((128, 128), np.float32)
    bd[:64, :64] = 1.0 / 64
    bd[64:, 64:] = 1.0 / 64
    c[:, 256:384] = bd
    c[:, 384] = EPS
    return c


def emit_rotary(S, cst_rot, ps, psb, w, tcol0, scale, dst, dstb, rings):
    C, Sg, P, rb = cst_rot
    rawb, rawbb = rings["raw"].next()
    S.act(rawb[:, 0:w], ps[:, 0:w], AF.Copy, scale=scale, reads=[psb], writes=[rawbb])
    t1, t1b = rings["t1"].next()
    S.tt("dve", t1[:, 0:w], rawb[:, 0:w], C[:, tcol0:tcol0 + w], ALU.mult, reads=[rawbb, rb], writes=[t1b])
    pp, ppb = rings["pp"].next()
    S.mm(pp[:, 0:w], P[:], rawb[:, 0:w], reads=[rawbb, rb], writes=[ppb])
    t2, t2b = rings["t2"].next()
    S.tt("dve", t2[:, 0:w], pp[:, 0:w], Sg[:, tcol0:tcol0 + w], ALU.mult, reads=[ppb, rb], writes=[t2b])
    S.tt("pool", dst, t1[:, 0:w], t2[:, 0:w], ALU.add, reads=[t1b, t2b], writes=[dstb])


def emit_ret_chain(S, nc, rk, rkb, vret, vretb, vcol0, c, blocks, kcols, vblks, Sf_init, Sb_init,
                   dec, cst, Sfb, Sbb, Sfb_b, Sbb_b, rings, sidx):
    nb = len(blocks)
    ktf, ktb_, cfb, decb = dec["ktab_f"], dec["ktab_b"], dec["cfb"], dec["b"]
    Dbs, Dbsb = rings["dbs"]
    Dfs, Dfsb = rings["dfs"]
    halves = ((0, 64), (64, 128))
    for n in range(nb):
        kc0 = kcols[n]
        tp_, tpb = rings["tps"].next()
        S.tr(tp_[:], rk[:, c, kc0:kc0 + 128], cst["identb"][:], reads=[rkb, cst["identb_b"]], writes=[tpb])
        kdf, kdfb = rings["kd"].next()
        S.tt("dve", kdf[:], tp_[:], ktf[c], ALU.mult, reads=[tpb, decb], writes=[kdfb])
        kdb, kdbb = rings["kd"].next()
        S.tt("dve", kdb[:], tp_[:], ktb_[c], ALU.mult, reads=[tpb, decb], writes=[kdbb])
        vv = vret[:, vblks[n], vcol0 + c * 128: vcol0 + (c + 1) * 128]
        dfp, dfpb = rings["dps"].next()
        S.mm(dfp[:], kdf[:], vv, reads=[kdfb, vretb], writes=[dfpb])
        dbp, dbpb = rings["dps"].next()
        S.mm(dbp[:], kdb[:], vv, reads=[kdbb, vretb], writes=[dbpb])
        S.cp("act", Dfs[:, n, :], dfp[:], reads=[dfpb], writes=[Dfsb[n]])
        S.cp("act", Dbs[:, n, :], dbp[:], reads=[dbpb], writes=[Dbsb[n]])
    f32a, f32ab = rings["sf32"].next()
    g32, g32b = rings["sb32"].next()
    for (t_, tb_, init, Sx, Sxb, i0) in ((f32a, f32ab, Sf_init, Sfb, Sfb_b, 0), (g32, g32b, Sb_init, Sbb, Sbb_b, nb)):
        if init is None:
            S.memset("pool", t_[:], 0.0, writes=[tb_])
        else:
            S.cp("pool", t_[:], init[0], reads=[init[1]], writes=[tb_])
        for (p0, p1) in halves:
            S.cp("pool", Sx[p0:p1, sidx(i0), p0:p1], t_[p0:p1, p0:p1], reads=[tb_], writes=[Sxb])
    curf, curfb = f32a, f32ab
    curg, curgb = g32, g32b
    for n in range(nb):
        m = nb - 1 - n
        nxt, nxtb = rings["sf32"].next()
        nyt, nytb = rings["sb32"].next()
        for (p0, p1) in halves:
            S.stt("dve", nxt[p0:p1, p0:p1], curf[p0:p1, p0:p1], cfb[p0:p1, 0, c:c + 1], Dfs[p0:p1, n, p0:p1],
                  ALU.mult, ALU.add, reads=[curfb, Dfsb[n], decb], writes=[nxtb])
            S.stt("dve", nyt[p0:p1, p0:p1], curg[p0:p1, p0:p1], cfb[p0:p1, 1, c:c + 1], Dbs[p0:p1, m, p0:p1],
                  ALU.mult, ALU.add, reads=[curgb, Dbsb[m], decb], writes=[nytb])
        for (p0, p1) in halves:
            S.cp("pool", Sfb[p0:p1, sidx(n + 1), p0:p1], nxt[p0:p1, p0:p1], reads=[nxtb], writes=[Sfb_b])
            S.cp("pool", Sbb[p0:p1, sidx(m), p0:p1], nyt[p0:p1, p0:p1], reads=[nytb], writes=[Sbb_b])
        curf, curfb = nxt, nxtb
        curg, curgb = nyt, nytb
    sf_last = (curf, curfb)
    cur, curb = curg, curgb
    return sf_last, (cur, curb)


def build_mod():
    nc = bass.Bass("TRN2", target_bir_lowering=False)
    d_c = nc.dram_tensor("cvec", [128, 16], F32, kind="ExternalInput")
    d_w = nc.dram_tensor("wmod", [DEPTH, 128, 8, 768], F32, kind="ExternalInput")
    d_b = nc.dram_tensor("bmod", [2, DEPTH * 768], F32, kind="ExternalInput")
    d_o = nc.dram_tensor("modv", [2, DEPTH * 768], F32, kind="ExternalOutput")
    with ExitStack() as es:
        S = Sched(nc, es)
        cv, cvb = load_const(S, nc, es, "cv", d_c, [128, 16])
        bt, btb = load_const(S, nc, es, "bt", d_b, [2, DEPTH * 768])
        sv = _sb(nc, es, "sv", [128, 16], F32)
        svb = Buf()
        S.act(sv[:], cv[:], AF.Silu, reads=[cvb], writes=[svb])
        wr = Ring(nc, es, "w", [128, 8, 768], F32, 2)
        pr = Ring(nc, es, "ps", [128, 1024], F32, 2, psum=True)
        ot = _sb(nc, es, "ot", [2, DEPTH * 768], F32)
        otb = Buf()
        for l in range(DEPTH):
            w, wb = wr.next()
            S.dma("sp", w[:], d_w[l], writes=[wb])
            ps, psb = pr.next()
            for (a, b_, o) in ((0, 512, 0), (512, 768, 512)):
                for k in range(8):
                    S.mm(ps[0:2, o:o + (b_ - a)], sv[:, 2 * k:2 * k + 2], w[:, k, a:b_], start=(k == 0), stop=(k == 7),
                         reads=[svb, wb], writes=[psb])
            S.tt("dve", ot[:, l * 768:(l + 1) * 768], ps[0:2, 0:768], bt[:, l * 768:(l + 1) * 768], ALU.add,
                 reads=[psb, btb], writes=[otb])
        S.dma("sp", d_o[:], ot[:], reads=[otb])
        S.emit()
    return nc


def to_fm(a):
    T, F = a.shape
    return np.ascontiguousarray(a.T.reshape(F // 128, 128, T).transpose(1, 0, 2))


def from_fm(a):
    P, K, T = a.shape
    return np.ascontiguousarray(a.transpose(1, 0, 2).reshape(K * 128, T).T)


def vec_pk(v):
    return np.ascontiguousarray(v.reshape(-1, 128).T)


_PROGS = {}


def _prog(name, builder):
    if name not in _PROGS:
        _PROGS[name] = builder()
    return _PROGS[name]


def run_mod(c, c_ctx, w_mod, b_mod, cores=NCORE):
    nc = _prog("mod", build_mod)
    cvec = np.zeros((128, 16), np.float32)
    cvec[:, 0::2] = vec_pk(c.reshape(-1))
    cvec[:, 1::2] = vec_pk(c_ctx.reshape(-1))
    maps = []
    for i in range(cores):
        cols = slice(i * 768, (i + 1) * 768)
        w = w_mod[:, :, cols]
        w = np.ascontiguousarray(w.reshape(DEPTH, 8, 128, 768).transpose(0, 2, 1, 3))
        b = np.ascontiguousarray(np.broadcast_to(b_mod[:, cols].reshape(1, DEPTH * 768), (2, DEPTH * 768)))
        maps.append({"cvec": cvec, "wmod": w, "bmod": b})
    res = run_bass_kernel_spmd(nc, maps, core_ids=list(range(cores)))
    out = np.zeros((DEPTH, 2, 6 * D), np.float32)
    for i in range(cores):
        o = res.results[i]["modv"].reshape(2, DEPTH, 768)
        out[:, :, i * 768:(i + 1) * 768] = o.transpose(1, 0, 2)
    return out


NH2 = 2308


def build_m2(final):
    nc = bass.Bass("TRN2", target_bir_lowering=False)
    ncols = NL if final else NT
    d_hm = nc.dram_tensor("hmid", [128, 8, NT], F32, kind="ExternalInput")
    d_h2 = nc.dram_tensor("h2e", [128, 8, NH2], BF16, kind="ExternalInput")
    d_wu = nc.dram_tensor("wup", [44, 128, 8, 128], F32, kind="ExternalInput")
    d_cw = nc.dram_tensor("cw", [128, 44, 4], F32, kind="ExternalInput")
    d_wd = nc.dram_tensor("wdn", [8, 128, 22, 128], F32, kind="ExternalInput")
    d_g2 = nc.dram_tensor("g2", [128, 2, 8], F32, kind="ExternalInput")
    d_cst = nc.dram_tensor("cst", [128, 385], F32, kind="ExternalInput")
    if final:
        d_fg = nc.dram_tensor("fg", [128, 8], F32, kind="ExternalInput")
        d_hn = nc.dram_tensor("hn_scratch", [128, 8, NL], F32)
        d_out = nc.dram_tensor("outT", [128, 8, NL], F32, kind="ExternalOutput")
    else:
        d_hn = nc.dram_tensor("hn", [128, 8, NT], F32, kind="ExternalOutput")
    if final:
        up_tiles = [(0, 512), (510, 1022), (1020, 1532), (1530, 2042), (2040, 2050)]
        dn_tiles = [(1, 513, 0), (513, 1025, 512), (1025, 1537, 1024), (1537, 2049, 1536)]
    else:
        up_tiles = [(0, 512), (510, 1022), (1020, 1532), (1530, 2042), (2040, 2308)]
        dn_tiles = [(1, 513, 0), (513, 1025, 512), (1025, 1537, 1024), (1537, 2049, 1536), (2051, 2307, 2048)]
    with ExitStack() as es0:
        S = Sched(nc, es0)
        cst = emit_consts(S, nc, es0, d_cst)
        with ExitStack() as es:
            h2, h2b = load_const(S, nc, es, "h2_sb", d_h2, [128, 8, NH2], BF16)
            cw, cwb = load_const(S, nc, es, "cw_sb", d_cw, [128, 44, 4])
            g2, g2b = load_const(S, nc, es, "g2_sb", d_g2, [128, 2, 8])
            g = _sb(nc, es, "g", [128, 22, NH2], BF16)
            gb = [Buf() for _ in range(22)]
            wr = Ring(nc, es, "wu", [128, 8, 128], BF16, 4)
            pa = Ring(nc, es, "pa", [128, 512], F32, 2, psum=True)
            pb = Ring(nc, es, "pb", [128, 512], F32, 2, psum=True)
            pd = Ring(nc, es, "pd", [128, 512], F32, 3, psum=True)
            tar = Ring(nc, es, "ta", [128, 512], F32, 2)
            tbr = Ring(nc, es, "tb", [128, 512], F32, 2)
            sar = Ring(nc, es, "sa", [128, 512], F32, 2)
            wdr = Ring(nc, es, "wd", [128, 22, 128], BF16, 2)
            hmr = Ring(nc, es, "hm", [128, 512], F32, 3)
            hor = Ring(nc, es, "ho", [128, 512], F32, 3)
            for f in range(22):
                wa, wab = wr.next()
                S.dma("pool", wa[:], d_wu[f], writes=[wab])
                wb_, wbb = wr.next()
                S.dma("pool", wb_[:], d_wu[22 + f], writes=[wbb])
                for (c0, c1) in up_tiles:
                    w = c1 - c0
                    psa, psab = pa.next()
                    for k in range(8):
                        S.mm(psa[:, 0:w], wa[:, k, :], h2[:, k, c0:c1], start=(k == 0), stop=(k == 7),
                             reads=[wab, h2b], writes=[psab])
                    psb, psbb = pb.next()
                    for k in range(8):
                        S.mm(psb[:, 0:w], wb_[:, k, :], h2[:, k, c0:c1], start=(k == 0), stop=(k == 7),
                             reads=[wbb, h2b], writes=[psbb])
                    n = w - 2
                    outs = []
                    for (ps_, psb_, ring, j) in ((psa, psab, tar, f), (psb, psbb, tbr, 22 + f)):
                        t, tb = ring.next()
                        S.act(t[:, 0:n], ps_[:, 1:1 + n], AF.Identity, bias=cw[:, j, 3:4], scale=cw[:, j, 1:2],
                              reads=[psb_, cwb], writes=[tb])
                        S.stt("dve", t[:, 0:n], ps_[:, 0:n], cw[:, j, 0:1], t[:, 0:n], ALU.mult, ALU.add,
                              reads=[psb_, cwb, tb], writes=[tb])
                        S.stt("dve", t[:, 0:n], ps_[:, 2:2 + n], cw[:, j, 2:3], t[:, 0:n], ALU.mult, ALU.add,
                              reads=[psb_, cwb, tb], writes=[tb])
                        outs.append((t, tb))
                    (ta, tab), (tb2, tb2b) = outs
                    sa, sab = sar.next()
                    S.act(sa[:, 0:n], ta[:, 0:n], AF.Silu, reads=[tab], writes=[sab])
                    S.tt("pool", g[:, f, c0 + 1:c0 + 1 + n], sa[:, 0:n], tb2[:, 0:n], ALU.mult,
                         reads=[sab, tb2b], writes=[gb[f]])
            for m in range(8):
                wd, wdb = wdr.next()
                S.dma("pool", wd[:], d_wd[m], writes=[wdb])
                for (c0, c1, o0) in dn_tiles:
                    w = c1 - c0
                    ps, psb = pd.next()
                    for f in range(22):
                        S.mm(ps[:, 0:w], wd[:, f, :], g[:, f, c0:c1], start=(f == 0), stop=(f == 21),
                             reads=[wdb, gb[f]], writes=[psb])
                    hm, hmb = hmr.next()
                    S.dma("sp", hm[:, 0:w], d_hm[:, m, o0:o0 + w], writes=[hmb])
                    ho, hob = hor.next()
                    s = 1 if o0 >= NL else 0
                    S.stt("dve", ho[:, 0:w], ps[:, 0:w], g2[:, s, m:m + 1], hm[:, 0:w], ALU.mult, ALU.add,
                          reads=[psb, hmb, g2b], writes=[hob])
                    S.dma("act", d_hn[:, m, o0:o0 + w], ho[:, 0:w], reads=[hob])
            S.barrier()
        if final:
            with ExitStack() as es:
                fg, fgb = load_const(S, nc, es, "fg_sb", d_fg, [128, 8])
                htr = Ring(nc, es, "ht", [128, 8, 256], F32, 2)
                sqr = Ring(nc, es, "sq", [128, 8, 256], F32, 1)
                psr = Ring(nc, es, "nps", [128, 512], F32, 2, psum=True)
                smr = Ring(nc, es, "sd", [128, 256], F32, 2)
                otr = Ring(nc, es, "ot", [128, 8, 256], F32, 2)
                for (c0, c1) in _split(0, NL, 256):
                    w = c1 - c0
                    ht, htb = htr.next()
                    S.dma("sp", ht[:, :, 0:w], d_hn[:, :, c0:c1], writes=[htb])
                    ot, otb = otr.next()
                    emit_norm_tile(S, cst, ht[:, :, 0:w], htb, w, fg, None, [fgb],
                                   lambda k, ot=ot, otb=otb, w=w: (ot[:, k, 0:w], otb), sqr, psr, smr, None)
                    S.dma("act", d_out[:, :, c0:c1], ot[:, :, 0:w], reads=[otb])
        S.emit()
    return nc


def prep_m2_weights(w_up_l, conv_w_l, conv_b_l, w_down_l):
    wup = np.ascontiguousarray(w_up_l.reshape(8, 128, 44, 128).transpose(2, 1, 0, 3))
    cw = np.zeros((128, 44, 4), np.float32)
    cw[:, :, 0:3] = conv_w_l.T.reshape(44, 128, 3).transpose(1, 0, 2)
    cw[:, :, 3] = conv_b_l.reshape(44, 128).T
    wdn = np.ascontiguousarray(w_down_l.reshape(22, 128, 8, 128).transpose(2, 1, 0, 3))
    return wup, cw, wdn


def make_h2e(h2_loc, h2_prev_last, h2_next_first, h2_ctx):
    e = np.zeros((128, 8, NH2), NPBF)
    if h2_prev_last is not None:
        e[:, :, 0] = h2_prev_last
    e[:, :, 1:2049] = h2_loc
    if h2_next_first is not None:
        e[:, :, 2049] = h2_next_first
    e[:, :, 2051:2307] = h2_ctx
    return e


FM_TILES_ALL = [(0, 512), (512, 1024), (1024, 1536), (1536, 2048), (2048, 2496), (2496, 2752)]
FM_TILES_LOC = [(256, 768), (768, 1280), (1280, 1792), (1792, 2304), (2496, 2752)]
FM_TILES_POOL = [(248, 760), (760, 1272), (1272, 1784), (1784, 2296), (2296, 2312), (2496, 2752)]
NPP = 2336


def loc_col(s):
    return s - HB if s < NS else NL + (s - NS)


def emit_norm_phase(S, nc, cst, d_h, ncols_list, ax, axb_of, scal_x, scal_c, which):
    with ExitStack() as es:
        htr = Ring(nc, es, "nht", [128, 8, 256], F32, 3)
        sqr = Ring(nc, es, "nsq", [128, 8, 256], F32, 2)
        psr = Ring(nc, es, "nps", [128, 512], F32, 3, psum=True)
        smr = Ring(nc, es, "nsd", [128, 256], F32, 2)
        tmr = Ring(nc, es, "ntm", [128, 256], F32, 4)
        pend = None

        def apply(p):
            (c0, c1, isc, ht, htb, w, st) = p
            sc = scal_c if isc else scal_x
            b = axb_of(c0)
            emit_norm_apply(S, cst, st, ht[:, :, 0:w], htb, w, sc["A" + which], sc["B" + which], sc["bufs"],
                            lambda k, c0=c0, c1=c1, b=b: (ax[:, k, c0:c1], b), smr, tmr)
        for (c0, c1, isc) in ncols_list:
            w = c1 - c0
            ht, htb = htr.next()
            S.dma("sp", ht[:, :, 0:w], d_h[:, :, c0:c1], writes=[htb])
            st = emit_norm_stats(S, cst, ht[:, :, 0:w], htb, w, sqr, psr)
            if pend is not None:
                apply(pend)
            pend = (c0, c1, isc, ht, htb, w, st)
        apply(pend)
        S.barrier()


def emit_decays(S, nc, es, d_dk, d_dm, d_lgc, d_cexp, need_q):
    tab = _sb(nc, es, "dectab", [128, 1536], F32)
    cfb = _sb(nc, es, "cfb", [128, 2, 3], F32)
    mc = _sb(nc, es, "mcomb", [128, 6, 128], F32) if need_q else None
    b = Buf()
    with ExitStack() as esi:
        dk, dkb = load_const(S, nc, esi, "dk_sb", d_dk, [128, 2, 1536])
        S.tt("dve", tab[:], dk[:, 0, :], dk[:, 1, :], ALU.mult, reads=[dkb], writes=[b])
        S.act(tab[:], tab[:], AF.Exp, reads=[b], writes=[b])
        ce, ceb = load_const(S, nc, esi, "ce_sb", d_cexp, [128, 2, 3])
        S.act(cfb[:], ce[:], AF.Exp, scale=128.0, reads=[ceb], writes=[b])
        if need_q:
            dm, dmb = load_const(S, nc, esi, "dm_sb", d_dm, [128, 4, 128])
            lgc, lgcb = load_const(S, nc, esi, "lgc_sb", d_lgc, [128, 2, 6])
            tmp = _sb(nc, esi, "mctmp", [128, 2, 128], F32)
            tb = Buf()
            for h in range(6):
                S.act(tmp[:, 0, :], dm[:, 0, :], AF.Exp, scale=lgc[:, 0, h:h + 1], reads=[dmb, lgcb], writes=[tb])
                S.act(tmp[:, 1, :], dm[:, 2, :], AF.Exp, scale=lgc[:, 1, h:h + 1], reads=[dmb, lgcb], writes=[tb])
                S.tt("dve", tmp[:, 0, :], tmp[:, 0, :], dm[:, 1, :], ALU.mult, reads=[tb, dmb], writes=[tb])
                S.tt("dve", tmp[:, 1, :], tmp[:, 1, :], dm[:, 3, :], ALU.mult, reads=[tb, dmb], writes=[tb])
                S.tt("dve", mc[:, h, :], tmp[:, 0, :], tmp[:, 1, :], ALU.add, reads=[tb], writes=[b])
        S.barrier()
    sl = lambda t, c: tab[:, (t * 3 + c) * 128:(t * 3 + c + 1) * 128]
    dec = dict(ktab_f=[sl(0, c) for c in range(3)], ktab_b=[sl(1, c) for c in range(3)],
               qtab_f=[sl(2, c) for c in range(3)], qtab_b=[sl(3, c) for c in range(3)], cfb=cfb, b=b)
    if need_q:
        dec["mcomb"] = mc
    return dec


def chain_rings(nc, es, deep=False):
    return dict(
        tps=Ring(nc, es, "c_tps", [128, 128], BF16, 2 if deep else 1, psum=True),
        dps=Ring(nc, es, "c_dps", [128, 128], F32, 4 if deep else 2, psum=True),
        kd=Ring(nc, es, "c_kd", [128, 128], BF16, 6),
        sf32=Ring(nc, es, "c_sf", [128, 128], F32, 3),
        sb32=Ring(nc, es, "c_sb", [128, 128], F32, 3),
        dbs=(_sb(nc, es, "c_dbs", [128, 16, 128], F32), [Buf() for _ in range(16)]),
        dfs=(_sb(nc, es, "c_dfs", [128, 16, 128], F32), [Buf() for _ in range(16)]),
    )


def build_r(stop=99):
    nc = bass.Bass("TRN2", target_bir_lowering=False)
    d_h = nc.dram_tensor("hT", [128, 8, NT], F32, kind="ExternalInput")
    d_m6 = nc.dram_tensor("mod6", [128, 2, 6, 8], F32, kind="ExternalInput")
    d_g = nc.dram_tensor("g12", [128, 2, 8], F32, kind="ExternalInput")
    d_wk = nc.dram_tensor("wk", [3, 128, 8, 128], F32, kind="ExternalInput")
    d_wv = nc.dram_tensor("wv", [128, 8, 384], F32, kind="ExternalInput")
    d_rope = nc.dram_tensor("rope", [128, 2 * NT + 128], BF16, kind="ExternalInput")
    d_dk = nc.dram_tensor("dk", [128, 2, 1536], F32, kind="ExternalInput")
    d_ce = nc.dram_tensor("cexp", [128, 2, 3], F32, kind="ExternalInput")
    d_cst = nc.dram_tensor("cst", [128, 385], F32, kind="ExternalInput")
    d_out = nc.dram_tensor("sl", [128, 4, 3, 128], F32, kind="ExternalOutput")
    with ExitStack() as es0:
        S = Sched(nc, es0)
        cst = emit_consts(S, nc, es0, d_cst)
        m6, m6b = load_const(S, nc, es0, "m6_sb", d_m6, [128, 2, 6, 8])
        g12, g12b = load_const(S, nc, es0, "g12_sb", d_g, [128, 2, 8])
        sx = emit_mod_scalars(S, nc, es0, (m6[:, 0], m6b), (g12[:, 0, :], g12b), (g12[:, 1, :], g12b), "sx")
        sc = emit_mod_scalars(S, nc, es0, (m6[:, 1], m6b), (g12[:, 0, :], g12b), (g12[:, 1, :], g12b), "sc")
        rope, ropeb = load_const(S, nc, es0, "rope_sb", d_rope, [128, 2 * NT + 128], BF16)
        crot = (rope[:, 0:NT], rope[:, NT:2 * NT], rope[:, 2 * NT:2 * NT + 128], ropeb)
        dec = emit_decays(S, nc, es0, d_dk, None, None, d_ce, False)
        rk = _sb(nc, es0, "rk", [128, 3, NT], BF16)
        rkb = Buf()
        vret = _sb(nc, es0, "vret", [128, 18, 384], BF16)
        vretb = Buf()
        with ExitStack() as es1:
            ax = _sb(nc, es1, "ax", [128, 8, NT], BF16)
            axb = Buf()
            tiles = [(a, b, False) for (a, b) in _split(0, NL, 256)] + [(NL, NT, True)]
            emit_norm_phase(S, nc, cst, d_h, tiles, ax, lambda c0: axb, sx, sc, "1")
            if stop <= 1:
                S.emit()
                return nc
            with ExitStack() as es:
                wr = Ring(nc, es, "wfm", [128, 8, 128], BF16, 2)
                accr = Ring(nc, es, "acc", [128, 512], F32, 3, psum=True)
                rr = dict(raw=Ring(nc, es, "r_raw", [128, 512], BF16, 2), t1=Ring(nc, es, "r_t1", [128, 512], F32, 2),
                          t2=Ring(nc, es, "r_t2", [128, 512], F32, 2), pp=Ring(nc, es, "r_pp", [128, 512], F32, 2, psum=True))
                for c in range(3):
                    w_, wb = wr.next()
                    S.dma("pool", w_[:], d_wk[c], writes=[wb])
                    for (c0, c1) in _split(0, NL, 512) + [(NL, NT)]:
                        w = c1 - c0
                        ps, psb = accr.next()
                        for k in range(8):
                            S.mm(ps[:, 0:w], w_[:, k, :], ax[:, k, c0:c1], start=(k == 0), stop=(k == 7),
                                 reads=[wb, axb], writes=[psb])
                        emit_rotary(S, crot, ps, psb, w, c0, 0.125, rk[:, c, c0:c1], rkb, rr)
                if stop <= 2:
                    S.emit()
                    return nc
                wv, wvb = load_const(S, nc, es, "wv_sb", d_wv, [128, 8, 384], BF16, q="pool")
                tmr = Ring(nc, es, "tm", [128, 512], F32, 2, psum=True)
                for blk in range(18):
                    c0 = blk * 128
                    ps, psb = tmr.next()
                    for k in range(8):
                        S.mm(ps[:, 0:384], ax[:, k, c0:c0 + 128], wv[:, k, :], start=(k == 0), stop=(k == 7),
                             reads=[wvb, axb], writes=[psb])
                    S.cp("act" if blk % 2 else "dve", vret[:, blk, :], ps[:, 0:384], reads=[psb], writes=[vretb])
                S.barrier()
        if stop <= 3:
            S.emit()
            return nc
        with ExitStack() as es:
            rings = chain_rings(nc, es, deep=True)
            Sfb = _sb(nc, es, "Sfb", [128, 17, 128], BF16)
            Sbb = _sb(nc, es, "Sbb", [128, 17, 128], BF16)
            sbuf_b = Buf()
            S.memset("pool", Sfb[:], 0.0, writes=[sbuf_b])
            S.memset("pool", Sbb[:], 0.0, writes=[sbuf_b])
            outt = _sb(nc, es, "outt", [128, 4, 3, 128], F32)
            outb = Buf()
            S.memset("dve", outt[:], 0.0, writes=[outb])
            for c in range(3):
                for (si, blocks) in ((0, list(range(16))), (1, [16, 17])):
                    kcols = [b * 128 for b in blocks]
                    (sf, sfb_), (sb_, sbb_) = emit_ret_chain(
                        S, nc, rk, rkb, vret, vretb, 0, c, blocks, kcols, blocks, None, None, dec, cst,
                        Sfb, Sbb, sbuf_b, sbuf_b, rings, lambda i: i)
                    for (p0, p1) in ((0, 64), (64, 128)):
                        S.cp("dve", outt[p0:p1, 2 * si, c, p0:p1], sf[p0:p1, p0:p1], reads=[sfb_], writes=[outb])
                        S.cp("dve", outt[p0:p1, 2 * si + 1, c, p0:p1], sb_[p0:p1, p0:p1], reads=[sbb_], writes=[outb])
            S.dma("sp", d_out[:], outt[:], reads=[outb])
        S.emit()
    return nc


def rope_tables(core):
    inv = (10000.0 ** (-np.arange(16, dtype=np.float32) / 16)).astype(np.float32)
    t = np.arange(NL) + core * NL
    row = (t // GW).astype(np.float32)
    col = (t % GW).astype(np.float32)
    p = np.arange(128)
    f = p % 64
    part = f // 32
    j = f % 16
    u = (f % 32) // 16
    pos = np.where(part[:, None] == 0, row[None, :], col[None, :]).astype(np.float32)
    ang = (pos * inv[j][:, None]).astype(np.float32)
    C = np.ones((128, NT), np.float32)
    Sg = np.zeros((128, NT), np.float32)
    C[:, :NL] = np.cos(ang)
    Sg[:, :NL] = np.where(u[:, None] == 0, -np.sin(ang), np.sin(ang))
    P = np.zeros((128, 128), np.float32)
    partner = np.where((f % 32) < 16, p + 16, p - 16)
    P[partner, p] = 1.0
    return np.concatenate([C, Sg, P], 1).astype(NPBF)


def decay_inputs(lg_f, lg_b):
    dk = np.zeros((128, 2, 4, 3, 128), np.float32)
    j = np.arange(128, dtype=np.float32)
    for c in range(3):
        for col in range(128):
            h = 2 * c + col // 64
            dk[:, 0, 0, c, col] = 127 - j
            dk[:, 1, 0, c, col] = lg_f[h]
            dk[:, 0, 1, c, col] = j
            dk[:, 1, 1, c, col] = lg_b[h]
        for p in range(128):
            h = 2 * c + p // 64
            dk[p, 0, 2, c, :] = j + 1
            dk[p, 1, 2, c, :] = lg_f[h]
            dk[p, 0, 3, c, :] = 128 - j
            dk[p, 1, 3, c, :] = lg_b[h]
    cexp = np.zeros((128, 2, 3), np.float32)
    for c in range(3):
        for half in range(2):
            cexp[half * 64:(half + 1) * 64, 0, c] = lg_f[2 * c + half]
            cexp[half * 64:(half + 1) * 64, 1, c] = lg_b[2 * c + half]
    lgc = np.zeros((128, 2, 6), np.float32)
    lgc[:, 0, :] = lg_f[None, :]
    lgc[:, 1, :] = lg_b[None, :]
    return dk.reshape(128, 2, 1536), cexp, lgc


def mod6_layout(mv_l):
    return np.ascontiguousarray(mv_l.reshape(2, 6, 8, 128).transpose(3, 0, 1, 2))


def prep_r_weights(w_in_l):
    wk = w_in_l[:, 1792:2176]
    wk = np.ascontiguousarray(wk.reshape(8, 128, 3, 128).transpose(2, 1, 0, 3))
    wv = np.ascontiguousarray(w_in_l[:, 2176:2560].reshape(8, 128, 384).transpose(1, 0, 2))
    return wk, wv


def build_m1(stop=99):
    nc = bass.Bass("TRN2", target_bir_lowering=False)
    d_h = nc.dram_tensor("hT", [128, 8, NT1], F32, kind="ExternalInput")
    d_m6 = nc.dram_tensor("mod6", [128, 2, 6, 8], F32, kind="ExternalInput")
    d_g = nc.dram_tensor("g12", [128, 2, 8], F32, kind="ExternalInput")
    d_wf = nc.dram_tensor("wfm", [17, 128, 8, 128], F32, kind="ExternalInput")
    d_wt = nc.dram_tensor("wtm", [128, 8, 768], F32, kind="ExternalInput")
    d_rope = nc.dram_tensor("rope", [128, 2 * NT + 128], BF16, kind="ExternalInput")
    d_gt = nc.dram_tensor("gtab", [128, 6, 640], BF16, kind="ExternalInput")
    d_st = nc.dram_tensor("stab", [4, 128, 6, 768], BF16, kind="ExternalInput")
    d_wbd = nc.dram_tensor("wbd", [128, 2, 128], F32, kind="ExternalInput")
    d_pv = nc.dram_tensor("pvec", [128, 2, 3], F32, kind="ExternalInput")
    d_ic = nc.dram_tensor("invcnt", [128, 2, 4, 8], F32, kind="ExternalInput")
    d_dk = nc.dram_tensor("dk", [128, 2, 1536], F32, kind="ExternalInput")
    d_dm = nc.dram_tensor("dm", [128, 4, 128], F32, kind="ExternalInput")
    d_lgc = nc.dram_tensor("lgc", [128, 2, 6], F32, kind="ExternalInput")
    d_ce = nc.dram_tensor("cexp", [128, 2, 3], F32, kind="ExternalInput")
    d_cf = nc.dram_tensor("coef", [128, 3, 54], F32, kind="ExternalInput")
    d_sa = nc.dram_tensor("sall", [128, 2, 3, 9, 128], F32, kind="ExternalInput")
    d_gn = nc.dram_tensor("gng", [128, 3], F32, kind="ExternalInput")
    d_wo = nc.dram_tensor("wout", [128, 8, 1024], F32, kind="ExternalInput")
    d_cst = nc.dram_tensor("cst", [128, 385], F32, kind="ExternalInput")
    d_hm = nc.dram_tensor("hmid", [128, 8, NT], F32, kind="ExternalOutput")
    d_h2 = nc.dram_tensor("h2", [128, 8, NT], BF16, kind="ExternalOutput")
    d_yx = nc.dram_tensor("yxs", [128, 8, NT], BF16)
    d_pp = nc.dram_tensor("pps", [128, 2, NPP], F32)
    dbg = {}
    with ExitStack() as es0:
        S = Sched(nc, es0)
        cst = emit_consts(S, nc, es0, d_cst)
        m6, m6b = load_const(S, nc, es0, "m6_sb", d_m6, [128, 2, 6, 8])
        g12, g12b = load_const(S, nc, es0, "g12_sb", d_g, [128, 2, 8])
        sx = emit_mod_scalars(S, nc, es0, (m6[:, 0], m6b), (g12[:, 0, :], g12b), (g12[:, 1, :], g12b), "sx")
        sc = emit_mod_scalars(S, nc, es0, (m6[:, 1], m6b), (g12[:, 0, :], g12b), (g12[:, 1, :], g12b), "sc")
        yxb = Buf()
        with ExitStack() as esP:
            naq = _sb(nc, esP, "naq", [128, 3, NT], BF16)
            nak = _sb(nc, esP, "nak", [128, 3, NT1], BF16)
            vna = _sb(nc, esP, "vna", [128, 22, 6, 65], BF16)
            rq = _sb(nc, esP, "rq", [128, 3, NT], BF16)
            rk = _sb(nc, esP, "rk", [128, 3, NT], BF16)
            gs = _sb(nc, esP, "gs", [128, 3, NT], BF16)
            vret = _sb(nc, esP, "vret", [128, 18, 384], BF16)
            naqb, nakb, vnab, rqb, rkb, gsb, vretb, ppb = (Buf() for _ in range(8))
            S.memset("pool", vna[:], 1.0, writes=[vnab])
            with ExitStack() as esA:
                ax = _sb(nc, esA, "ax", [128, 8, NT1], BF16)
                axb = Buf()
                tiles = [(a, b, False) for (a, b) in _split(0, NS, 256)] + [(NS, NT1, True)]
                emit_norm_phase(S, nc, cst, d_h, tiles, ax, lambda c0: axb, sx, sc, "1")
                with ExitStack() as es:
                    rope, ropeb = load_const(S, nc, es, "rope_sb", d_rope, [128, 2 * NT + 128], BF16)
                    crot = (rope[:, 0:NT], rope[:, NT:2 * NT], rope[:, 2 * NT:2 * NT + 128], ropeb)
                    wr = Ring(nc, es, "wfm_sb", [128, 8, 128], BF16, 4)
                    accr = Ring(nc, es, "acc", [128, 512], F32, 3, psum=True)
                    rr = dict(raw=Ring(nc, es, "r_raw", [128, 512], BF16, 2), t1=Ring(nc, es, "r_t1", [128, 512], F32, 2),
                              t2=Ring(nc, es, "r_t2", [128, 512], F32, 2),
                              pp=Ring(nc, es, "r_pp", [128, 512], F32, 2, psum=True))
                    ptr = Ring(nc, es, "ptmp", [128, 512], F32, 2)
                    zt = _sb(nc, es, "zt", [128, 2, 8], F32)
                    ztb = Buf()
                    S.memset("dve", zt[:], 0.0, writes=[ztb])
                    S.dma("sp", d_pp[:, :, 2064:2072], zt[:], reads=[ztb], writes=[ppb])
                    S.dma("sp", d_pp[:, :, 2328:2336], zt[:], reads=[ztb], writes=[ppb])
                    for j in range(17):
                        w_, wb = wr.next()
                        S.dma("pool", w_[:], d_wf[j], writes=[wb])
                        kind = ("pool", "pool", "q", "q", "q", "k", "k", "k", "rq", "rq", "rq", "rk", "rk", "rk",
                                "g", "g", "g")[j]
                        tl = FM_TILES_POOL if kind == "pool" else (FM_TILES_ALL if kind == "k" else FM_TILES_LOC)
                        for (c0, c1) in tl:
                            w = c1 - c0
                            ps, psb = accr.next()
                            for k in range(8):
                                S.mm(ps[:, 0:w], w_[:, k, :], ax[:, k, c0:c1], start=(k == 0), stop=(k == 7),
                                     reads=[wb, axb], writes=[psb])
                            lc = loc_col(c0)
                            if kind == "pool":
                                t, tb = ptr.next()
                                S.cp("act", t[:, 0:w], ps[:, 0:w], reads=[psb], writes=[tb])
                                dc = (c0 - 248) if c0 < NS else 2072 + (c0 - NS)
                                S.dma("sp", d_pp[:, j, dc:dc + w], t[:, 0:w], reads=[tb], writes=[ppb])
                            elif kind == "q":
                                S.act(naq[:, j - 2, lc:lc + w], ps[:, 0:w], AF.Copy, scale=0.125, reads=[psb], writes=[naqb])
                            elif kind == "k":
                                S.cp("dve", nak[:, j - 5, c0:c1], ps[:, 0:w], reads=[psb], writes=[nakb])
                            elif kind == "rq":
                                emit_rotary(S, crot, ps, psb, w, lc, 1.0, rq[:, j - 8, lc:lc + w], rqb, rr)
                            elif kind == "rk":
                                emit_rotary(S, crot, ps, psb, w, lc, 0.125, rk[:, j - 11, lc:lc + w], rkb, rr)
                            else:
                                S.act(gs[:, j - 14, lc:lc + w], ps[:, 0:w], AF.Silu, reads=[psb], writes=[gsb])
                    wv, wvb = load_const(S, nc, es, "wv_sb", d_wt, [128, 8, 768], BF16, q="pool")
                    tmr = Ring(nc, es, "tm", [128, 1024], F32, 1, psum=True)
                    for blk in range(22):
                        c0 = blk * 128 if blk < 20 else NS + (blk - 20) * 128
                        ri = blk - 2 if 2 <= blk < 18 else (16 + blk - 20 if blk >= 20 else None)
                        ps, psb = tmr.next()
                        for k in range(8):
                            S.mm(ps[:, 0:384], ax[:, k, c0:c0 + 128], wv[:, k, 0:384], start=(k == 0), stop=(k == 7),
                                 reads=[wvb, axb], writes=[psb])
                        if ri is not None:
                            for k in range(8):
                                S.mm(ps[:, 512:896], ax[:, k, c0:c0 + 128], wv[:, k, 384:768], start=(k == 0),
                                     stop=(k == 7), reads=[wvb, axb], writes=[psb])
                        for h in range(6):
                            S.cp("dve" if h % 2 else "act", vna[:, blk, h, 0:64], ps[:, h * 64:(h + 1) * 64],
                                 reads=[psb], writes=[vnab])
                        if ri is not None:
                            S.cp("act", vret[:, ri, :], ps[:, 512:896], reads=[psb], writes=[vretb])
                    S.barrier()
            if stop <= 1:
                for (nm, t) in (("naq", naq), ("nak", nak), ("rq", rq), ("rk", rk), ("gs", gs)):
                    dd = nc.dram_tensor("dbg_" + nm, list(t.shape), BF16, kind="ExternalOutput")
                    S.dma("sp", dd[:], t[:])
                dd = nc.dram_tensor("dbg_vna", [128, 22, 6, 65], BF16, kind="ExternalOutput")
                S.dma("sp", dd[:], vna[:])
                dd = nc.dram_tensor("dbg_vret", [128, 18, 384], BF16, kind="ExternalOutput")
                S.dma("sp", dd[:], vret[:])
                dd = nc.dram_tensor("dbg_pp", [128, 2, NPP], F32, kind="ExternalOutput")
                S.dma("sp", dd[:], d_pp[:])
                S.emit()
                return nc
            with ExitStack() as es:
                gt, gtb = load_const(S, nc, es, "gt_sb", d_gt, [128, 6, 640], BF16)
                str_ = Ring(nc, es, "st_sb", [128, 6, 768], BF16, 1)
                psS = Ring(nc, es, "psS", [128, 1024], F32, 3, psum=True)
                psO = Ring(nc, es, "psO", [128, 6, 65], F32, 1, psum=True)
                psT = Ring(nc, es, "psT", [128, 3, 128], BF16, 1, psum=True)
                pr = Ring(nc, es, "Pt", [128, 1024], BF16, 4)
                otr = Ring(nc, es, "otok", [128, 384], BF16, 2)
                recr = Ring(nc, es, "rec", [128, 6], F32, 2)
                yr = Ring(nc, es, "yxn", [128, 3, 128], BF16, 2)
                spec = {p: (i, b0) for i, (p, b0) in enumerate(SPECIAL)}

                def na_pv(st_):
                    (po, pob, h, pt, ptb, kb) = st_
                    nb = len(kb)
                    for j, (kc0, vb) in enumerate(kb):
                        S.mm(po[:, h, :], pt[:, j * 128:(j + 1) * 128], vna[:, vb, h, :], start=(j == 0),
                             stop=(j == nb - 1), reads=[ptb, vnab], writes=[pob])

                def na_epilogue(po, pob, q0):
                    rec, recb = recr.next()
                    S.recip(rec[:], po[:, :, 64], reads=[pob], writes=[recb])
                    ot, otb = otr.next()
                    for h in range(6):
                        S.ts("dve", ot[:, h * 64:(h + 1) * 64], po[:, h, 0:64], rec[:, h:h + 1], ALU.mult,
                             reads=[pob, recb], writes=[otb])
                    pT, pTb = psT.next()
                    for j in range(3):
                        S.tr(pT[:, j, :], ot[:, j * 128:(j + 1) * 128], cst["identb"][:],
                             reads=[otb, cst["identb_b"]], writes=[pTb])
                    yt, ytb = yr.next()
                    S.cp("act", yt[:], pT[:], reads=[pTb], writes=[ytb])
                    S.dma("sp", d_yx[:, 2:5, q0:q0 + 128], yt[:], reads=[ytb], writes=[yxb])

                pend = None
                for pi in range(18):
                    tab = None
                    if pi < 16:
                        q0 = 128 * pi
                        if pi in spec:
                            si, b0 = spec[pi]
                            st, stb = str_.next()
                            S.dma("sp", st[:], d_st[si], writes=[stb])
                            tab = (st, stb)
                            kb = [(128 * (b0 + s), b0 + s) for s in range(6)]
                        else:
                            tab = (gt, gtb)
                            kb = [(128 * (pi + s), pi + s) for s in range(5)]
                    else:
                        q0 = NL + 128 * (pi - 16)
                        kb = []
                    nbl = len(kb)
                    kb = kb + [(NS, 20), (NS + 128, 21)]
                    nb = len(kb)
                    po, pob = psO.next()
                    for h in range(6):
                        c = h // 2
                        p0 = 64 * (h % 2)
                        sp_, spb = psS.next()
                        for j, (kc0, vb) in enumerate(kb):
                            S.mm(sp_[:, j * 128:(j + 1) * 128], nak[p0:p0 + 64, c, kc0:kc0 + 128],
                                 naq[p0:p0 + 64, c, q0:q0 + 128], reads=[nakb, naqb], writes=[spb])
                        if nbl:
                            S.tt("dve", sp_[:, 0:nbl * 128], sp_[:, 0:nbl * 128], tab[0][:, h, 0:nbl * 128], ALU.add,
                                 reads=[spb, tab[1]], writes=[spb])
                        pt, ptb = pr.next()
                        n1 = min(nb, 4) * 128
                        S.act(pt[:, 0:n1], sp_[:, 0:n1], AF.Exp, reads=[spb], writes=[ptb])
                        if nb > 4:
                            S.act(pt[:, 512:nb * 128], sp_[:, 512:nb * 128], AF.Exp, reads=[spb], writes=[ptb])
                        if pend is not None:
                            na_pv(pend[0])
                            if pend[1] is not None:
                                na_epilogue(*pend[1])
                        pend = ((po, pob, h, pt, ptb, kb), (po, pob, q0) if h == 5 else None)
                na_pv(pend[0])
                na_epilogue(*pend[1])
                S.barrier()
            if stop <= 2:
                dd = nc.dram_tensor("dbg_yx", [128, 8, NT], BF16, kind="ExternalOutput")
                S.dma("sp", dd[:, 2:5, :], d_yx[:, 2:5, :])
                S.emit()
                return nc
            with ExitStack() as es:
                dec = emit_decays(S, nc, es, d_dk, d_dm, d_lgc, d_ce, True)
                decb = dec["b"]
                mcomb = dec["mcomb"]
                gng, gngb = load_const(S, nc, es, "gng_sb", d_gn, [128, 3])
                cf, cfb_ = load_const(S, nc, es, "coef_sb", d_cf, [128, 3, 54])
                coef = _sb(nc, es, "coef_e", [128, 54], F32)
                coefb = Buf()
                S.tt("dve", coef[:], cf[:, 0, :], cf[:, 1, :], ALU.mult, reads=[cfb_], writes=[coefb])
                S.act(coef[:], coef[:], AF.Exp, reads=[coefb], writes=[coefb])
                S.tt("dve", coef[:], coef[:], cf[:, 2, :], ALU.mult, reads=[coefb, cfb_], writes=[coefb])
                rings = chain_rings(nc, es)
                sar = Ring(nc, es, "sall_sb", [128, 9, 128], F32, 2)
                sinr = Ring(nc, es, "sin", [128, 128], F32, 4)
                Sfbs = [_sb(nc, es, f"Sfb{c}", [128, 20, 128], BF16) for c in range(3)]
                Sbbs = [_sb(nc, es, f"Sbb{c}", [128, 20, 128], BF16) for c in range(3)]
                stbs = [Buf() for _ in range(3)]
                psA = Ring(nc, es, "psA", [128, 128], F32, 2, psum=True)
                psY = Ring(nc, es, "psY", [128, 128], F32, 1, psum=True)
                psG = Ring(nc, es, "psG", [128, 512], F32, 2, psum=True)
                atr = Ring(nc, es, "at", [128, 128], BF16, 6)
                qdr = Ring(nc, es, "qd", [128, 128], BF16, 6)
                ytr = Ring(nc, es, "ytile", [128, 512], F32, 3)
                gnt = Ring(nc, es, "gnt", [128, 512], F32, 8)
                gno = Ring(nc, es, "gno", [128, 512], BF16, 2)
                segs = ((list(range(16)), 0), ([16, 17], 17))
                for c in range(3):
                    Sfb, Sbb, stb_ = Sfbs[c], Sbbs[c], stbs[c]
                    S.memset("pool", Sfb[:], 0.0, writes=[stb_])
                    S.memset("pool", Sbb[:], 0.0, writes=[stb_])
                    sins = []
                    for d in range(2):
                        sa, sab = sar.next()
                        S.dma("sp", sa[:], d_sa[:, d, c], writes=[sab])
                        si_, sib = sinr.next()
                        for src in range(9):
                            col = (d * 3 + c) * 9 + src
                            if src == 0:
                                S.ts("dve", si_[:], sa[:, src, :], coef[:, col:col + 1], ALU.mult,
                                     reads=[sab, coefb], writes=[sib])
                            else:
                                S.stt("dve", si_[:], sa[:, src, :], coef[:, col:col + 1], si_[:], ALU.mult, ALU.add,
                                      reads=[sab, coefb, sib], writes=[sib])
                        sins.append((si_, sib))
                    for (blocks, s0) in segs:
                        kcols = [b * 128 for b in blocks]
                        fi, bi = (sins[0], sins[1]) if s0 == 0 else (None, None)
                        emit_ret_chain(S, nc, rk, rkb, vret, vretb, 0, c, blocks, kcols, blocks,
                                       None if fi is None else (fi[0][:], fi[1]), None if bi is None else (bi[0][:], bi[1]),
                                       dec, cst, Sfb, Sbb, stb_, stb_, rings, lambda i, s0=s0: s0 + i)
                for (blocks, s0) in segs:
                    nbk = len(blocks)
                    for t0 in range(0, nbk, 4):
                        tb_ = blocks[t0:t0 + 4]
                        wt = 128 * len(tb_)
                        yts = [ytr.next() for _ in range(3)]
                        for ii, blk in enumerate(tb_):
                            n = t0 + ii
                            k0 = blk * 128
                            for c in range(3):
                                Sfb, Sbb, stb_ = Sfbs[c], Sbbs[c], stbs[c]
                                yt, ytb = yts[c]
                                ats = []
                                for hh in range(2):
                                    p0 = 64 * hh
                                    pa_, pab = psA.next()
                                    S.mm(pa_[:], rk[p0:p0 + 64, c, k0:k0 + 128], rq[p0:p0 + 64, c, k0:k0 + 128],
                                         reads=[rkb, rqb], writes=[pab])
                                    at, atb = atr.next()
                                    S.tt("dve", at[:], pa_[:], mcomb[:, 2 * c + hh, :], ALU.mult, reads=[pab, decb],
                                         writes=[atb])
                                    ats.append((at, atb))
                                qf, qfb = qdr.next()
                                S.tt("pool", qf[:], rq[:, c, k0:k0 + 128], dec["qtab_f"][c], ALU.mult, reads=[rqb, decb],
                                     writes=[qfb])
                                qb_, qbb = qdr.next()
                                S.tt("pool", qb_[:], rq[:, c, k0:k0 + 128], dec["qtab_b"][c], ALU.mult, reads=[rqb, decb],
                                     writes=[qbb])
                                py, pyb = psY.next()
                                S.mm(py[:], Sfb[:, s0 + n, :], qf[:], start=True, stop=False, reads=[stb_, qfb], writes=[pyb])
                                S.mm(py[:], Sbb[:, s0 + n + 1, :], qb_[:], start=False, stop=False, reads=[stb_, qbb],
                                     writes=[pyb])
                                S.mm(py[0:64, :], vret[:, blk, (2 * c) * 64:(2 * c + 1) * 64], ats[0][0][:], start=False,
                                     stop=True, reads=[vretb, ats[0][1]], writes=[pyb])
                                S.mm(py[64:128, :], vret[:, blk, (2 * c + 1) * 64:(2 * c + 2) * 64], ats[1][0][:],
                                     start=False, stop=True, reads=[vretb, ats[1][1]], writes=[pyb], tp=(0, 64))
                                S.cp("act", yt[:, ii * 128:(ii + 1) * 128], py[:], reads=[pyb], writes=[ytb])
                        col0 = blocks[t0] * 128
                        for c in range(3):
                            yt, ytb = yts[c]
                            gq = [gnt.next() for _ in range(5)]
                            (ysq, ysqb), (mean, meanb), (m2, m2b), (var, varb), (yc, ycb) = gq
                            S.act(ysq[:, 0:wt], yt[:, 0:wt], AF.Square, reads=[ytb], writes=[ysqb])
                            p1, p1b = psG.next()
                            S.mm(p1[:, 0:wt], cst["bd64"][:], yt[:, 0:wt], reads=[ytb, cst["b"]], writes=[p1b])
                            S.cp("act", mean[:, 0:wt], p1[:, 0:wt], reads=[p1b], writes=[meanb])
                            p2, p2b = psG.next()
                            S.mm(p2[:, 0:wt], cst["bd64"][:], ysq[:, 0:wt], reads=[ysqb, cst["b"]], writes=[p2b])
                            S.tt("pool", m2[:, 0:wt], mean[:, 0:wt], mean[:, 0:wt], ALU.mult, reads=[meanb], writes=[m2b])
                            S.tt("dve", var[:, 0:wt], p2[:, 0:wt], m2[:, 0:wt], ALU.subtract, reads=[p2b, m2b], writes=[varb])
                            S.act(var[:, 0:wt], var[:, 0:wt], AF.Sqrt, bias=cst["eps"][:], scale=1.0, reads=[varb, cst["b"]],
                                  writes=[varb])
                            S.recip(var[:, 0:wt], var[:, 0:wt], reads=[varb], writes=[varb])
                            S.tt("pool", yc[:, 0:wt], yt[:, 0:wt], mean[:, 0:wt], ALU.subtract, reads=[ytb, meanb], writes=[ycb])
                            S.tt("dve", yc[:, 0:wt], yc[:, 0:wt], var[:, 0:wt], ALU.mult, reads=[ycb, varb], writes=[ycb])
                            go, gob = gno.next()
                            S.ts("dve", yc[:, 0:wt], yc[:, 0:wt], gng[:, c:c + 1], ALU.mult, reads=[ycb, gngb], writes=[ycb])
                            S.tt("pool", go[:, 0:wt], yc[:, 0:wt], gs[:, c, col0:col0 + wt], ALU.mult,
                                 reads=[ycb, gsb], writes=[gob])
                            S.dma("sp", d_yx[:, 5 + c, col0:col0 + wt], go[:, 0:wt], reads=[gob], writes=[yxb])
                S.barrier()
            with ExitStack() as es:
                pv, pvb = load_const(S, nc, es, "pv_sb", d_pv, [128, 2, 3])
                ic, icb = load_const(S, nc, es, "ic_sb", d_ic, [128, 2, 4, 8])
                wbd, wbdb = load_const(S, nc, es, "wbd_sb", d_wbd, [128, 2, 128], BF16, q="pool")
                X = _sb(nc, es, "pX", [128, NPP], F32)
                A_ = _sb(nc, es, "pA", [128, NPP], F32)
                B_ = _sb(nc, es, "pB", [128, NPP], F32)
                md = _sb(nc, es, "pmd", [128, NT], BF16)
                t8 = _sb(nc, es, "pt8", [128, 8], F32)
                Xb, Ab, Bb, mdb, t8b = (Buf() for _ in range(5))
                psP = Ring(nc, es, "psP", [128, 512], F32, 2, psum=True)
                ypr = Ring(nc, es, "ypool", [128, 512], BF16, 2)
                N = NPP
                for g2 in range(2):
                    S.dma("sp", X[:], d_pp[:, g2, :], reads=[ppb], writes=[Xb])
                    S.ts("dve", X[:, 0:8], X[:, 0:8], pv[:, 0, 2:3], ALU.mult, reads=[Xb, pvb], writes=[Xb])
                    S.ts("dve", X[:, 2056:2064], X[:, 2056:2064], pv[:, 1, 2:3], ALU.mult, reads=[Xb, pvb], writes=[Xb])
                    S.tt("dve", A_[:, 1:N], X[:, 0:N - 1], X[:, 1:N], ALU.add, reads=[Xb], writes=[Ab])
                    S.tt("pool", B_[:, 2:N - 1], A_[:, 1:N - 2], A_[:, 3:N], ALU.add, reads=[Ab], writes=[Bb])
                    if g2 == 1:
                        S.tt("dve", A_[:, 4:N - 3], B_[:, 2:N - 5], B_[:, 6:N - 1], ALU.add, reads=[Bb], writes=[Ab])
                        S.tt("pool", B_[:, 8:N - 7], A_[:, 4:N - 11], A_[:, 12:N - 3], ALU.add, reads=[Ab], writes=[Bb])
                    for half, src, srcb in ((0, A_, Ab), (1, B_, Bb)):
                        p0, p1 = 64 * half, 64 * half + 64
                        for (xc, mc_, wd_) in ((8, 0, NL), (2072, NL, NCTX)):
                            S.stt("dve", md[p0:p1, mc_:mc_ + wd_], src[p0:p1, xc:xc + wd_], pv[p0:p1, g2, 1:2],
                                  X[p0:p1, xc:xc + wd_], ALU.mult, ALU.subtract, reads=[srcb, Xb, pvb], writes=[mdb])
                        for r, (xc, mc_) in enumerate(((8, 0), (2048, 2040), (2072, 2048), (2320, 2296))):
                            S.tt("dve", t8[p0:p1, :], src[p0:p1, xc:xc + 8], ic[p0:p1, g2, r, :], ALU.mult,
                                 reads=[srcb, icb], writes=[t8b])
                            S.tt("dve", md[p0:p1, mc_:mc_ + 8], t8[p0:p1, :], X[p0:p1, xc:xc + 8], ALU.subtract,
                                 reads=[t8b, Xb, mdb], writes=[mdb])
                    for (c0, c1) in _split(0, NL, 512) + [(NL, NT)]:
                        w = c1 - c0
                        ps, psb = psP.next()
                        S.mm(ps[:, 0:w], wbd[:, g2, :], md[:, c0:c1], reads=[wbdb, mdb], writes=[psb])
                        yp, ypb = ypr.next()
                        S.act(yp[:, 0:w], ps[:, 0:w], AF.Copy, scale=pv[:, g2, 0:1], reads=[psb, pvb], writes=[ypb])
                        S.dma("sp", d_yx[:, g2, c0:c1], yp[:, 0:w], reads=[ypb], writes=[yxb])
                S.barrier()
        if stop <= 3:
            dd = nc.dram_tensor("dbg_yx", [128, 8, NT], BF16, kind="ExternalOutput")
            S.dma("sp", dd[:], d_yx[:])
            S.emit()
            return nc
        with ExitStack() as es:
            wo, wob = load_const(S, nc, es, "wo_sb", d_wo, [128, 8, 1024], BF16, q="pool")
            yxr = Ring(nc, es, "yxt", [128, 8, 256], BF16, 2)
            htr = Ring(nc, es, "o_ht", [128, 8, 256], F32, 2)
            hmr = Ring(nc, es, "o_hm", [128, 8, 256], F32, 2)
            h2r = Ring(nc, es, "o_h2", [128, 8, 256], BF16, 2)
            accr = Ring(nc, es, "o_acc", [128, 256], F32, 3, psum=True)
            sqr = Ring(nc, es, "o_sq", [128, 8, 256], F32, 1)
            psr = Ring(nc, es, "o_nps", [128, 512], F32, 2, psum=True)
            smr = Ring(nc, es, "o_sd", [128, 256], F32, 2)
            tmr = Ring(nc, es, "o_tm", [128, 256], F32, 3)
            for (c0, c1) in _split(0, NL, 256) + [(NL, NT)]:
                w = c1 - c0
                isc = c0 >= NL
                sc_ = sc if isc else sx
                s0 = NS + (c0 - NL) if isc else HB + c0
                yt, ytb = yxr.next()
                S.dma("sp", yt[:, :, 0:w], d_yx[:, :, c0:c1], reads=[yxb], writes=[ytb])
                ht, htb = htr.next()
                S.dma("sp", ht[:, :, 0:w], d_h[:, :, s0:s0 + w], writes=[htb])
                hm, hmb = hmr.next()
                for m in range(8):
                    ps, psb = accr.next()
                    for k in range(8):
                        S.mm(ps[:, 0:w], wo[:, k, m * 128:(m + 1) * 128], yt[:, k, 0:w], start=(k == 0), stop=(k == 7),
                             reads=[wob, ytb], writes=[psb])
                    S.stt("dve", hm[:, m, 0:w], ps[:, 0:w], sc_["G1"][:, m:m + 1], ht[:, m, 0:w], ALU.mult, ALU.add,
                          reads=[psb, htb] + sc_["bufs"], writes=[hmb])
                S.dma("sp", d_hm[:, :, c0:c1], hm[:, :, 0:w], reads=[hmb])
                h2, h2b = h2r.next()
                emit_norm_tile(S, cst, hm[:, :, 0:w], hmb, w, sc_["A2"], sc_["B2"], sc_["bufs"],
                               lambda k, h2=h2, h2b=h2b, w=w: (h2[:, k, 0:w], h2b), sqr, psr, smr, tmr)
                S.dma("sp", d_h2[:, :, c0:c1], h2[:, :, 0:w], reads=[h2b])
        S.emit()
    return nc


FM_COLS = list(range(0, 1024)) + list(range(1408, 2176)) + list(range(2560, 2944))
TM_COLS = list(range(1024, 1408)) + list(range(2176, 2560))


def prep_m1_weights(w_in_l, w_out_l, pool_w_l):
    wf = np.ascontiguousarray(w_in_l[:, FM_COLS].reshape(8, 128, 17, 128).transpose(2, 1, 0, 3))
    wt = np.ascontiguousarray(w_in_l[:, TM_COLS].reshape(8, 128, 768).transpose(1, 0, 2))
    wo = np.ascontiguousarray(w_out_l.reshape(8, 128, 1024).transpose(1, 0, 2))
    wbd = np.zeros((128, 2, 128), np.float32)
    for g2 in range(2):
        for gh in range(2):
            wbd[gh * 64:(gh + 1) * 64, g2, gh * 64:(gh + 1) * 64] = pool_w_l[2 * g2 + gh]
    return wf, wt, wo, wbd


def pool_consts(core, pool_scale_l):
    pv = np.zeros((128, 2, 3), np.float32)
    ic = np.ones((128, 2, 4, 8), np.float32)
    for g2 in range(2):
        pv[:, g2, 0] = pool_scale_l[g2 * 128:(g2 + 1) * 128]
        for gh in range(2):
            w = POOL_WINDOWS[2 * g2 + gh]
            sl = slice(gh * 64, (gh + 1) * 64)
            pv[sl, g2, 1] = 1.0 / w
            regs = ((core * NL + np.arange(8), L), (core * NL + NL - 8 + np.arange(8), L),
                    (np.arange(8), NCTX), (NCTX - 8 + np.arange(8), NCTX))
            for r, (t, ln) in enumerate(regs):
                lo = np.clip(t - w // 2, 0, ln)
                hi = np.clip(t - w // 2 + w, 0, ln)
                ic[sl, g2, r, :] = (1.0 / (hi - lo).astype(np.float32))[None, :]
    pv[:, 0, 2] = 1.0 if core > 0 else 0.0
    pv[:, 1, 2] = 1.0 if core < NCORE - 1 else 0.0
    return pv, ic


def _na_block(rpb_h, qrow0, krow0):
    a = np.repeat(np.arange(2), 64)
    qc = np.tile(np.arange(64), 2)
    qrow = (qrow0 + a)[:, None]
    krow = (krow0 + a)[None, :]
    kc = qc[None, :]
    qcc = qc[:, None]
    rs = np.clip(qrow - 4, 0, ROWS - 8)
    vrow = (krow >= 0) & (krow < ROWS) & (krow >= rs) & (krow < rs + 8)
    ws = np.clip(qcc - 8, 0, GW - 16)
    vcol = (kc >= ws) & (kc < ws + 16)
    dr = np.clip(krow - qrow + 7, 0, 14)
    dc = np.clip(kc - qcc + 15, 0, 30)
    val = rpb_h[dr, dc]
    return np.where(vrow & vcol, val, np.float32(NEG)).astype(np.float32)


def na_tables(rpb_l, core):
    gt = np.zeros((128, 6, 5, 128), np.float32)
    st = np.zeros((4, 128, 6, 6, 128), np.float32)
    for h in range(6):
        for s in range(5):
            gt[:, h, s, :] = _na_block(rpb_l[h], 32 + 8, 32 + 2 * (4 + s) - 4).T
        for si, (pi, b0) in enumerate(SPECIAL):
            for s in range(6):
                st[si, :, h, s, :] = _na_block(rpb_l[h], RPC * core + 2 * pi, RPC * core + 2 * (b0 + s) - 4).T
    return gt.reshape(128, 6, 640).astype(NPBF), st.reshape(4, 128, 6, 768).astype(NPBF)


def dm_table():
    j = np.arange(128)[:, None]
    i = np.arange(128)[None, :]
    dm = np.zeros((128, 4, 128), np.float32)
    dm[:, 0, :] = np.maximum(i - j, 0)
    dm[:, 1, :] = (i >= j)
    dm[:, 2, :] = np.maximum(j - i, 0)
    dm[:, 3, :] = (j >= i)
    return dm


def coef_inputs(core, lg_f, lg_b):
    cf = np.zeros((128, 3, 2, 3, 9), np.float32)
    for c in range(3):
        for half in range(2):
            sl = slice(half * 64, (half + 1) * 64)
            h = 2 * c + half
            cf[sl, 1, 0, c, :] = lg_f[h]
            cf[sl, 1, 1, c, :] = lg_b[h]
    for j in range(NCORE):
        if j < core:
            cf[:, 0, 0, :, j] = NL * (core - 1 - j)
            cf[:, 2, 0, :, j] = 1.0
        if j > core:
            cf[:, 0, 1, :, j] = NL * (j - core - 1)
            cf[:, 2, 1, :, j] = 1.0
    cf[:, 0, 0, :, 8] = NL * core
    cf[:, 2, 0, :, 8] = 1.0
    cf[:, 0, 1, :, 8] = NL * (NCORE - 1 - core)
    cf[:, 2, 1, :, 8] = 1.0
    return cf.reshape(128, 3, 54)


def assemble_sall(sls):
    sa = np.zeros((128, 2, 3, 9, 128), np.float32)
    for j in range(NCORE):
        sa[:, 0, :, j, :] = sls[j][:, 0]
        sa[:, 1, :, j, :] = sls[j][:, 1]
    sa[:, 0, :, 8, :] = sls[0][:, 2]
    sa[:, 1, :, 8, :] = sls[0][:, 3]
    return sa


def slab_with_halo(h_full, core):
    out = np.zeros((NS, h_full.shape[1]), h_full.dtype)
    t0 = core * NL - HB
    a = max(t0, 0)
    b = min(t0 + NS, L)
    out[a - t0:b - t0] = h_full[a:b]
    return out


def _run(nc, maps):
    res = run_bass_kernel_spmd(nc, maps, core_ids=list(range(NCORE)))
    return res.results


def kernel(x, c, ctx, c_ctx, w_mod, b_mod, norm1_g, w_in, pool_w, pool_scale, na_rpb, ret_decay_fwd,
           ret_decay_bwd, ret_gn_g, w_out, norm2_g, w_up, conv_w, conv_b, w_down, final_g):
    f = lambda a: np.ascontiguousarray(np.asarray(a, dtype=np.float32))
    x, c, ctx, c_ctx, w_mod, b_mod, norm1_g, w_in, pool_w, pool_scale, na_rpb, ret_decay_fwd, ret_decay_bwd, \
        ret_gn_g, w_out, norm2_g, w_up, conv_w, conv_b, w_down, final_g = map(f, (
            x, c, ctx, c_ctx, w_mod, b_mod, norm1_g, w_in, pool_w, pool_scale, na_rpb, ret_decay_fwd, ret_decay_bwd,
            ret_gn_g, w_out, norm2_g, w_up, conv_w, conv_b, w_down, final_g))
    mv = run_mod(c, c_ctx, w_mod, b_mod)
    h = x[0]
    hc = ctx[0]
    cst = make_cst()
    ropes = [rope_tables(i) for i in range(NCORE)]
    dm = dm_table()
    out = None
    for l in range(DEPTH):
        last = l == DEPTH - 1
        mod6 = mod6_layout(mv[l])
        g12 = np.ascontiguousarray(np.stack([vec_pk(norm1_g[l]), vec_pk(norm2_g[l])], 1))
        dk, cexp, lgc = decay_inputs(ret_decay_fwd[l], ret_decay_bwd[l])
        hcT = to_fm(hc)
        wk, wv = prep_r_weights(w_in[l])
        maps = []
        for i in range(NCORE):
            hT = np.concatenate([to_fm(h[i * NL:(i + 1) * NL]), hcT], 2)
            maps.append({"hT": hT, "mod6": mod6, "g12": g12, "wk": wk, "wv": wv, "rope": ropes[i], "dk": dk,
                         "cexp": cexp, "cst": cst})
        res = _run(_prog("r", build_r), maps)
        sall = assemble_sall([res[i]["sl"] for i in range(NCORE)])
        wf, wt, wo, wbd = prep_m1_weights(w_in[l], w_out[l], pool_w[l])
        gng = vec_pk(ret_gn_g[l])
        maps = []
        for i in range(NCORE):
            hT = np.concatenate([to_fm(slab_with_halo(h, i)), hcT], 2)
            pv, ic = pool_consts(i, pool_scale[l])
            gt, st = na_tables(na_rpb[l], i)
            maps.append({"hT": hT, "mod6": mod6, "g12": g12, "wfm": wf, "wtm": wt, "rope": ropes[i], "gtab": gt,
                         "stab": st, "wbd": wbd, "pvec": pv, "invcnt": ic, "dk": dk, "dm": dm, "lgc": lgc,
                         "cexp": cexp, "coef": coef_inputs(i, ret_decay_fwd[l], ret_decay_bwd[l]), "sall": sall,
                         "gng": gng, "wout": wo, "cst": cst})
        res = _run(_prog("m1", build_m1), maps)
        hmid = [res[i]["hmid"] for i in range(NCORE)]
        h2 = [np.asarray(res[i]["h2"]) for i in range(NCORE)]
        wup, cw, wdn = prep_m2_weights(w_up[l], conv_w[l], conv_b[l], w_down[l])
        g2 = np.ascontiguousarray(np.stack([vec_pk(mv[l, 0, 5 * D:]), vec_pk(mv[l, 1, 5 * D:])], 1))
        maps = []
        for i in range(NCORE):
            prev = h2[i - 1][:, :, NL - 1] if i > 0 else None
            nxt = h2[i + 1][:, :, 0] if i < NCORE - 1 else None
            h2e = make_h2e(h2[i][:, :, :NL], prev, nxt, h2[i][:, :, NL:])
            m = {"hmid": hmid[i], "h2e": h2e, "wup": wup, "cw": cw, "wdn": wdn, "g2": g2, "cst": cst}
            if last:
                m["fg"] = vec_pk(final_g)
            maps.append(m)
        if last:
            res = _run(_prog("m2f", lambda: build_m2(True)), maps)
            out = np.concatenate([from_fm(res[i]["outT"]) for i in range(NCORE)], 0)
        else:
            res = _run(_prog("m2", lambda: build_m2(False)), maps)
            h = np.concatenate([from_fm(res[i]["hn"][:, :, :NL]) for i in range(NCORE)], 0)
            hc = from_fm(res[0]["hn"][:, :, NL:])
    return out.reshape(1, L, D).astype(np.float32)
```

```python
import numpy as np
from contextlib import ExitStack
import ml_dtypes
import concourse.bass as bass
import concourse.mybir as mybir
from concourse.bass_utils import run_bass_kernel_spmd

F32 = mybir.dt.float32
BF16 = mybir.dt.bfloat16
ALU = mybir.AluOpType
AF = mybir.ActivationFunctionType
NPBF = ml_dtypes.bfloat16

NCORE = 8
D = 1024
L = 16384
GW = 64
ROWS = 256
RPC = 32
NL = 2048
NCTX = 256
NT = NL + NCTX
HB = 256
HA = 192
NS = HB + NL + HA
NT1 = NS + NCTX
DEPTH = 4
DFF = 2816
EPS = 1e-6
POOL_WINDOWS = (2, 4, 8, 16)
NEG = -1e30
SPECIAL = ((0, 0), (1, 1), (14, 13), (15, 14))


class Buf:
    __slots__ = ("w", "r")

    def __init__(self):
        self.w = None
        self.r = {}


class Sched:
    def __init__(self, nc, es, n_dma_sems=10):
        self.nc = nc
        self.names = ("pe", "act", "dve", "pool", "sp")
        self.sems = []
        self.cnt = []
        self._es = es
        self.esem = {e: self._new_sem("s_" + e) for e in ("pe", "act", "dve", "pool")}
        self.dq = {q: [self._new_sem(f"d_{q}{i}") for i in range(n_dma_sems)] for q in ("sp", "pool", "act")}
        self.dq_next = {q: 0 for q in self.dq}
        self.obs = {e: {} for e in self.names}
        self.prog = {e: [] for e in self.names}

    def _new_sem(self, name):
        s = self._es.enter_context(self.nc.semaphore(name))
        self.sems.append(s)
        self.cnt.append(0)
        return len(self.sems) - 1

    def _wait(self, e, deps):
        best = {}
        for d in deps:
            if d is None:
                continue
            s, v = d
            if best.get(s, 0) < v:
                best[s] = v
        for s, v in best.items():
            if e == "pe" and s == self.esem["pe"]:
                continue
            if self.obs[e].get(s, 0) >= v:
                continue
            self.prog[e].append((0, s, v))
            self.obs[e][s] = v

    def _deps(self, reads, writes, own=None):
        deps = []
        for b in reads:
            deps.append(b.w)
        for b in writes:
            if b.w is not None and b.w[0] != own:
                deps.append(b.w)
            for s, v in b.r.items():
                if s != own:
                    deps.append((s, v))
        return deps

    def _commit(self, ev, reads, writes):
        s, v = ev
        for b in reads:
            if b.r.get(s, 0) < v:
                b.r[s] = v
        for b in writes:
            b.w = ev
            b.r = {}

    def op(self, e, fn, reads=(), writes=()):
        self._wait(e, self._deps(reads, writes))
        s = self.esem[e]
        self.prog[e].append((1, fn, s, 1))
        self.cnt[s] += 1
        ev = (s, self.cnt[s])
        self._commit(ev, reads, writes)
        return ev

    def dma(self, q, out, in_, reads=(), writes=()):
        pool = self.dq[q]
        i = self.dq_next[q]
        self.dq_next[q] = (i + 1) % len(pool)
        s = pool[i]
        deps = self._deps(reads, writes)
        deps.append((s, self.cnt[s]))
        self._wait(q, deps)
        self.prog[q].append((1, (lambda e, out=out, in_=in_: e.dma_start(out=out, in_=in_)), s, 16))
        self.cnt[s] += 16
        ev = (s, self.cnt[s])
        self._commit(ev, reads, writes)
        return ev

    def mm(self, out, lhsT, rhs, start=True, stop=True, reads=(), writes=(), tp=None):
        kw = {} if tp is None else {"tile_position": tp}
        return self.op("pe", lambda e: e.matmul(out, lhsT, rhs, start=start, stop=stop, **kw), reads, writes)

    def tr(self, out, in_, ident, reads=(), writes=()):
        return self.op("pe", lambda e: e.transpose(out, in_, ident), reads, writes)

    def act(self, out, in_, func, bias=None, scale=None, reads=(), writes=()):
        kw = {}
        if bias is not None:
            kw["bias"] = bias
        if scale is not None:
            kw["scale"] = scale
        return self.op("act", lambda e: e.activation(out, in_, func, **kw), reads, writes)

    def tt(self, eng, out, a, b, op, reads=(), writes=()):
        return self.op(eng, lambda e: e.tensor_tensor(out, a, b, op), reads, writes)

    def ts(self, eng, out, a, s1, op0, s2=None, op1=None, reads=(), writes=()):
        if op1 is None:
            return self.op(eng, lambda e: e.tensor_scalar(out, a, s1, None, op0), reads, writes)
        return self.op(eng, lambda e: e.tensor_scalar(out, a, s1, s2, op0, op1), reads, writes)

    def stt(self, eng, out, in0, scalar, in1, op0, op1, reads=(), writes=()):
        return self.op(eng, lambda e: e.scalar_tensor_tensor(out, in0, scalar, in1, op0, op1), reads, writes)

    def cp(self, eng, out, in_, reads=(), writes=()):
        if eng == "act":
            return self.op("act", lambda e: e.copy(out, in_), reads, writes)
        return self.op(eng, lambda e: e.tensor_copy(out, in_), reads, writes)

    def recip(self, out, in_, reads=(), writes=()):
        return self.op("dve", lambda e: e.reciprocal(out, in_), reads, writes)

    def memset(self, eng, ap, val, writes=()):
        return self.op(eng, lambda e: e.memset(ap, val), (), writes)

    def barrier(self):
        deps = [(s, c) for s, c in enumerate(self.cnt) if c > 0]
        for e in self.names:
            self._wait(e, deps)

    def emit(self):
        self.barrier()
        nc = self.nc
        sems = self.sems
        prog = self.prog

        def replay(name):
            def f(eng):
                for it in prog[name]:
                    if it[0] == 0:
                        eng.wait_ge(sems[it[1]], it[2])
                    else:
                        it[1](eng).then_inc(sems[it[2]], it[3])
            return f
        with nc.Block() as block:
            block.sync(replay("sp"))
            block.scalar(replay("act"))
            block.vector(replay("dve"))
            block.gpsimd(replay("pool"))
            block.tensor(replay("pe"))


class Ring:
    def __init__(self, nc, es, name, shape, dt, n, psum=False):
        alloc = nc.psum_tensor if psum else nc.sbuf_tensor
        self.t = [es.enter_context(alloc(f"{name}{i}", shape, dt)) for i in range(n)]
        self.b = [Buf() for _ in range(n)]
        self.i = 0

    def next(self):
        i = self.i
        self.i = (i + 1) % len(self.t)
        return self.t[i], self.b[i]


def _sb(nc, es, name, shape, dt):
    return es.enter_context(nc.sbuf_tensor(name, shape, dt))


def _ps(nc, es, name, shape, dt):
    return es.enter_context(nc.psum_tensor(name, shape, dt))


def _split(c0, c1, step):
    out = []
    while c0 < c1:
        out.append((c0, min(c0 + step, c1)))
        c0 += step
    return out


def load_const(S, nc, es, name, dram, shape, dt=F32, q="sp"):
    t = _sb(nc, es, name, shape, dt)
    b = Buf()
    S.dma(q, t[:], dram[:], writes=[b])
    return t, b


def emit_mod_scalars(S, nc, es, mod6, g1, g2, pfx):
    (m6, m6b), (g1t, g1b), (g2t, g2b) = mod6, g1, g2
    A = _sb(nc, es, pfx + "A", [128, 2, 8], F32)
    b = Buf()
    S.ts("dve", A[:, 0, :], m6[:, 1, :], 1.0, ALU.add, reads=[m6b], writes=[b])
    S.tt("dve", A[:, 0, :], A[:, 0, :], g1t[:], ALU.mult, reads=[b, g1b], writes=[b])
    S.ts("dve", A[:, 1, :], m6[:, 4, :], 1.0, ALU.add, reads=[m6b, b], writes=[b])
    S.tt("dve", A[:, 1, :], A[:, 1, :], g2t[:], ALU.mult, reads=[b, g2b], writes=[b])
    return dict(A1=A[:, 0, :], B1=m6[:, 0, :], G1=m6[:, 2, :], A2=A[:, 1, :], B2=m6[:, 3, :], G2=m6[:, 5, :],
                bufs=[b, m6b])


def emit_norm_stats(S, cst, ht, htb, w, ring_sq, ring_ps):
    sq, sqb = ring_sq.next()
    S.act(sq[:, :, 0:w], ht, AF.Square, reads=[htb], writes=[sqb])
    ps, psb = ring_ps.next()
    for k in range(8):
        S.mm(ps[:, 0:w], cst["ones"][:], sq[:, k, 0:w], start=(k == 0), stop=(k == 7),
             reads=[sqb, cst["b"]], writes=[psb])
    return ps, psb


def emit_norm_apply(S, cst, stats, ht, htb, w, Asc, Bsc, scb, out_fn, ring_small, ring_tmp):
    ps, psb = stats
    sd, sdb = ring_small.next()
    S.act(sd[:, 0:w], ps[:, 0:w], AF.Sqrt, bias=cst["eps"][:], scale=1.0, reads=[psb, cst["b"]], writes=[sdb])
    S.recip(sd[:, 0:w], sd[:, 0:w], reads=[sdb], writes=[sdb])
    for k in range(8):
        dst, dstb = out_fn(k)
        if Bsc is None:
            S.stt("dve", dst, ht[:, k, :], Asc[:, k:k + 1], sd[:, 0:w], ALU.mult, ALU.mult,
                  reads=[htb, sdb] + scb, writes=[dstb])
        else:
            tmp, tmpb = ring_tmp.next()
            S.stt("dve", tmp[:, 0:w], ht[:, k, :], Asc[:, k:k + 1], sd[:, 0:w], ALU.mult, ALU.mult,
                  reads=[htb, sdb] + scb, writes=[tmpb])
            S.act(dst, tmp[:, 0:w], AF.Identity, bias=Bsc[:, k:k + 1], scale=1.0, reads=[tmpb] + scb, writes=[dstb])


def emit_norm_tile(S, cst, ht, htb, w, Asc, Bsc, scb, out_fn, ring_sq, ring_ps, ring_small, ring_tmp):
    st = emit_norm_stats(S, cst, ht, htb, w, ring_sq, ring_ps)
    emit_norm_apply(S, cst, st, ht, htb, w, Asc, Bsc, scb, out_fn, ring_small, ring_tmp)


def emit_consts(S, nc, es, d_cst):
    t, b = load_const(S, nc, es, "cst_sb", d_cst, [128, 513], F32)
    identb = _sb(nc, es, "identb", [128, 128], BF16)
    bb = Buf()
    S.cp("dve", identb[:], t[:, 128:256], reads=[b], writes=[bb])
    return dict(ones=t[:, 0:128], identf=t[:, 128:256], bd64=t[:, 256:384], eps=t[:, 384:385], bdm=t[:, 385:513], b=b,
                identb=identb, identb_b=bb)


def make_cst():
    c = np.zeros((128, 513), np.float32)
    c[:, 0:128] = 1.0 / 1024.0
    c[:, 128:256] = np.eye(128, dtype=np.float32)
    bd = np.zeros((128, 128), np.float32)
    bd[:64, :64] = 1.0 / 64
    bd[64:, 64:] = 1.0 / 64
    c[:, 256:384] = bd
    c[:, 384] = EPS
    c[:, 385:513] = (bd > 0).astype(np.float32)
    return c


def emit_rotary(S, cst_rot, ps, psb, w, tcol0, scale, dst, dstb, rings):
    C, Sg, P, rb = cst_rot
    rawb, rawbb = rings["raw"].next()
    S.act(rawb[:, 0:w], ps[:, 0:w], AF.Copy, scale=scale, reads=[psb], writes=[rawbb])
    t1, t1b = rings["t1"].next()
    S.tt("dve", t1[:, 0:w], rawb[:, 0:w], C[:, tcol0:tcol0 + w], ALU.mult, reads=[rawbb, rb], writes=[t1b])
    pp, ppb = rings["pp"].next()
    S.mm(pp[:, 0:w], P[:], rawb[:, 0:w], reads=[rawbb, rb], writes=[ppb])
    t2, t2b = rings["t2"].next()
    S.tt("dve", t2[:, 0:w], pp[:, 0:w], Sg[:, tcol0:tcol0 + w], ALU.mult, reads=[ppb, rb], writes=[t2b])
    S.tt("pool", dst, t1[:, 0:w], t2[:, 0:w], ALU.add, reads=[t1b, t2b], writes=[dstb])


def emit_ret_chain(S, nc, rk, rkb, vret, vretb, vcol0, c, blocks, kcols, vblks, Sf_init, Sb_init,
                   dec, cst, Sfb, Sbb, Sfb_b, Sbb_b, rings, sidx):
    nb = len(blocks)
    ktf, ktb_, cfb, decb = dec["ktab_f"], dec["ktab_b"], dec["cfb"], dec["b"]
    D2, D2b = rings["d2"]
    bdm = cst["bdm"]
    for n in range(nb):
        kc0 = kcols[n]
        tp_, tpb = rings["tps"].next()
        S.tr(tp_[:], rk[:, c, kc0:kc0 + 128], cst["identb"][:], reads=[rkb, cst["identb_b"]], writes=[tpb])
        kdf, kdfb = rings["kd"].next()
        S.tt("dve", kdf[:], tp_[:], ktf[c], ALU.mult, reads=[tpb, decb], writes=[kdfb])
        kdb, kdbb = rings["kd"].next()
        S.tt("dve", kdb[:], tp_[:], ktb_[c], ALU.mult, reads=[tpb, decb], writes=[kdbb])
        vv = vret[:, vblks[n], vcol0 + c * 128: vcol0 + (c + 1) * 128]
        dp, dpb = rings["dps"].next()
        S.mm(dp[:, 0:128], kdf[:], vv, reads=[kdfb, vretb], writes=[dpb])
        S.mm(dp[:, 128:256], kdb[:], vv, reads=[kdbb, vretb], writes=[dpb])
        S.cp("act", D2[:, n, :], dp[:], reads=[dpb], writes=[D2b[n]])
    f32a, f32ab = rings["sf32"].next()
    g32, g32b = rings["sb32"].next()
    for (t_, tb_, init, Sx, Sxb, i0) in ((f32a, f32ab, Sf_init, Sfb, Sfb_b, 0), (g32, g32b, Sb_init, Sbb, Sbb_b, nb)):
        if init is None:
            S.memset("pool", t_[:], 0.0, writes=[tb_])
        else:
            S.cp("pool", t_[:], init[0], reads=[init[1]], writes=[tb_])
        S.tt("pool", Sx[:, sidx(i0), :], t_[:], bdm[:], ALU.mult, reads=[tb_, cst["b"]], writes=[Sxb])
    curf, curfb = f32a, f32ab
    curg, curgb = g32, g32b
    for n in range(nb):
        m = nb - 1 - n
        nxt, nxtb = rings["sf32"].next()
        nyt, nytb = rings["sb32"].next()
        S.stt("dve", nxt[:], curf[:], cfb[:, 0, c:c + 1], D2[:, n, 0:128], ALU.mult, ALU.add,
              reads=[curfb, D2b[n], decb], writes=[nxtb])
        S.stt("dve", nyt[:], curg[:], cfb[:, 1, c:c + 1], D2[:, m, 128:256], ALU.mult, ALU.add,
              reads=[curgb, D2b[m], decb], writes=[nytb])
        S.tt("pool", Sfb[:, sidx(n + 1), :], nxt[:], bdm[:], ALU.mult, reads=[nxtb, cst["b"]], writes=[Sfb_b])
        S.tt("pool", Sbb[:, sidx(m), :], nyt[:], bdm[:], ALU.mult, reads=[nytb, cst["b"]], writes=[Sbb_b])
        curf, curfb = nxt, nxtb
        curg, curgb = nyt, nytb
    return (curf, curfb), (curg, curgb)


def build_mod():
    nc = bass.Bass("TRN2", target_bir_lowering=False)
    d_c = nc.dram_tensor("cvec", [128, 16], F32, kind="ExternalInput")
    d_w = nc.dram_tensor("wmod", [DEPTH, 128, 8, 768], F32, kind="ExternalInput")
    d_b = nc.dram_tensor("bmod", [2, DEPTH * 768], F32, kind="ExternalInput")
    d_o = nc.dram_tensor("modv", [2, DEPTH * 768], F32, kind="ExternalOutput")
    with ExitStack() as es:
        S = Sched(nc, es)
        cv, cvb = load_const(S, nc, es, "cv", d_c, [128, 16])
        bt, btb = load_const(S, nc, es, "bt", d_b, [2, DEPTH * 768])
        sv = _sb(nc, es, "sv", [128, 16], F32)
        svb = Buf()
        S.act(sv[:], cv[:], AF.Silu, reads=[cvb], writes=[svb])
        wr = Ring(nc, es, "w", [128, 8, 768], F32, 2)
        pr = Ring(nc, es, "ps", [128, 1024], F32, 2, psum=True)
        ot = _sb(nc, es, "ot", [2, DEPTH * 768], F32)
        otb = Buf()
        for l in range(DEPTH):
            w, wb = wr.next()
            S.dma("sp", w[:], d_w[l], writes=[wb])
            ps, psb = pr.next()
            for (a, b_, o) in ((0, 512, 0), (512, 768, 512)):
                for k in range(8):
                    S.mm(ps[0:2, o:o + (b_ - a)], sv[:, 2 * k:2 * k + 2], w[:, k, a:b_], start=(k == 0), stop=(k == 7),
                         reads=[svb, wb], writes=[psb])
            S.tt("dve", ot[:, l * 768:(l + 1) * 768], ps[0:2, 0:768], bt[:, l * 768:(l + 1) * 768], ALU.add,
                 reads=[psb, btb], writes=[otb])
        S.dma("sp", d_o[:], ot[:], reads=[otb])
        S.emit()
    return nc


def to_fm(a):
    T, F = a.shape
    return np.ascontiguousarray(a.T.reshape(F // 128, 128, T).transpose(1, 0, 2))


def from_fm(a):
    P, K, T = a.shape
    return np.ascontiguousarray(a.transpose(1, 0, 2).reshape(K * 128, T).T)


def vec_pk(v):
    return np.ascontiguousarray(v.reshape(-1, 128).T)


_PROGS = {}


def _prog(name, builder):
    if name not in _PROGS:
        _PROGS[name] = builder()
    return _PROGS[name]


def run_mod(c, c_ctx, w_mod, b_mod, cores=NCORE):
    nc = _prog("mod", build_mod)
    cvec = np.zeros((128, 16), np.float32)
    cvec[:, 0::2] = vec_pk(c.reshape(-1))
    cvec[:, 1::2] = vec_pk(c_ctx.reshape(-1))
    maps = []
    for i in range(cores):
        cols = slice(i * 768, (i + 1) * 768)
        w = w_mod[:, :, cols]
        w = np.ascontiguousarray(w.reshape(DEPTH, 8, 128, 768).transpose(0, 2, 1, 3))
        b = np.ascontiguousarray(np.broadcast_to(b_mod[:, cols].reshape(1, DEPTH * 768), (2, DEPTH * 768)))
        maps.append({"cvec": cvec, "wmod": w, "bmod": b})
    res = run_bass_kernel_spmd(nc, maps, core_ids=list(range(cores)))
    out = np.zeros((DEPTH, 2, 6 * D), np.float32)
    for i in range(cores):
        o = res.results[i]["modv"].reshape(2, DEPTH, 768)
        out[:, :, i * 768:(i + 1) * 768] = o.transpose(1, 0, 2)
    return out


NH2 = 2308


def build_m2(final):
    nc = bass.Bass("TRN2", target_bir_lowering=False)
    ncols = NL if final else NT
    d_hm = nc.dram_tensor("hmid", [128, 8, NT], F32, kind="ExternalInput")
    d_h2 = nc.dram_tensor("h2e", [128, 8, NH2], BF16, kind="ExternalInput")
    d_wu = nc.dram_tensor("wup", [44, 128, 8, 128], F32, kind="ExternalInput")
    d_cw = nc.dram_tensor("cw", [128, 44, 4], F32, kind="ExternalInput")
    d_wd = nc.dram_tensor("wdn", [8, 128, 22, 128], F32, kind="ExternalInput")
    d_g2 = nc.dram_tensor("g2", [128, 2, 8], F32, kind="ExternalInput")
    d_cst = nc.dram_tensor("cst", [128, 513], F32, kind="ExternalInput")
    if final:
        d_fg = nc.dram_tensor("fg", [128, 8], F32, kind="ExternalInput")
        d_hn = nc.dram_tensor("hn_scratch", [128, 8, NL], F32)
        d_out = nc.dram_tensor("outT", [128, 8, NL], F32, kind="ExternalOutput")
    else:
        d_hn = nc.dram_tensor("hn", [128, 8, NT], F32, kind="ExternalOutput")
    if final:
        up_tiles = [(0, 512), (510, 1022), (1020, 1532), (1530, 2042), (2040, 2050)]
        dn_tiles = [(1, 513, 0), (513, 1025, 512), (1025, 1537, 1024), (1537, 2049, 1536)]
    else:
        up_tiles = [(0, 512), (510, 1022), (1020, 1532), (1530, 2042), (2040, 2308)]
        dn_tiles = [(1, 513, 0), (513, 1025, 512), (1025, 1537, 1024), (1537, 2049, 1536), (2051, 2307, 2048)]
    with ExitStack() as es0:
        S = Sched(nc, es0)
        cst = emit_consts(S, nc, es0, d_cst)
        with ExitStack() as es:
            h2, h2b = load_const(S, nc, es, "h2_sb", d_h2, [128, 8, NH2], BF16)
            cw, cwb = load_const(S, nc, es, "cw_sb", d_cw, [128, 44, 4])
            g2, g2b = load_const(S, nc, es, "g2_sb", d_g2, [128, 2, 8])
            g = _sb(nc, es, "g", [128, 22, NH2], BF16)
            gb = [Buf() for _ in range(22)]
            wr = Ring(nc, es, "wu", [128, 8, 128], BF16, 6)
            pa = Ring(nc, es, "pa", [128, 512], F32, 3, psum=True)
            pb = Ring(nc, es, "pb", [128, 512], F32, 2, psum=True)
            pd = Ring(nc, es, "pd", [128, 512], F32, 3, psum=True)
            tar = Ring(nc, es, "ta", [128, 512], F32, 4)
            tbr = Ring(nc, es, "tb", [128, 512], F32, 4)
            sar = Ring(nc, es, "sa", [128, 512], F32, 4)
            wdr = Ring(nc, es, "wd", [128, 22, 128], BF16, 2)
            hmr = Ring(nc, es, "hm", [128, 512], F32, 3)
            hor = Ring(nc, es, "ho", [128, 512], F32, 3)
            def issue_w(f):
                wa, wab = wr.next()
                S.dma("pool", wa[:], d_wu[f], writes=[wab])
                wb_, wbb = wr.next()
                S.dma("pool", wb_[:], d_wu[22 + f], writes=[wbb])
                return (wa, wab, wb_, wbb)
            pend_w = [issue_w(f) for f in range(2)]
            for f in range(22):
                if f + 2 < 22:
                    pend_w.append(issue_w(f + 2))
                wa, wab, wb_, wbb = pend_w.pop(0)
                for (c0, c1) in up_tiles:
                    w = c1 - c0
                    psa, psab = pa.next()
                    for k in range(8):
                        S.mm(psa[:, 0:w], wa[:, k, :], h2[:, k, c0:c1], start=(k == 0), stop=(k == 7),
                             reads=[wab, h2b], writes=[psab])
                    psb, psbb = pb.next()
                    for k in range(8):
                        S.mm(psb[:, 0:w], wb_[:, k, :], h2[:, k, c0:c1], start=(k == 0), stop=(k == 7),
                             reads=[wbb, h2b], writes=[psbb])
                    n = w - 2
                    outs = []
                    for (ps_, psb_, ring, j) in ((psa, psab, tar, f), (psb, psbb, tbr, 22 + f)):
                        t, tb = ring.next()
                        S.act(t[:, 0:n], ps_[:, 1:1 + n], AF.Identity, bias=cw[:, j, 3:4], scale=cw[:, j, 1:2],
                              reads=[psb_, cwb], writes=[tb])
                        S.stt("dve", t[:, 0:n], ps_[:, 0:n], cw[:, j, 0:1], t[:, 0:n], ALU.mult, ALU.add,
                              reads=[psb_, cwb, tb], writes=[tb])
                        S.stt("dve", t[:, 0:n], ps_[:, 2:2 + n], cw[:, j, 2:3], t[:, 0:n], ALU.mult, ALU.add,
                              reads=[psb_, cwb, tb], writes=[tb])
                        outs.append((t, tb))
                    (ta, tab), (tb2, tb2b) = outs
                    sa, sab = sar.next()
                    S.act(sa[:, 0:n], ta[:, 0:n], AF.Silu, reads=[tab], writes=[sab])
                    S.tt("pool", g[:, f, c0 + 1:c0 + 1 + n], sa[:, 0:n], tb2[:, 0:n], ALU.mult,
                         reads=[sab, tb2b], writes=[gb[f]])
            for m in range(8):
                wd, wdb = wdr.next()
                S.dma("pool", wd[:], d_wd[m], writes=[wdb])
                for (c0, c1, o0) in dn_tiles:
                    w = c1 - c0
                    ps, psb = pd.next()
                    for f in range(22):
                        S.mm(ps[:, 0:w], wd[:, f, :], g[:, f, c0:c1], start=(f == 0), stop=(f == 21),
                             reads=[wdb, gb[f]], writes=[psb])
                    hm, hmb = hmr.next()
                    S.dma("sp", hm[:, 0:w], d_hm[:, m, o0:o0 + w], writes=[hmb])
                    ho, hob = hor.next()
                    s = 1 if o0 >= NL else 0
                    S.stt("dve", ho[:, 0:w], ps[:, 0:w], g2[:, s, m:m + 1], hm[:, 0:w], ALU.mult, ALU.add,
                          reads=[psb, hmb, g2b], writes=[hob])
                    S.dma("act", d_hn[:, m, o0:o0 + w], ho[:, 0:w], reads=[hob])
            S.barrier()
        if final:
            with ExitStack() as es:
                fg, fgb = load_const(S, nc, es, "fg_sb", d_fg, [128, 8])
                htr = Ring(nc, es, "ht", [128, 8, 256], F32, 2)
                sqr = Ring(nc, es, "sq", [128, 8, 256], F32, 1)
                psr = Ring(nc, es, "nps", [128, 512], F32, 2, psum=True)
                smr = Ring(nc, es, "sd", [128, 256], F32, 2)
                otr = Ring(nc, es, "ot", [128, 8, 256], F32, 2)
                for (c0, c1) in _split(0, NL, 256):
                    w = c1 - c0
                    ht, htb = htr.next()
                    S.dma("sp", ht[:, :, 0:w], d_hn[:, :, c0:c1], writes=[htb])
                    ot, otb = otr.next()
                    emit_norm_tile(S, cst, ht[:, :, 0:w], htb, w, fg, None, [fgb],
                                   lambda k, ot=ot, otb=otb, w=w: (ot[:, k, 0:w], otb), sqr, psr, smr, None)
                    S.dma("act", d_out[:, :, c0:c1], ot[:, :, 0:w], reads=[otb])
        S.emit()
    return nc


def prep_m2_weights(w_up_l, conv_w_l, conv_b_l, w_down_l):
    wup = np.ascontiguousarray(w_up_l.reshape(8, 128, 44, 128).transpose(2, 1, 0, 3))
    cw = np.zeros((128, 44, 4), np.float32)
    cw[:, :, 0:3] = conv_w_l.T.reshape(44, 128, 3).transpose(1, 0, 2)
    cw[:, :, 3] = conv_b_l.reshape(44, 128).T
    wdn = np.ascontiguousarray(w_down_l.reshape(22, 128, 8, 128).transpose(2, 1, 0, 3))
    return wup, cw, wdn


def make_h2e(h2_loc, h2_prev_last, h2_next_first, h2_ctx):
    e = np.zeros((128, 8, NH2), NPBF)
    if h2_prev_last is not None:
        e[:, :, 0] = h2_prev_last
    e[:, :, 1:2049] = h2_loc
    if h2_next_first is not None:
        e[:, :, 2049] = h2_next_first
    e[:, :, 2051:2307] = h2_ctx
    return e


FM_TILES_ALL = [(0, 512), (512, 1024), (1024, 1536), (1536, 2048), (2048, 2496), (2496, 2752)]
FM_TILES_LOC = [(256, 768), (768, 1280), (1280, 1792), (1792, 2304), (2496, 2752)]
FM_TILES_POOL = [(248, 760), (760, 1272), (1272, 1784), (1784, 2296), (2296, 2312), (2496, 2752)]
NPP = 2336


def loc_col(s):
    return s - HB if s < NS else NL + (s - NS)


def emit_norm_phase(S, nc, cst, d_h, ncols_list, ax, axb_of, scal_x, scal_c, which):
    with ExitStack() as es:
        htr = Ring(nc, es, "nht", [128, 8, 256], F32, 3)
        sqr = Ring(nc, es, "nsq", [128, 8, 256], F32, 2)
        psr = Ring(nc, es, "nps", [128, 512], F32, 3, psum=True)
        smr = Ring(nc, es, "nsd", [128, 256], F32, 2)
        tmr = Ring(nc, es, "ntm", [128, 256], F32, 4)
        pend = None

        def apply(p):
            (c0, c1, isc, ht, htb, w, st) = p
            sc = scal_c if isc else scal_x
            b = axb_of(c0)
            emit_norm_apply(S, cst, st, ht[:, :, 0:w], htb, w, sc["A" + which], sc["B" + which], sc["bufs"],
                            lambda k, c0=c0, c1=c1, b=b: (ax[:, k, c0:c1], b), smr, tmr)
        for (c0, c1, isc) in ncols_list:
            w = c1 - c0
            ht, htb = htr.next()
            S.dma("sp", ht[:, :, 0:w], d_h[:, :, c0:c1], writes=[htb])
            st = emit_norm_stats(S, cst, ht[:, :, 0:w], htb, w, sqr, psr)
            if pend is not None:
                apply(pend)
            pend = (c0, c1, isc, ht, htb, w, st)
        apply(pend)
        S.barrier()


def emit_decays(S, nc, es, d_dk, d_dm, d_lgc, d_cexp, need_q):
    tab = _sb(nc, es, "dectab", [128, 1536], F32)
    cfb = _sb(nc, es, "cfb", [128, 2, 3], F32)
    mc = _sb(nc, es, "mcomb", [128, 6, 128], F32) if need_q else None
    b = Buf()
    with ExitStack() as esi:
        dk, dkb = load_const(S, nc, esi, "dk_sb", d_dk, [128, 2, 1536])
        S.tt("dve", tab[:], dk[:, 0, :], dk[:, 1, :], ALU.mult, reads=[dkb], writes=[b])
        S.act(tab[:], tab[:], AF.Exp, reads=[b], writes=[b])
        ce, ceb = load_const(S, nc, esi, "ce_sb", d_cexp, [128, 2, 3])
        S.act(cfb[:], ce[:], AF.Exp, scale=128.0, reads=[ceb], writes=[b])
        if need_q:
            dm, dmb = load_const(S, nc, esi, "dm_sb", d_dm, [128, 4, 128])
            lgc, lgcb = load_const(S, nc, esi, "lgc_sb", d_lgc, [128, 2, 6])
            tmp = _sb(nc, esi, "mctmp", [128, 2, 128], F32)
            tb = Buf()
            for h in range(6):
                S.act(tmp[:, 0, :], dm[:, 0, :], AF.Exp, scale=lgc[:, 0, h:h + 1], reads=[dmb, lgcb], writes=[tb])
                S.act(tmp[:, 1, :], dm[:, 2, :], AF.Exp, scale=lgc[:, 1, h:h + 1], reads=[dmb, lgcb], writes=[tb])
                S.tt("dve", tmp[:, 0, :], tmp[:, 0, :], dm[:, 1, :], ALU.mult, reads=[tb, dmb], writes=[tb])
                S.tt("dve", tmp[:, 1, :], tmp[:, 1, :], dm[:, 3, :], ALU.mult, reads=[tb, dmb], writes=[tb])
                S.tt("dve", mc[:, h, :], tmp[:, 0, :], tmp[:, 1, :], ALU.add, reads=[tb], writes=[b])
        S.barrier()
    sl = lambda t, c: tab[:, (t * 3 + c) * 128:(t * 3 + c + 1) * 128]
    dec = dict(ktab_f=[sl(0, c) for c in range(3)], ktab_b=[sl(1, c) for c in range(3)],
               qtab_f=[sl(2, c) for c in range(3)], qtab_b=[sl(3, c) for c in range(3)], cfb=cfb, b=b)
    if need_q:
        dec["mcomb"] = mc
    return dec


def chain_rings(nc, es, deep=False):
    return dict(
        tps=Ring(nc, es, "c_tps", [128, 128], BF16, 2 if deep else 1, psum=True),
        dps=Ring(nc, es, "c_dps", [128, 256], F32, 3 if deep else 2, psum=True),
        kd=Ring(nc, es, "c_kd", [128, 128], BF16, 6),
        sf32=Ring(nc, es, "c_sf", [128, 128], F32, 3),
        sb32=Ring(nc, es, "c_sb", [128, 128], F32, 3),
        d2=(_sb(nc, es, "c_d2", [128, 16, 256], F32), [Buf() for _ in range(16)]),
    )


def build_r(stop=99):
    nc = bass.Bass("TRN2", target_bir_lowering=False)
    d_h = nc.dram_tensor("hT", [128, 8, NT], F32, kind="ExternalInput")
    d_m6 = nc.dram_tensor("mod6", [128, 2, 6, 8], F32, kind="ExternalInput")
    d_g = nc.dram_tensor("g12", [128, 2, 8], F32, kind="ExternalInput")
    d_wk = nc.dram_tensor("wk", [3, 128, 8, 128], F32, kind="ExternalInput")
    d_wv = nc.dram_tensor("wv", [128, 8, 384], F32, kind="ExternalInput")
    d_rope = nc.dram_tensor("rope", [128, 2 * NT + 128], BF16, kind="ExternalInput")
    d_dk = nc.dram_tensor("dk", [128, 2, 1536], F32, kind="ExternalInput")
    d_ce = nc.dram_tensor("cexp", [128, 2, 3], F32, kind="ExternalInput")
    d_cst = nc.dram_tensor("cst", [128, 513], F32, kind="ExternalInput")
    d_out = nc.dram_tensor("sl", [128, 4, 3, 128], F32, kind="ExternalOutput")
    with ExitStack() as es0:
        S = Sched(nc, es0)
        cst = emit_consts(S, nc, es0, d_cst)
        m6, m6b = load_const(S, nc, es0, "m6_sb", d_m6, [128, 2, 6, 8])
        g12, g12b = load_const(S, nc, es0, "g12_sb", d_g, [128, 2, 8])
        sx = emit_mod_scalars(S, nc, es0, (m6[:, 0], m6b), (g12[:, 0, :], g12b), (g12[:, 1, :], g12b), "sx")
        sc = emit_mod_scalars(S, nc, es0, (m6[:, 1], m6b), (g12[:, 0, :], g12b), (g12[:, 1, :], g12b), "sc")
        rope, ropeb = load_const(S, nc, es0, "rope_sb", d_rope, [128, 2 * NT + 128], BF16)
        crot = (rope[:, 0:NT], rope[:, NT:2 * NT], rope[:, 2 * NT:2 * NT + 128], ropeb)
        dec = emit_decays(S, nc, es0, d_dk, None, None, d_ce, False)
        rk = _sb(nc, es0, "rk", [128, 3, NT], BF16)
        rkb = Buf()
        vret = _sb(nc, es0, "vret", [128, 18, 384], BF16)
        vretb = Buf()
        with ExitStack() as es1:
            ax = _sb(nc, es1, "ax", [128, 8, NT], BF16)
            axb = Buf()
            tiles = [(a, b, False) for (a, b) in _split(0, NL, 256)] + [(NL, NT, True)]
            emit_norm_phase(S, nc, cst, d_h, tiles, ax, lambda c0: axb, sx, sc, "1")
            if stop <= 1:
                S.emit()
                return nc
            with ExitStack() as es:
                wr = Ring(nc, es, "wfm", [128, 8, 128], BF16, 3)
                accr = Ring(nc, es, "acc", [128, 512], F32, 3, psum=True)
                rr = dict(raw=Ring(nc, es, "r_raw", [128, 512], BF16, 2), t1=Ring(nc, es, "r_t1", [128, 512], F32, 2),
                          t2=Ring(nc, es, "r_t2", [128, 512], F32, 2), pp=Ring(nc, es, "r_pp", [128, 512], F32, 2, psum=True))
                wks = []
                for c in range(3):
                    w_, wb = wr.next()
                    S.dma("pool", w_[:], d_wk[c], writes=[wb])
                    wks.append((w_, wb))
                wv, wvb = load_const(S, nc, es, "wv_sb", d_wv, [128, 8, 384], BF16, q="pool")
                for c in range(3):
                    w_, wb = wks[c]
                    for (c0, c1) in _split(0, NL, 512) + [(NL, NT)]:
                        w = c1 - c0
                        ps, psb = accr.next()
                        for k in range(8):
                            S.mm(ps[:, 0:w], w_[:, k, :], ax[:, k, c0:c1], start=(k == 0), stop=(k == 7),
                                 reads=[wb, axb], writes=[psb])
                        emit_rotary(S, crot, ps, psb, w, c0, 0.125, rk[:, c, c0:c1], rkb, rr)
                if stop <= 2:
                    S.emit()
                    return nc
                tmr = Ring(nc, es, "tm", [128, 512], F32, 2, psum=True)
                for blk in range(18):
                    c0 = blk * 128
                    ps, psb = tmr.next()
                    for k in range(8):
                        S.mm(ps[:, 0:384], ax[:, k, c0:c0 + 128], wv[:, k, :], start=(k == 0), stop=(k == 7),
                             reads=[wvb, axb], writes=[psb])
                    S.cp("act" if blk % 2 else "dve", vret[:, blk, :], ps[:, 0:384], reads=[psb], writes=[vretb])
                S.barrier()
        if stop <= 3:
            S.emit()
            return nc
        with ExitStack() as es:
            rings = chain_rings(nc, es, deep=True)
            Sfb = _sb(nc, es, "Sfb", [128, 17, 128], BF16)
            Sbb = _sb(nc, es, "Sbb", [128, 17, 128], BF16)
            sbuf_b = Buf()
            S.memset("pool", Sfb[:], 0.0, writes=[sbuf_b])
            S.memset("pool", Sbb[:], 0.0, writes=[sbuf_b])
            outt = _sb(nc, es, "outt", [128, 4, 3, 128], F32)
            outb = Buf()
            S.memset("dve", outt[:], 0.0, writes=[outb])
            for c in range(3):
                for (si, blocks) in ((0, list(range(16))), (1, [16, 17])):
                    kcols = [b * 128 for b in blocks]
                    (sf, sfb_), (sb_, sbb_) = emit_ret_chain(
                        S, nc, rk, rkb, vret, vretb, 0, c, blocks, kcols, blocks, None, None, dec, cst,
                        Sfb, Sbb, sbuf_b, sbuf_b, rings, lambda i: i)
                    for (p0, p1) in ((0, 64), (64, 128)):
                        S.cp("dve", outt[p0:p1, 2 * si, c, p0:p1], sf[p0:p1, p0:p1], reads=[sfb_], writes=[outb])
                        S.cp("dve", outt[p0:p1, 2 * si + 1, c, p0:p1], sb_[p0:p1, p0:p1], reads=[sbb_], writes=[outb])
            S.dma("sp", d_out[:], outt[:], reads=[outb])
        S.emit()
    return nc


def rope_tables(core):
    inv = (10000.0 ** (-np.arange(16, dtype=np.float32) / 16)).astype(np.float32)
    t = np.arange(NL) + core * NL
    row = (t // GW).astype(np.float32)
    col = (t % GW).astype(np.float32)
    p = np.arange(128)
    f = p % 64
    part = f // 32
    j = f % 16
    u = (f % 32) // 16
    pos = np.where(part[:, None] == 0, row[None, :], col[None, :]).astype(np.float32)
    ang = (pos * inv[j][:, None]).astype(np.float32)
    C = np.ones((128, NT), np.float32)
    Sg = np.zeros((128, NT), np.float32)
    C[:, :NL] = np.cos(ang)
    Sg[:, :NL] = np.where(u[:, None] == 0, -np.sin(ang), np.sin(ang))
    P = np.zeros((128, 128), np.float32)
    partner = np.where((f % 32) < 16, p + 16, p - 16)
    P[partner, p] = 1.0
    return np.concatenate([C, Sg, P], 1).astype(NPBF)


def decay_inputs(lg_f, lg_b):
    dk = np.zeros((128, 2, 4, 3, 128), np.float32)
    j = np.arange(128, dtype=np.float32)
    for c in range(3):
        for col in range(128):
            h = 2 * c + col // 64
            dk[:, 0, 0, c, col] = 127 - j
            dk[:, 1, 0, c, col] = lg_f[h]
            dk[:, 0, 1, c, col] = j
            dk[:, 1, 1, c, col] = lg_b[h]
        for p in range(128):
            h = 2 * c + p // 64
            dk[p, 0, 2, c, :] = j + 1
            dk[p, 1, 2, c, :] = lg_f[h]
            dk[p, 0, 3, c, :] = 128 - j
            dk[p, 1, 3, c, :] = lg_b[h]
    cexp = np.zeros((128, 2, 3), np.float32)
    for c in range(3):
        for half in range(2):
            cexp[half * 64:(half + 1) * 64, 0, c] = lg_f[2 * c + half]
            cexp[half * 64:(half + 1) * 64, 1, c] = lg_b[2 * c + half]
    lgc = np.zeros((128, 2, 6), np.float32)
    lgc[:, 0, :] = lg_f[None, :]
    lgc[:, 1, :] = lg_b[None, :]
    return dk.reshape(128, 2, 1536), cexp, lgc


def mod6_layout(mv_l):
    return np.ascontiguousarray(mv_l.reshape(2, 6, 8, 128).transpose(3, 0, 1, 2))


def prep_r_weights(w_in_l):
    wk = w_in_l[:, 1792:2176]
    wk = np.ascontiguousarray(wk.reshape(8, 128, 3, 128).transpose(2, 1, 0, 3))
    wv = np.ascontiguousarray(w_in_l[:, 2176:2560].reshape(8, 128, 384).transpose(1, 0, 2))
    return wk, wv


def build_m1(stop=99):
    nc = bass.Bass("TRN2", target_bir_lowering=False)
    d_h = nc.dram_tensor("hT", [128, 8, NT1], F32, kind="ExternalInput")
    d_m6 = nc.dram_tensor("mod6", [128, 2, 6, 8], F32, kind="ExternalInput")
    d_g = nc.dram_tensor("g12", [128, 2, 8], F32, kind="ExternalInput")
    d_wf = nc.dram_tensor("wfm", [17, 128, 8, 128], F32, kind="ExternalInput")
    d_wt = nc.dram_tensor("wtm", [128, 8, 768], F32, kind="ExternalInput")
    d_rope = nc.dram_tensor("rope", [128, 2 * NT + 128], BF16, kind="ExternalInput")
    d_gt = nc.dram_tensor("gtab", [128, 6, 640], BF16, kind="ExternalInput")
    d_st = nc.dram_tensor("stab", [4, 128, 6, 768], BF16, kind="ExternalInput")
    d_wbd = nc.dram_tensor("wbd", [128, 2, 128], F32, kind="ExternalInput")
    d_pv = nc.dram_tensor("pvec", [128, 2, 3], F32, kind="ExternalInput")
    d_ic = nc.dram_tensor("invcnt", [128, 2, 4, 8], F32, kind="ExternalInput")
    d_dk = nc.dram_tensor("dk", [128, 2, 1536], F32, kind="ExternalInput")
    d_dm = nc.dram_tensor("dm", [128, 4, 128], F32, kind="ExternalInput")
    d_lgc = nc.dram_tensor("lgc", [128, 2, 6], F32, kind="ExternalInput")
    d_ce = nc.dram_tensor("cexp", [128, 2, 3], F32, kind="ExternalInput")
    d_cf = nc.dram_tensor("coef", [128, 3, 54], F32, kind="ExternalInput")
    d_sa = nc.dram_tensor("sall", [128, 2, 3, 9, 128], F32, kind="ExternalInput")
    d_gn = nc.dram_tensor("gng", [128, 3], F32, kind="ExternalInput")
    d_wo = nc.dram_tensor("wout", [128, 8, 1024], F32, kind="ExternalInput")
    d_cst = nc.dram_tensor("cst", [128, 513], F32, kind="ExternalInput")
    d_hm = nc.dram_tensor("hmid", [128, 8, NT], F32, kind="ExternalOutput")
    d_h2 = nc.dram_tensor("h2", [128, 8, NT], BF16, kind="ExternalOutput")
    d_yx = nc.dram_tensor("yxs", [128, 8, NT], BF16)
    d_pp = nc.dram_tensor("pps", [128, 2, NPP], F32)
    dbg = {}
    with ExitStack() as es0:
        S = Sched(nc, es0)
        cst = emit_consts(S, nc, es0, d_cst)
        m6, m6b = load_const(S, nc, es0, "m6_sb", d_m6, [128, 2, 6, 8])
        g12, g12b = load_const(S, nc, es0, "g12_sb", d_g, [128, 2, 8])
        sx = emit_mod_scalars(S, nc, es0, (m6[:, 0], m6b), (g12[:, 0, :], g12b), (g12[:, 1, :], g12b), "sx")
        sc = emit_mod_scalars(S, nc, es0, (m6[:, 1], m6b), (g12[:, 0, :], g12b), (g12[:, 1, :], g12b), "sc")
        yxb = Buf()
        with ExitStack() as esP:
            naq = _sb(nc, esP, "naq", [128, 3, NT], BF16)
            nak = _sb(nc, esP, "nak", [128, 3, NT1], BF16)
            vna = _sb(nc, esP, "vna", [128, 22, 6, 65], BF16)
            rq = _sb(nc, esP, "rq", [128, 3, NT], BF16)
            rk = _sb(nc, esP, "rk", [128, 3, NT], BF16)
            gs = _sb(nc, esP, "gs", [128, 3, NT], BF16)
            vret = _sb(nc, esP, "vret", [128, 18, 384], BF16)
            naqb, nakb, vnab, rqb, rkb, gsb, vretb, ppb = (Buf() for _ in range(8))
            S.memset("pool", vna[:], 1.0, writes=[vnab])
            with ExitStack() as esA:
                ax = _sb(nc, esA, "ax", [128, 8, NT1], BF16)
                axb = Buf()
                tiles = [(a, b, False) for (a, b) in _split(0, NS, 256)] + [(NS, NT1, True)]
                emit_norm_phase(S, nc, cst, d_h, tiles, ax, lambda c0: axb, sx, sc, "1")
                with ExitStack() as es:
                    rope, ropeb = load_const(S, nc, es, "rope_sb", d_rope, [128, 2 * NT + 128], BF16)
                    crot = (rope[:, 0:NT], rope[:, NT:2 * NT], rope[:, 2 * NT:2 * NT + 128], ropeb)
                    wr = Ring(nc, es, "wfm_sb", [128, 8, 128], BF16, 4)
                    accr = Ring(nc, es, "acc", [128, 512], F32, 3, psum=True)
                    rr = dict(raw=Ring(nc, es, "r_raw", [128, 512], BF16, 2), t1=Ring(nc, es, "r_t1", [128, 512], F32, 2),
                              t2=Ring(nc, es, "r_t2", [128, 512], F32, 2),
                              pp=Ring(nc, es, "r_pp", [128, 512], F32, 2, psum=True))
                    ptr = Ring(nc, es, "ptmp", [128, 512], F32, 2)
                    zt = _sb(nc, es, "zt", [128, 2, 8], F32)
                    ztb = Buf()
                    S.memset("dve", zt[:], 0.0, writes=[ztb])
                    S.dma("sp", d_pp[:, :, 2064:2072], zt[:], reads=[ztb], writes=[ppb])
                    S.dma("sp", d_pp[:, :, 2328:2336], zt[:], reads=[ztb], writes=[ppb])
                    wv, wvb = load_const(S, nc, es, "wv_sb", d_wt, [128, 8, 768], BF16, q="pool")

                    def issue_wf(j):
                        w_, wb = wr.next()
                        S.dma("pool", w_[:], d_wf[j], writes=[wb])
                        return (w_, wb)
                    pend_wf = [issue_wf(j) for j in range(2)]
                    for j in range(17):
                        if j + 2 < 17:
                            pend_wf.append(issue_wf(j + 2))
                        w_, wb = pend_wf.pop(0)
                        kind = ("pool", "pool", "q", "q", "q", "k", "k", "k", "rq", "rq", "rq", "rk", "rk", "rk",
                                "g", "g", "g")[j]
                        tl = FM_TILES_POOL if kind == "pool" else (FM_TILES_ALL if kind == "k" else FM_TILES_LOC)
                        for (c0, c1) in tl:
                            w = c1 - c0
                            ps, psb = accr.next()
                            for k in range(8):
                                S.mm(ps[:, 0:w], w_[:, k, :], ax[:, k, c0:c1], start=(k == 0), stop=(k == 7),
                                     reads=[wb, axb], writes=[psb])
                            lc = loc_col(c0)
                            if kind == "pool":
                                t, tb = ptr.next()
                                S.cp("act", t[:, 0:w], ps[:, 0:w], reads=[psb], writes=[tb])
                                dc = (c0 - 248) if c0 < NS else 2072 + (c0 - NS)
                                S.dma("sp", d_pp[:, j, dc:dc + w], t[:, 0:w], reads=[tb], writes=[ppb])
                            elif kind == "q":
                                S.act(naq[:, j - 2, lc:lc + w], ps[:, 0:w], AF.Copy, scale=0.125, reads=[psb], writes=[naqb])
                            elif kind == "k":
                                S.cp("dve", nak[:, j - 5, c0:c1], ps[:, 0:w], reads=[psb], writes=[nakb])
                            elif kind == "rq":
                                emit_rotary(S, crot, ps, psb, w, lc, 1.0, rq[:, j - 8, lc:lc + w], rqb, rr)
                            elif kind == "rk":
                                emit_rotary(S, crot, ps, psb, w, lc, 0.125, rk[:, j - 11, lc:lc + w], rkb, rr)
                            else:
                                S.act(gs[:, j - 14, lc:lc + w], ps[:, 0:w], AF.Silu, reads=[psb], writes=[gsb])
                    tmr = Ring(nc, es, "tm", [128, 1024], F32, 1, psum=True)
                    for blk in range(22):
                        c0 = blk * 128 if blk < 20 else NS + (blk - 20) * 128
                        ri = blk - 2 if 2 <= blk < 18 else (16 + blk - 20 if blk >= 20 else None)
                        ps, psb = tmr.next()
                        for k in range(8):
                            S.mm(ps[:, 0:384], ax[:, k, c0:c0 + 128], wv[:, k, 0:384], start=(k == 0), stop=(k == 7),
                                 reads=[wvb, axb], writes=[psb])
                        if ri is not None:
                            for k in range(8):
                                S.mm(ps[:, 512:896], ax[:, k, c0:c0 + 128], wv[:, k, 384:768], start=(k == 0),
                                     stop=(k == 7), reads=[wvb, axb], writes=[psb])
                        for h in range(6):
                            S.cp("dve" if h % 2 else "act", vna[:, blk, h, 0:64], ps[:, h * 64:(h + 1) * 64],
                                 reads=[psb], writes=[vnab])
                        if ri is not None:
                            S.cp("act", vret[:, ri, :], ps[:, 512:896], reads=[psb], writes=[vretb])
                    S.barrier()
            if stop <= 1:
                for (nm, t) in (("naq", naq), ("nak", nak), ("rq", rq), ("rk", rk), ("gs", gs)):
                    dd = nc.dram_tensor("dbg_" + nm, list(t.shape), BF16, kind="ExternalOutput")
                    S.dma("sp", dd[:], t[:])
                dd = nc.dram_tensor("dbg_vna", [128, 22, 6, 65], BF16, kind="ExternalOutput")
                S.dma("sp", dd[:], vna[:])
                dd = nc.dram_tensor("dbg_vret", [128, 18, 384], BF16, kind="ExternalOutput")
                S.dma("sp", dd[:], vret[:])
                dd = nc.dram_tensor("dbg_pp", [128, 2, NPP], F32, kind="ExternalOutput")
                S.dma("sp", dd[:], d_pp[:])
                S.emit()
                return nc
            with ExitStack() as es:
                gt, gtb = load_const(S, nc, es, "gt_sb", d_gt, [128, 6, 640], BF16)
                str_ = Ring(nc, es, "st_sb", [128, 6, 768], BF16, 1)
                psS = Ring(nc, es, "psS", [128, 1024], F32, 3, psum=True)
                psO = Ring(nc, es, "psO", [128, 6, 65], F32, 1, psum=True)
                psT = Ring(nc, es, "psT", [128, 3, 128], BF16, 1, psum=True)
                pr = Ring(nc, es, "Pt", [128, 1024], BF16, 4)
                otr = Ring(nc, es, "otok", [128, 384], BF16, 2)
                recr = Ring(nc, es, "rec", [128, 6], F32, 2)
                yr = Ring(nc, es, "yxn", [128, 3, 128], BF16, 2)
                spec = {p: (i, b0) for i, (p, b0) in enumerate(SPECIAL)}

                def na_pv(st_):
                    (po, pob, h, pt, ptb, kb) = st_
                    nb = len(kb)
                    for j, (kc0, vb) in enumerate(kb):
                        S.mm(po[:, h, :], pt[:, j * 128:(j + 1) * 128], vna[:, vb, h, :], start=(j == 0),
                             stop=(j == nb - 1), reads=[ptb, vnab], writes=[pob])

                def na_epilogue(po, pob, q0):
                    rec, recb = recr.next()
                    S.recip(rec[:], po[:, :, 64], reads=[pob], writes=[recb])
                    ot, otb = otr.next()
                    for h in range(6):
                        S.ts("dve", ot[:, h * 64:(h + 1) * 64], po[:, h, 0:64], rec[:, h:h + 1], ALU.mult,
                             reads=[pob, recb], writes=[otb])
                    pT, pTb = psT.next()
                    for j in range(3):
                        S.tr(pT[:, j, :], ot[:, j * 128:(j + 1) * 128], cst["identb"][:],
                             reads=[otb, cst["identb_b"]], writes=[pTb])
                    yt, ytb = yr.next()
                    S.cp("act", yt[:], pT[:], reads=[pTb], writes=[ytb])
                    S.dma("sp", d_yx[:, 2:5, q0:q0 + 128], yt[:], reads=[ytb], writes=[yxb])

                pend = None
                for pi in range(18):
                    tab = None
                    if pi < 16:
                        q0 = 128 * pi
                        if pi in spec:
                            si, b0 = spec[pi]
                            st, stb = str_.next()
                            S.dma("sp", st[:], d_st[si], writes=[stb])
                            tab = (st, stb)
                            kb = [(128 * (b0 + s), b0 + s) for s in range(6)]
                        else:
                            tab = (gt, gtb)
                            kb = [(128 * (pi + s), pi + s) for s in range(5)]
                    else:
                        q0 = NL + 128 * (pi - 16)
                        kb = []
                    nbl = len(kb)
                    kb = kb + [(NS, 20), (NS + 128, 21)]
                    nb = len(kb)
                    po, pob = psO.next()
                    for h in range(6):
                        c = h // 2
                        p0 = 64 * (h % 2)
                        sp_, spb = psS.next()
                        for j, (kc0, vb) in enumerate(kb):
                            S.mm(sp_[:, j * 128:(j + 1) * 128], nak[p0:p0 + 64, c, kc0:kc0 + 128],
                                 naq[p0:p0 + 64, c, q0:q0 + 128], reads=[nakb, naqb], writes=[spb])
                        if nbl:
                            S.tt("dve", sp_[:, 0:nbl * 128], sp_[:, 0:nbl * 128], tab[0][:, h, 0:nbl * 128], ALU.add,
                                 reads=[spb, tab[1]], writes=[spb])
                        pt, ptb = pr.next()
                        n1 = min(nb, 4) * 128
                        S.act(pt[:, 0:n1], sp_[:, 0:n1], AF.Exp, reads=[spb], writes=[ptb])
                        if nb > 4:
                            S.act(pt[:, 512:nb * 128], sp_[:, 512:nb * 128], AF.Exp, reads=[spb], writes=[ptb])
                        if pend is not None:
                            na_pv(pend[0])
                            if pend[1] is not None:
                                na_epilogue(*pend[1])
                        pend = ((po, pob, h, pt, ptb, kb), (po, pob, q0) if h == 5 else None)
                na_pv(pend[0])
                na_epilogue(*pend[1])
                S.barrier()
            if stop <= 2:
                dd = nc.dram_tensor("dbg_yx", [128, 8, NT], BF16, kind="ExternalOutput")
                S.dma("sp", dd[:, 2:5, :], d_yx[:, 2:5, :])
                S.emit()
                return nc
            with ExitStack() as es:
                dec = emit_decays(S, nc, es, d_dk, d_dm, d_lgc, d_ce, True)
                decb = dec["b"]
                mcomb = dec["mcomb"]
                gng, gngb = load_const(S, nc, es, "gng_sb", d_gn, [128, 3])
                cf, cfb_ = load_const(S, nc, es, "coef_sb", d_cf, [128, 3, 54])
                coef = _sb(nc, es, "coef_e", [128, 54], F32)
                coefb = Buf()
                S.tt("dve", coef[:], cf[:, 0, :], cf[:, 1, :], ALU.mult, reads=[cfb_], writes=[coefb])
                S.act(coef[:], coef[:], AF.Exp, reads=[coefb], writes=[coefb])
                S.tt("dve", coef[:], coef[:], cf[:, 2, :], ALU.mult, reads=[coefb, cfb_], writes=[coefb])
                rings = chain_rings(nc, es)
                sar = Ring(nc, es, "sall_sb", [128, 9, 128], F32, 2)
                sinr = Ring(nc, es, "sin", [128, 128], F32, 4)
                Sfbs = [_sb(nc, es, f"Sfb{c}", [128, 20, 128], BF16) for c in range(3)]
                Sbbs = [_sb(nc, es, f"Sbb{c}", [128, 20, 128], BF16) for c in range(3)]
                stbs = [Buf() for _ in range(3)]
                psA = Ring(nc, es, "psA", [128, 128], F32, 2, psum=True)
                psY = Ring(nc, es, "psY", [128, 128], F32, 1, psum=True)
                psG = Ring(nc, es, "psG", [128, 512], F32, 2, psum=True)
                atr = Ring(nc, es, "at", [128, 128], BF16, 6)
                qdr = Ring(nc, es, "qd", [128, 128], BF16, 6)
                ytr = Ring(nc, es, "ytile", [128, 512], F32, 3)
                gnt = Ring(nc, es, "gnt", [128, 512], F32, 8)
                gno = Ring(nc, es, "gno", [128, 512], BF16, 2)
                segs = ((list(range(16)), 0), ([16, 17], 17))
                for c in range(3):
                    Sfb, Sbb, stb_ = Sfbs[c], Sbbs[c], stbs[c]
                    S.memset("pool", Sfb[:], 0.0, writes=[stb_])
                    S.memset("pool", Sbb[:], 0.0, writes=[stb_])
                    sins = []
                    for d in range(2):
                        sa, sab = sar.next()
                        S.dma("sp", sa[:], d_sa[:, d, c], writes=[sab])
                        si_, sib = sinr.next()
                        for src in range(9):
                            col = (d * 3 + c) * 9 + src
                            if src == 0:
                                S.ts("dve", si_[:], sa[:, src, :], coef[:, col:col + 1], ALU.mult,
                                     reads=[sab, coefb], writes=[sib])
                            else:
                                S.stt("dve", si_[:], sa[:, src, :], coef[:, col:col + 1], si_[:], ALU.mult, ALU.add,
                                      reads=[sab, coefb, sib], writes=[sib])
                        sins.append((si_, sib))
                    for (blocks, s0) in segs:
                        kcols = [b * 128 for b in blocks]
                        fi, bi = (sins[0], sins[1]) if s0 == 0 else (None, None)
                        emit_ret_chain(S, nc, rk, rkb, vret, vretb, 0, c, blocks, kcols, blocks,
                                       None if fi is None else (fi[0][:], fi[1]), None if bi is None else (bi[0][:], bi[1]),
                                       dec, cst, Sfb, Sbb, stb_, stb_, rings, lambda i, s0=s0: s0 + i)
                for (blocks, s0) in segs:
                    nbk = len(blocks)
                    for t0 in range(0, nbk, 4):
                        tb_ = blocks[t0:t0 + 4]
                        wt = 128 * len(tb_)
                        yts = [ytr.next() for _ in range(3)]
                        for ii, blk in enumerate(tb_):
                            n = t0 + ii
                            k0 = blk * 128
                            for c in range(3):
                                Sfb, Sbb, stb_ = Sfbs[c], Sbbs[c], stbs[c]
                                yt, ytb = yts[c]
                                ats = []
                                for hh in range(2):
                                    p0 = 64 * hh
                                    pa_, pab = psA.next()
                                    S.mm(pa_[:], rk[p0:p0 + 64, c, k0:k0 + 128], rq[p0:p0 + 64, c, k0:k0 + 128],
                                         reads=[rkb, rqb], writes=[pab])
                                    at, atb = atr.next()
                                    S.tt("dve", at[:], pa_[:], mcomb[:, 2 * c + hh, :], ALU.mult, reads=[pab, decb],
                                         writes=[atb])
                                    ats.append((at, atb))
                                qf, qfb = qdr.next()
                                S.tt("pool", qf[:], rq[:, c, k0:k0 + 128], dec["qtab_f"][c], ALU.mult, reads=[rqb, decb],
                                     writes=[qfb])
                                qb_, qbb = qdr.next()
                                S.tt("pool", qb_[:], rq[:, c, k0:k0 + 128], dec["qtab_b"][c], ALU.mult, reads=[rqb, decb],
                                     writes=[qbb])
                                py, pyb = psY.next()
                                S.mm(py[:], Sfb[:, s0 + n, :], qf[:], start=True, stop=False, reads=[stb_, qfb], writes=[pyb])
                                S.mm(py[:], Sbb[:, s0 + n + 1, :], qb_[:], start=False, stop=False, reads=[stb_, qbb],
                                     writes=[pyb])
                                S.mm(py[0:64, :], vret[:, blk, (2 * c) * 64:(2 * c + 1) * 64], ats[0][0][:], start=False,
                                     stop=True, reads=[vretb, ats[0][1]], writes=[pyb])
                                S.mm(py[64:128, :], vret[:, blk, (2 * c + 1) * 64:(2 * c + 2) * 64], ats[1][0][:],
                                     start=False, stop=True, reads=[vretb, ats[1][1]], writes=[pyb], tp=(0, 64))
                                S.cp("act", yt[:, ii * 128:(ii + 1) * 128], py[:], reads=[pyb], writes=[ytb])
                        col0 = blocks[t0] * 128
                        for c in range(3):
                            yt, ytb = yts[c]
                            gq = [gnt.next() for _ in range(5)]
                            (ysq, ysqb), (mean, meanb), (m2, m2b), (var, varb), (yc, ycb) = gq
                            S.act(ysq[:, 0:wt], yt[:, 0:wt], AF.Square, reads=[ytb], writes=[ysqb])
                            p1, p1b = psG.next()
                            S.mm(p1[:, 0:wt], cst["bd64"][:], yt[:, 0:wt], reads=[ytb, cst["b"]], writes=[p1b])
                            S.cp("act", mean[:, 0:wt], p1[:, 0:wt], reads=[p1b], writes=[meanb])
                            p2, p2b = psG.next()
                            S.mm(p2[:, 0:wt], cst["bd64"][:], ysq[:, 0:wt], reads=[ysqb, cst["b"]], writes=[p2b])
                            S.tt("pool", m2[:, 0:wt], mean[:, 0:wt], mean[:, 0:wt], ALU.mult, reads=[meanb], writes=[m2b])
                            S.tt("dve", var[:, 0:wt], p2[:, 0:wt], m2[:, 0:wt], ALU.subtract, reads=[p2b, m2b], writes=[varb])
                            S.act(var[:, 0:wt], var[:, 0:wt], AF.Sqrt, bias=cst["eps"][:], scale=1.0, reads=[varb, cst["b"]],
                                  writes=[varb])
                            S.recip(var[:, 0:wt], var[:, 0:wt], reads=[varb], writes=[varb])
                            S.tt("pool", yc[:, 0:wt], yt[:, 0:wt], mean[:, 0:wt], ALU.subtract, reads=[ytb, meanb], writes=[ycb])
                            S.tt("dve", yc[:, 0:wt], yc[:, 0:wt], var[:, 0:wt], ALU.mult, reads=[ycb, varb], writes=[ycb])
                            go, gob = gno.next()
                            S.ts("dve", yc[:, 0:wt], yc[:, 0:wt], gng[:, c:c + 1], ALU.mult, reads=[ycb, gngb], writes=[ycb])
                            S.tt("pool", go[:, 0:wt], yc[:, 0:wt], gs[:, c, col0:col0 + wt], ALU.mult,
                                 reads=[ycb, gsb], writes=[gob])
                            S.dma("sp", d_yx[:, 5 + c, col0:col0 + wt], go[:, 0:wt], reads=[gob], writes=[yxb])
                S.barrier()
            with ExitStack() as es:
                pv, pvb = load_const(S, nc, es, "pv_sb", d_pv, [128, 2, 3])
                ic, icb = load_const(S, nc, es, "ic_sb", d_ic, [128, 2, 4, 8])
                wbd, wbdb = load_const(S, nc, es, "wbd_sb", d_wbd, [128, 2, 128], BF16, q="pool")
                X = _sb(nc, es, "pX", [128, NPP], F32)
                A_ = _sb(nc, es, "pA", [128, NPP], F32)
                B_ = _sb(nc, es, "pB", [128, NPP], F32)
                md = _sb(nc, es, "pmd", [128, NT], BF16)
                t8 = _sb(nc, es, "pt8", [128, 8], F32)
                Xb, Ab, Bb, mdb, t8b = (Buf() for _ in range(5))
                psP = Ring(nc, es, "psP", [128, 512], F32, 2, psum=True)
                ypr = Ring(nc, es, "ypool", [128, 512], BF16, 2)
                N = NPP
                for g2 in range(2):
                    S.dma("sp", X[:], d_pp[:, g2, :], reads=[ppb], writes=[Xb])
                    S.ts("dve", X[:, 0:8], X[:, 0:8], pv[:, 0, 2:3], ALU.mult, reads=[Xb, pvb], writes=[Xb])
                    S.ts("dve", X[:, 2056:2064], X[:, 2056:2064], pv[:, 1, 2:3], ALU.mult, reads=[Xb, pvb], writes=[Xb])
                    S.tt("dve", A_[:, 1:N], X[:, 0:N - 1], X[:, 1:N], ALU.add, reads=[Xb], writes=[Ab])
                    S.tt("pool", B_[:, 2:N - 1], A_[:, 1:N - 2], A_[:, 3:N], ALU.add, reads=[Ab], writes=[Bb])
                    if g2 == 1:
                        S.tt("dve", A_[:, 4:N - 3], B_[:, 2:N - 5], B_[:, 6:N - 1], ALU.add, reads=[Bb], writes=[Ab])
                        S.tt("pool", B_[:, 8:N - 7], A_[:, 4:N - 11], A_[:, 12:N - 3], ALU.add, reads=[Ab], writes=[Bb])
                    for half, src, srcb in ((0, A_, Ab), (1, B_, Bb)):
                        p0, p1 = 64 * half, 64 * half + 64
                        for (xc, mc_, wd_) in ((8, 0, NL), (2072, NL, NCTX)):
                            S.stt("dve", md[p0:p1, mc_:mc_ + wd_], src[p0:p1, xc:xc + wd_], pv[p0:p1, g2, 1:2],
                                  X[p0:p1, xc:xc + wd_], ALU.mult, ALU.subtract, reads=[srcb, Xb, pvb], writes=[mdb])
                        for r, (xc, mc_) in enumerate(((8, 0), (2048, 2040), (2072, 2048), (2320, 2296))):
                            S.tt("dve", t8[p0:p1, :], src[p0:p1, xc:xc + 8], ic[p0:p1, g2, r, :], ALU.mult,
                                 reads=[srcb, icb], writes=[t8b])
                            S.tt("dve", md[p0:p1, mc_:mc_ + 8], t8[p0:p1, :], X[p0:p1, xc:xc + 8], ALU.subtract,
                                 reads=[t8b, Xb, mdb], writes=[mdb])
                    for (c0, c1) in _split(0, NL, 512) + [(NL, NT)]:
                        w = c1 - c0
                        ps, psb = psP.next()
                        S.mm(ps[:, 0:w], wbd[:, g2, :], md[:, c0:c1], reads=[wbdb, mdb], writes=[psb])
                        yp, ypb = ypr.next()
                        S.act(yp[:, 0:w], ps[:, 0:w], AF.Copy, scale=pv[:, g2, 0:1], reads=[psb, pvb], writes=[ypb])
                        S.dma("sp", d_yx[:, g2, c0:c1], yp[:, 0:w], reads=[ypb], writes=[yxb])
                S.barrier()
        if stop <= 3:
            dd = nc.dram_tensor("dbg_yx", [128, 8, NT], BF16, kind="ExternalOutput")
            S.dma("sp", dd[:], d_yx[:])
            S.emit()
            return nc
        with ExitStack() as es:
            wo, wob = load_const(S, nc, es, "wo_sb", d_wo, [128, 8, 1024], BF16, q="pool")
            yxr = Ring(nc, es, "yxt", [128, 8, 256], BF16, 2)
            htr = Ring(nc, es, "o_ht", [128, 8, 256], F32, 2)
            hmr = Ring(nc, es, "o_hm", [128, 8, 256], F32, 2)
            h2r = Ring(nc, es, "o_h2", [128, 8, 256], BF16, 2)
            accr = Ring(nc, es, "o_acc", [128, 256], F32, 3, psum=True)
            sqr = Ring(nc, es, "o_sq", [128, 8, 256], F32, 1)
            psr = Ring(nc, es, "o_nps", [128, 512], F32, 2, psum=True)
            smr = Ring(nc, es, "o_sd", [128, 256], F32, 2)
            tmr = Ring(nc, es, "o_tm", [128, 256], F32, 3)
            for (c0, c1) in _split(0, NL, 256) + [(NL, NT)]:
                w = c1 - c0
                isc = c0 >= NL
                sc_ = sc if isc else sx
                s0 = NS + (c0 - NL) if isc else HB + c0
                yt, ytb = yxr.next()
                S.dma("sp", yt[:, :, 0:w], d_yx[:, :, c0:c1], reads=[yxb], writes=[ytb])
                ht, htb = htr.next()
                S.dma("sp", ht[:, :, 0:w], d_h[:, :, s0:s0 + w], writes=[htb])
                hm, hmb = hmr.next()
                for m in range(8):
                    ps, psb = accr.next()
                    for k in range(8):
                        S.mm(ps[:, 0:w], wo[:, k, m * 128:(m + 1) * 128], yt[:, k, 0:w], start=(k == 0), stop=(k == 7),
                             reads=[wob, ytb], writes=[psb])
                    S.stt("dve", hm[:, m, 0:w], ps[:, 0:w], sc_["G1"][:, m:m + 1], ht[:, m, 0:w], ALU.mult, ALU.add,
                          reads=[psb, htb] + sc_["bufs"], writes=[hmb])
                S.dma("sp", d_hm[:, :, c0:c1], hm[:, :, 0:w], reads=[hmb])
                h2, h2b = h2r.next()
                emit_norm_tile(S, cst, hm[:, :, 0:w], hmb, w, sc_["A2"], sc_["B2"], sc_["bufs"],
                               lambda k, h2=h2, h2b=h2b, w=w: (h2[:, k, 0:w], h2b), sqr, psr, smr, tmr)
                S.dma("sp", d_h2[:, :, c0:c1], h2[:, :, 0:w], reads=[h2b])
        S.emit()
    return nc


FM_COLS = list(range(0, 1024)) + list(range(1408, 2176)) + list(range(2560, 2944))
TM_COLS = list(range(1024, 1408)) + list(range(2176, 2560))


def prep_m1_weights(w_in_l, w_out_l, pool_w_l):
    wf = np.ascontiguousarray(w_in_l[:, FM_COLS].reshape(8, 128, 17, 128).transpose(2, 1, 0, 3))
    wt = np.ascontiguousarray(w_in_l[:, TM_COLS].reshape(8, 128, 768).transpose(1, 0, 2))
    wo = np.ascontiguousarray(w_out_l.reshape(8, 128, 1024).transpose(1, 0, 2))
    wbd = np.zeros((128, 2, 128), np.float32)
    for g2 in range(2):
        for gh in range(2):
            wbd[gh * 64:(gh + 1) * 64, g2, gh * 64:(gh + 1) * 64] = pool_w_l[2 * g2 + gh]
    return wf, wt, wo, wbd


def pool_consts(core, pool_scale_l):
    pv = np.zeros((128, 2, 3), np.float32)
    ic = np.ones((128, 2, 4, 8), np.float32)
    for g2 in range(2):
        pv[:, g2, 0] = pool_scale_l[g2 * 128:(g2 + 1) * 128]
        for gh in range(2):
            w = POOL_WINDOWS[2 * g2 + gh]
            sl = slice(gh * 64, (gh + 1) * 64)
            pv[sl, g2, 1] = 1.0 / w
            regs = ((core * NL + np.arange(8), L), (core * NL + NL - 8 + np.arange(8), L),
                    (np.arange(8), NCTX), (NCTX - 8 + np.arange(8), NCTX))
            for r, (t, ln) in enumerate(regs):
                lo = np.clip(t - w // 2, 0, ln)
                hi = np.clip(t - w // 2 + w, 0, ln)
                ic[sl, g2, r, :] = (1.0 / (hi - lo).astype(np.float32))[None, :]
    pv[:, 0, 2] = 1.0 if core > 0 else 0.0
    pv[:, 1, 2] = 1.0 if core < NCORE - 1 else 0.0
    return pv, ic


def _na_block(rpb_h, qrow0, krow0):
    a = np.repeat(np.arange(2), 64)
    qc = np.tile(np.arange(64), 2)
    qrow = (qrow0 + a)[:, None]
    krow = (krow0 + a)[None, :]
    kc = qc[None, :]
    qcc = qc[:, None]
    rs = np.clip(qrow - 4, 0, ROWS - 8)
    vrow = (krow >= 0) & (krow < ROWS) & (krow >= rs) & (krow < rs + 8)
    ws = np.clip(qcc - 8, 0, GW - 16)
    vcol = (kc >= ws) & (kc < ws + 16)
    dr = np.clip(krow - qrow + 7, 0, 14)
    dc = np.clip(kc - qcc + 15, 0, 30)
    val = rpb_h[dr, dc]
    return np.where(vrow & vcol, val, np.float32(NEG)).astype(np.float32)


def na_tables(rpb_l, core):
    gt = np.zeros((128, 6, 5, 128), np.float32)
    st = np.zeros((4, 128, 6, 6, 128), np.float32)
    for h in range(6):
        for s in range(5):
            gt[:, h, s, :] = _na_block(rpb_l[h], 32 + 8, 32 + 2 * (4 + s) - 4).T
        for si, (pi, b0) in enumerate(SPECIAL):
            for s in range(6):
                st[si, :, h, s, :] = _na_block(rpb_l[h], RPC * core + 2 * pi, RPC * core + 2 * (b0 + s) - 4).T
    return gt.reshape(128, 6, 640).astype(NPBF), st.reshape(4, 128, 6, 768).astype(NPBF)


def dm_table():
    j = np.arange(128)[:, None]
    i = np.arange(128)[None, :]
    dm = np.zeros((128, 4, 128), np.float32)
    dm[:, 0, :] = np.maximum(i - j, 0)
    dm[:, 1, :] = (i >= j)
    dm[:, 2, :] = np.maximum(j - i, 0)
    dm[:, 3, :] = (j >= i)
    return dm


def coef_inputs(core, lg_f, lg_b):
    cf = np.zeros((128, 3, 2, 3, 9), np.float32)
    for c in range(3):
        for half in range(2):
            sl = slice(half * 64, (half + 1) * 64)
            h = 2 * c + half
            cf[sl, 1, 0, c, :] = lg_f[h]
            cf[sl, 1, 1, c, :] = lg_b[h]
    for j in range(NCORE):
        if j < core:
            cf[:, 0, 0, :, j] = NL * (core - 1 - j)
            cf[:, 2, 0, :, j] = 1.0
        if j > core:
            cf[:, 0, 1, :, j] = NL * (j - core - 1)
            cf[:, 2, 1, :, j] = 1.0
    cf[:, 0, 0, :, 8] = NL * core
    cf[:, 2, 0, :, 8] = 1.0
    cf[:, 0, 1, :, 8] = NL * (NCORE - 1 - core)
    cf[:, 2, 1, :, 8] = 1.0
    return cf.reshape(128, 3, 54)


def assemble_sall(sls):
    sa = np.zeros((128, 2, 3, 9, 128), np.float32)
    for j in range(NCORE):
        sa[:, 0, :, j, :] = sls[j][:, 0]
        sa[:, 1, :, j, :] = sls[j][:, 1]
    sa[:, 0, :, 8, :] = sls[0][:, 2]
    sa[:, 1, :, 8, :] = sls[0][:, 3]
    return sa


def slab_with_halo(h_full, core):
    out = np.zeros((NS, h_full.shape[1]), h_full.dtype)
    t0 = core * NL - HB
    a = max(t0, 0)
    b = min(t0 + NS, L)
    out[a - t0:b - t0] = h_full[a:b]
    return out


def _run(nc, maps):
    res = run_bass_kernel_spmd(nc, maps, core_ids=list(range(NCORE)))
    return res.results


def kernel(x, c, ctx, c_ctx, w_mod, b_mod, norm1_g, w_in, pool_w, pool_scale, na_rpb, ret_decay_fwd,
           ret_decay_bwd, ret_gn_g, w_out, norm2_g, w_up, conv_w, conv_b, w_down, final_g):
    f = lambda a: np.ascontiguousarray(np.asarray(a, dtype=np.float32))
    x, c, ctx, c_ctx, w_mod, b_mod, norm1_g, w_in, pool_w, pool_scale, na_rpb, ret_decay_fwd, ret_decay_bwd, \
        ret_gn_g, w_out, norm2_g, w_up, conv_w, conv_b, w_down, final_g = map(f, (
            x, c, ctx, c_ctx, w_mod, b_mod, norm1_g, w_in, pool_w, pool_scale, na_rpb, ret_decay_fwd, ret_decay_bwd,
            ret_gn_g, w_out, norm2_g, w_up, conv_w, conv_b, w_down, final_g))
    mv = run_mod(c, c_ctx, w_mod, b_mod)
    h = x[0]
    hc = ctx[0]
    cst = make_cst()
    ropes = [rope_tables(i) for i in range(NCORE)]
    dm = dm_table()
    out = None
    for l in range(DEPTH):
        last = l == DEPTH - 1
        mod6 = mod6_layout(mv[l])
        g12 = np.ascontiguousarray(np.stack([vec_pk(norm1_g[l]), vec_pk(norm2_g[l])], 1))
        dk, cexp, lgc = decay_inputs(ret_decay_fwd[l], ret_decay_bwd[l])
        hcT = to_fm(hc)
        wk, wv = prep_r_weights(w_in[l])
        maps = []
        for i in range(NCORE):
            hT = np.concatenate([to_fm(h[i * NL:(i + 1) * NL]), hcT], 2)
            maps.append({"hT": hT, "mod6": mod6, "g12": g12, "wk": wk, "wv": wv, "rope": ropes[i], "dk": dk,
                         "cexp": cexp, "cst": cst})
        res = _run(_prog("r", build_r), maps)
        sall = assemble_sall([res[i]["sl"] for i in range(NCORE)])
        wf, wt, wo, wbd = prep_m1_weights(w_in[l], w_out[l], pool_w[l])
        gng = vec_pk(ret_gn_g[l])
        maps = []
        for i in range(NCORE):
            hT = np.concatenate([to_fm(slab_with_halo(h, i)), hcT], 2)
            pv, ic = pool_consts(i, pool_scale[l])
            gt, st = na_tables(na_rpb[l], i)
            maps.append({"hT": hT, "mod6": mod6, "g12": g12, "wfm": wf, "wtm": wt, "rope": ropes[i], "gtab": gt,
                         "stab": st, "wbd": wbd, "pvec": pv, "invcnt": ic, "dk": dk, "dm": dm, "lgc": lgc,
                         "cexp": cexp, "coef": coef_inputs(i, ret_decay_fwd[l], ret_decay_bwd[l]), "sall": sall,
                         "gng": gng, "wout": wo, "cst": cst})
        res = _run(_prog("m1", build_m1), maps)
        hmid = [res[i]["hmid"] for i in range(NCORE)]
        h2 = [np.asarray(res[i]["h2"]) for i in range(NCORE)]
        wup, cw, wdn = prep_m2_weights(w_up[l], conv_w[l], conv_b[l], w_down[l])
        g2 = np.ascontiguousarray(np.stack([vec_pk(mv[l, 0, 5 * D:]), vec_pk(mv[l, 1, 5 * D:])], 1))
        maps = []
        for i in range(NCORE):
            prev = h2[i - 1][:, :, NL - 1] if i > 0 else None
            nxt = h2[i + 1][:, :, 0] if i < NCORE - 1 else None
            h2e = make_h2e(h2[i][:, :, :NL], prev, nxt, h2[i][:, :, NL:])
            m = {"hmid": hmid[i], "h2e": h2e, "wup": wup, "cw": cw, "wdn": wdn, "g2": g2, "cst": cst}
            if last:
                m["fg"] = vec_pk(final_g)
            maps.append(m)
        if last:
            res = _run(_prog("m2f", lambda: build_m2(True)), maps)
            out = np.concatenate([from_fm(res[i]["outT"]) for i in range(NCORE)], 0)
        else:
            res = _run(_prog("m2", lambda: build_m2(False)), maps)
            h = np.concatenate([from_fm(res[i]["hn"][:, :, :NL]) for i in range(NCORE)], 0)
            hc = from_fm(res[0]["hn"][:, :, NL:])
    return out.reshape(1, L, D).astype(np.float32)
```
